# Optimizing a Trainium2 kernel written in Bass

```python
import math
import jax
import jax.numpy as jnp
from jax import lax
import numpy as np

D_MODEL = 1024
BATCH = 16
SEQ = 2048
DEPTH = 2
DEC_BATCH = 32
DEC_SEQ = 64
PAST_LEN = 2048

CHUNK = 64
EPS = 1e-6
D_INNER = 2 * D_MODEL
SSM_HEAD_DIM = 64
SSM_HEADS = D_INNER // SSM_HEAD_DIM
SSM_GROUPS = 4
SSM_HEADS_PER_GROUP = SSM_HEADS // SSM_GROUPS
D_STATE = 128
CONV_K = 4
CONV_CH = D_INNER + 2 * SSM_GROUPS * D_STATE
HEAD_DIM = 64
N_Q_HEADS = D_MODEL // HEAD_DIM
N_KV_HEADS = 4
Q_PER_KV = N_Q_HEADS // N_KV_HEADS
WINDOW = 128
WIN_CHUNKS = WINDOW // CHUNK
N_BUCKETS = 32
MAX_DISTANCE = 128
D_FF = math.ceil(8 * D_MODEL / 3 / 256) * 256
IN_SPLITS = (
    D_INNER,
    D_INNER + CONV_CH,
    D_INNER + CONV_CH + SSM_HEADS,
    D_INNER + CONV_CH + SSM_HEADS + N_Q_HEADS * HEAD_DIM,
    D_INNER + CONV_CH + SSM_HEADS + N_Q_HEADS * HEAD_DIM + N_KV_HEADS * HEAD_DIM,
    D_INNER + CONV_CH + SSM_HEADS + N_Q_HEADS * HEAD_DIM + 2 * N_KV_HEADS * HEAD_DIM,
    D_INNER + CONV_CH + SSM_HEADS + N_Q_HEADS * HEAD_DIM + 2 * N_KV_HEADS * HEAD_DIM + D_MODEL,
)
IN_COLS = IN_SPLITS[-1] + D_MODEL

kernel_name = "hybrid_ssd_swa_stream_step"


def rmsnorm(x, g):
    xf = x.astype(jnp.float32)
    y = xf * lax.rsqrt(jnp.mean(xf * xf, axis=-1, keepdims=True) + EPS)
    return (y * g.astype(jnp.float32)).astype(x.dtype)


def t5_bucket(rel):
    n = -rel
    half = N_BUCKETS // 2
    max_exact = half // 2
    ret = jnp.where(n < 0, half, 0)
    n = jnp.abs(n)
    nf = jnp.maximum(n, 1).astype(jnp.float32)
    large = max_exact + (jnp.log(nf / max_exact) / math.log(MAX_DISTANCE / max_exact)
                         * (half - max_exact)).astype(jnp.int32)
    large = jnp.minimum(large, half - 1)
    return ret + jnp.where(n < max_exact, n, large)


def causal_conv(u, prev, w, b):
    full = jnp.concatenate([prev.astype(u.dtype), u], axis=1)
    L = u.shape[1]
    out = full[:, 0:L] * w[0]
    for k in range(1, CONV_K):
        out = out + full[:, k:k + L] * w[k]
    return out + b, full[:, -(CONV_K - 1):]


def ssd_scan(xh, dt, a, bm, cm, state0):
    b, L = xh.shape[:2]
    lc = min(CHUNK, L)
    nc = L // lc

    def chunks(t):
        return jnp.swapaxes(t.reshape((b, nc, lc) + t.shape[2:]), 0, 1)

    xs = (chunks(xh.astype(jnp.float32) * dt[..., None]), chunks(dt * a),
          chunks(bm.astype(jnp.float32)), chunks(cm.astype(jnp.float32)))
    causal = jnp.tril(jnp.ones((lc, lc), dtype=bool))

    def step(state, inp):
        xdt, da, bc, cc = inp
        acum = jnp.cumsum(da, axis=1)
        seg = acum[:, :, None, :] - acum[:, None, :, :]
        decay = jnp.exp(jnp.where(causal[None, :, :, None], seg, -jnp.inf))
        decay = decay.reshape(b, lc, lc, SSM_GROUPS, SSM_HEADS_PER_GROUP)
        cb = jnp.einsum("blgn,bsgn->blsg", cc, bc)
        xg = xdt.reshape(b, lc, SSM_GROUPS, SSM_HEADS_PER_GROUP, SSM_HEAD_DIM)
        y_in = jnp.einsum("blsg,blsgh,bsghp->blghp", cb, decay, xg)
        sg = state.reshape(b, SSM_GROUPS, SSM_HEADS_PER_GROUP, SSM_HEAD_DIM, D_STATE)
        y_past = jnp.einsum("blgn,bghpn->blghp", cc, sg) * jnp.exp(acum).reshape(
            b, lc, SSM_GROUPS, SSM_HEADS_PER_GROUP)[..., None]
        tail = jnp.exp(acum[:, -1:] - acum).reshape(b, lc, SSM_GROUPS, SSM_HEADS_PER_GROUP)
        new = sg * jnp.exp(acum[:, -1]).reshape(b, SSM_GROUPS, SSM_HEADS_PER_GROUP)[..., None, None] \
            + jnp.einsum("blgn,blgh,blghp->bghpn", bc, tail, xg)
        return new.reshape(b, SSM_HEADS, SSM_HEAD_DIM, D_STATE), y_in + y_past

    final, ys = lax.scan(step, state0.astype(jnp.float32), xs)
    y = jnp.swapaxes(ys, 0, 1).reshape(b, L, SSM_HEADS, SSM_HEAD_DIM)
    return y, final


def band_rows(past, new, nc):
    rows = jnp.concatenate([past.astype(new.dtype), new], axis=1)
    if nc == 1:
        return rows[:, None]
    rc = rows.reshape((rows.shape[0], WIN_CHUNKS + nc, CHUNK) + rows.shape[2:])
    return jnp.concatenate([rc[:, m:m + nc] for m in range(WIN_CHUNKS + 1)], axis=2)


def sliding_attention(q, k, v, k_past, v_past, past_valid, q_norm_g, k_norm_g, sinks, rel_bias):
    b, L = q.shape[:2]
    lq = min(CHUNK, L)
    nc = L // lq
    lk = WINDOW + lq
    q = rmsnorm(q.reshape(b, L, N_KV_HEADS, Q_PER_KV, HEAD_DIM), q_norm_g)
    k = rmsnorm(k.reshape(b, L, N_KV_HEADS, HEAD_DIM), k_norm_g)
    v = v.reshape(b, L, N_KV_HEADS, HEAD_DIM)
    kb = band_rows(k_past, k, nc)
    vb = band_rows(v_past, v, nc)
    key_pos = jnp.arange(nc)[:, None] * lq + jnp.arange(lk)[None, :] - WINDOW
    valid = jnp.logical_or(key_pos >= 0, past_valid)
    rel = jnp.arange(lk)[None, :] - WINDOW - jnp.arange(lq)[:, None]
    bias = jnp.transpose(rel_bias[t5_bucket(rel)], (2, 0, 1)).reshape(
        N_KV_HEADS, Q_PER_KV, lq, lk).astype(jnp.float32)
    qb = q.reshape(b, nc, lq, N_KV_HEADS, Q_PER_KV, HEAD_DIM)
    s = jnp.einsum("bnqkgd,bnskd->bnkgqs", qb, kb).astype(jnp.float32) * (HEAD_DIM ** -0.5) + bias
    s = jnp.where(valid[None, :, None, None, None, :], s, -jnp.inf)
    sink = jnp.broadcast_to(sinks.astype(jnp.float32).reshape(1, 1, N_KV_HEADS, Q_PER_KV, 1, 1),
                            s.shape[:-1] + (1,))
    p = jax.nn.softmax(jnp.concatenate([s, sink], axis=-1), axis=-1)[..., :-1]
    o = jnp.einsum("bnkgqs,bnskd->bnqkgd", p.astype(v.dtype), vb)
    return o.reshape(b, L, N_Q_HEADS * HEAD_DIM), k, v


def trunk_layer(x, c, conv_prev, ssm_prev, k_past, v_past, past_valid,
                ada_w, ada_b, norm_mix_g, norm_ffn_g, w_in, conv_w, conv_b, dt_bias, a_log, d_skip,
                ssm_norm_g, q_norm_g, k_norm_g, sinks, rel_bias, w_br_ssm, w_br_attn, w_out,
                w_gate_up, w_down):
    b, L, _ = x.shape
    mod = jax.nn.silu(c) @ ada_w + ada_b
    sh_m, sc_m, gt_m, sh_f, sc_f, gt_f = jnp.split(mod[:, None, :], 6, axis=-1)
    h = rmsnorm(x, norm_mix_g) * (1 + sc_m) + sh_m
    z, xbc, dt_raw, q, k, v, g_ssm, g_attn = jnp.split(h @ w_in, IN_SPLITS, axis=-1)
    xbc, conv_new = causal_conv(xbc, conv_prev, conv_w, conv_b)
    xbc = jax.nn.silu(xbc)
    xs, bm, cm = jnp.split(xbc, [D_INNER, D_INNER + SSM_GROUPS * D_STATE], axis=-1)
    dt = jax.nn.softplus(dt_raw.astype(jnp.float32) + dt_bias.astype(jnp.float32))
    a = -jnp.exp(a_log.astype(jnp.float32))
    xh = xs.reshape(b, L, SSM_HEADS, SSM_HEAD_DIM)
    y, ssm_new = ssd_scan(xh, dt, a, bm.reshape(b, L, SSM_GROUPS, D_STATE),
                          cm.reshape(b, L, SSM_GROUPS, D_STATE), ssm_prev)
    y = (y.astype(x.dtype) + xh * d_skip[:, None]).reshape(b, L, D_INNER) * jax.nn.silu(z)
    y = rmsnorm(y.reshape(b, L, SSM_GROUPS, D_INNER // SSM_GROUPS),
                ssm_norm_g.reshape(SSM_GROUPS, D_INNER // SSM_GROUPS)).reshape(b, L, D_INNER)
    o, k_new, v_new = sliding_attention(q, k, v, k_past, v_past, past_valid,
                                        q_norm_g, k_norm_g, sinks, rel_bias)
    mixed = jax.nn.sigmoid(g_ssm) * (y @ w_br_ssm) + jax.nn.sigmoid(g_attn) * (o @ w_br_attn)
    x = x + gt_m * (mixed @ w_out)
    h2 = rmsnorm(x, norm_ffn_g) * (1 + sc_f) + sh_f
    gate, up = jnp.split(h2 @ w_gate_up, 2, axis=-1)
    x = x + gt_f * ((jax.nn.silu(gate) * up) @ w_down)
    return x, conv_new, ssm_new.astype(x.dtype), k_new, v_new


def setup_inputs(seed: int = 0) -> dict:
    key = jax.random.key(seed)
    ks = jax.random.split(key, 32)
    f32 = jnp.float32

    def nrm(i, shape, scale):
        return jax.random.normal(ks[i], shape, f32) * scale

    u = jax.random.uniform(ks[16], (DEPTH, SSM_HEADS), f32)
    dt0 = jnp.exp(u * (math.log(0.1) - math.log(1e-3)) + math.log(1e-3))
    return {
        "x_prompt": nrm(0, (BATCH, SEQ, D_MODEL), 1.0),
        "x_sample": nrm(1, (DEC_BATCH, DEC_SEQ, D_MODEL), 1.0),
        "cache_k": nrm(2, (DEPTH, DEC_BATCH, WINDOW, N_KV_HEADS, HEAD_DIM), 1.0),
        "cache_v": nrm(3, (DEPTH, DEC_BATCH, WINDOW, N_KV_HEADS, HEAD_DIM), 1.0),
        "state_conv": nrm(4, (DEPTH, DEC_BATCH, CONV_K - 1, CONV_CH), 1.0),
        "state_ssm": nrm(5, (DEPTH, DEC_BATCH, SSM_HEADS, SSM_HEAD_DIM, D_STATE), 0.3),
        "c_prompt": nrm(6, (BATCH, D_MODEL), 1.0),
        "c_sample": nrm(7, (DEC_BATCH, D_MODEL), 1.0),
        "rel_bias": nrm(8, (N_BUCKETS, N_Q_HEADS), 0.2),
        "ada_w": nrm(9, (DEPTH, D_MODEL, 6 * D_MODEL), 0.5 * D_MODEL ** -0.5),
        "ada_b": nrm(10, (DEPTH, 6 * D_MODEL), 0.02),
        "norm_mix_g": 1.0 + nrm(11, (DEPTH, D_MODEL), 0.05),
        "norm_ffn_g": 1.0 + nrm(12, (DEPTH, D_MODEL), 0.05),
        "w_in": nrm(13, (DEPTH, D_MODEL, IN_COLS), D_MODEL ** -0.5),
        "conv_w": nrm(14, (DEPTH, CONV_K, CONV_CH), CONV_K ** -0.5),
        "conv_b": nrm(15, (DEPTH, CONV_CH), 0.02),
        "dt_bias": dt0 + jnp.log(-jnp.expm1(-dt0)),
        "a_log": jnp.log(jax.random.uniform(ks[17], (DEPTH, SSM_HEADS), f32, 1.0, 16.0)),
        "d_skip": 1.0 + nrm(18, (DEPTH, SSM_HEADS), 0.05),
        "ssm_norm_g": 1.0 + nrm(19, (DEPTH, D_INNER), 0.05),
        "q_norm_g": 1.0 + nrm(20, (DEPTH, HEAD_DIM), 0.05),
        "k_norm_g": 1.0 + nrm(21, (DEPTH, HEAD_DIM), 0.05),
        "sinks": nrm(22, (DEPTH, N_Q_HEADS), 0.5),
        "w_br_ssm": nrm(23, (DEPTH, D_INNER, D_MODEL), D_INNER ** -0.5),
        "w_br_attn": nrm(24, (DEPTH, N_Q_HEADS * HEAD_DIM, D_MODEL), (N_Q_HEADS * HEAD_DIM) ** -0.5),
        "w_out": nrm(25, (DEPTH, D_MODEL, D_MODEL), D_MODEL ** -0.5),
        "w_gate_up": nrm(26, (DEPTH, D_MODEL, 2 * D_FF), D_MODEL ** -0.5),
        "w_down": nrm(27, (DEPTH, D_FF, D_MODEL), D_FF ** -0.5),
    }


def reference(x_prompt, x_sample, cache_k, cache_v, state_conv, state_ssm, c_prompt, c_sample,
              rel_bias, ada_w, ada_b, norm_mix_g, norm_ffn_g, w_in, conv_w, conv_b, dt_bias, a_log,
              d_skip, ssm_norm_g, q_norm_g, k_norm_g, sinks, w_br_ssm, w_br_attn, w_out,
              w_gate_up, w_down):
    bp = x_prompt.shape[0]
    zero_conv = jnp.zeros((bp, CONV_K - 1, CONV_CH), x_prompt.dtype)
    zero_ssm = jnp.zeros((bp, SSM_HEADS, SSM_HEAD_DIM, D_STATE), jnp.float32)
    zero_kv = jnp.zeros((bp, WINDOW, N_KV_HEADS, HEAD_DIM), x_prompt.dtype)
    xp, xs = x_prompt, x_sample
    conv_p, conv_s, ssm_p, ssm_s, k_p, k_s, v_p, v_s = [], [], [], [], [], [], [], []
    for l in range(DEPTH):
        lp = (ada_w[l], ada_b[l], norm_mix_g[l], norm_ffn_g[l], w_in[l], conv_w[l], conv_b[l],
              dt_bias[l], a_log[l], d_skip[l], ssm_norm_g[l], q_norm_g[l], k_norm_g[l], sinks[l],
              rel_bias, w_br_ssm[l], w_br_attn[l], w_out[l], w_gate_up[l], w_down[l])
        xp, cvp, ssp, kp, vp = trunk_layer(xp, c_prompt, zero_conv, zero_ssm, zero_kv, zero_kv, False, *lp)
        xs, cvs, sss, ksn, vsn = trunk_layer(xs, c_sample, state_conv[l], state_ssm[l],
                                             cache_k[l], cache_v[l], True, *lp)
        conv_p.append(cvp)
        conv_s.append(cvs)
        ssm_p.append(ssp)
        ssm_s.append(sss)
        k_p.append(kp[:, -WINDOW:])
        v_p.append(vp[:, -WINDOW:])
        k_s.append(ksn)
        v_s.append(vsn)
    return (xp, xs, jnp.stack(conv_p), jnp.stack(conv_s), jnp.stack(ssm_p), jnp.stack(ssm_s),
            jnp.stack(k_p), jnp.stack(k_s), jnp.stack(v_p), jnp.stack(v_s))
```

```python
import math
import numpy as np
from contextlib import ExitStack
import concourse.bass as bass
import concourse.mybir as mybir
from concourse.bass_utils import run_bass_kernel_spmd

F32 = mybir.dt.float32
BF16 = mybir.dt.bfloat16
AF = mybir.ActivationFunctionType
ALU = mybir.AluOpType
AX = mybir.AxisListType

SAME_ENGINE_SYNC = True
SAME_HOP_US = 0.25
HOP_US = 0.7
EPS = 1e-6
D = 1024
KD = 8
NT = 256
NCH = 4
INC = 8736
C_Z, C_XBC, C_DT, C_Q, C_K, C_GS, C_GA = 0, 2048, 5120, 5152, 6176, 6688, 7712
DFF = 2816


class Tok:
    __slots__ = ("w", "r", "name", "excl")

    def __init__(self, name=""):
        self.w = None
        self.r = []
        self.name = name
        self.excl = False


class Slot:
    def __init__(self, sem, name=""):
        self.sem = sem
        self.name = name
        self.last = None
        self.queue = None


class Op:
    __slots__ = ("id", "stream", "fn", "dma", "slot", "deps", "dur", "issue", "ev", "succ", "nleft", "ready")


class _Rec:
    def __init__(self, stream):
        self.stream = stream
        self.t = 0.0

    def then_inc(self, *a, **k):
        return self

    def __getattr__(self, name):
        def f(*a, **k):
            def fsz(ap):
                n = 1
                for d in list(ap.shape)[1:]:
                    n *= d
                return n
            st = self.stream
            if name == "matmul":
                rhs = k.get("rhs", a[2] if len(a) > 2 else None)
                n = fsz(rhs)
                mult = 4.0 if rhs.dtype == F32 else 1.0
                self.t += max(n, 64) * mult / 2050.0 + 0.012
            elif name == "transpose":
                src = k.get("in_", a[1] if len(a) > 1 else None)
                mult = 4.0 if src.dtype == F32 else 1.0
                self.t += max(fsz(src), 64) * mult / 2050.0 + 0.03
            else:
                out = k.get("out", a[0] if a else None)
                n = fsz(out) if out is not None else 64
                if name == "reciprocal":
                    n *= 6
                if st == "dve":
                    self.t += n / 960.0 + 0.08
                elif st == "act":
                    self.t += n / 1400.0 + 0.22
                else:
                    self.t += n / 500.0 + 0.15
            return self
        return f


def _os_env_inorder():
    import os
    return os.environ.get("K_INORDER", "pe,sp").split(",")


class Prog:
    def __init__(self, nc):
        self.nc = nc
        self.es = ExitStack()
        self.names = ("pe", "act", "dve", "pool", "sp")
        self.sems = {nm: self.es.enter_context(nc.semaphore("sem_" + nm)) for nm in self.names}
        self.in_order_safe = set(_os_env_inorder())
        self.slots = []
        self.ops = []

    def sbuf(self, name, shape, dtype):
        return self.es.enter_context(self.nc.sbuf_tensor(name, list(shape), dtype))

    def psum(self, name, shape, dtype):
        return self.es.enter_context(self.nc.psum_tensor(name, list(shape), dtype))

    def slot(self, name):
        sem = self.es.enter_context(self.nc.semaphore("dq_" + name))
        s = Slot(sem, name)
        self.slots.append(s)
        return s

    def _mk(self, stream, reads, writes):
        o = Op()
        o.id = len(self.ops)
        o.stream = stream
        o.fn = None
        o.dma = None
        o.slot = None
        deps = set()
        for t in reads:
            if t.w is not None:
                deps.add(t.w)
            if t.excl:
                deps.update(t.r)
        for t in writes:
            if t.w is not None:
                deps.add(t.w)
            deps.update(t.r)
        o.deps = deps
        for t in reads:
            t.r.append(o.id)
        for t in writes:
            t.w = o.id
            t.r = []
        self.ops.append(o)
        return o

    def op(self, eng, fn, reads=(), writes=()):
        o = self._mk(eng, reads, writes)
        o.fn = fn
        rec = _Rec(eng)
        fn(rec)
        o.dur = rec.t
        o.issue = o.dur
        return o.id

    def dma(self, q, out, in_, slot, reads=(), writes=(), cont=False):
        o = self._mk(q, reads, writes)
        o.dma = (out, in_)
        o.slot = slot
        assert slot.queue in (None, q)
        slot.queue = q
        if slot.last is not None:
            if cont:
                o.deps.update(self.ops[slot.last].deps)
                o.deps.discard(o.id)
            else:
                o.deps.add(slot.last)
        slot.last = o.id
        n = 1
        for d in list(out.shape):
            n *= d
        nbytes = n * (4 if in_.dtype == F32 else 2)
        o.issue = 0.06 if q == "sp" else 1.0
        o.dur = nbytes / 350e3 + 0.2
        return o.id

    def schedule(self, window=600):
        ops = self.ops
        n = len(ops)
        for o in ops:
            o.succ = []
            o.nleft = 0
            o.ready = 0.0
        for o in ops:
            o.deps.discard(o.id)
            for d in o.deps:
                ops[d].succ.append(o.id)
            o.nleft = len(o.deps)
        eng_free = {nm: 0.0 for nm in self.names}
        dma_free = 0.0
        order = {nm: [] for nm in self.names}
        done = [False] * n
        ready = [o.id for o in ops if o.nleft == 0]
        head = 0
        fin = [0.0] * n
        nsched = 0
        while nsched < n:
            best = None
            bk = None
            lim = head + window
            for i in ready:
                if i >= lim:
                    continue
                o = ops[i]
                st = max(eng_free[o.stream], o.ready)
                k = (st, i)
                if bk is None or k < bk:
                    bk = k
                    best = i
            if best is None:
                best = min(ready)
                o = ops[best]
                bk = (max(eng_free[o.stream], o.ready), best)
            o = ops[best]
            ready.remove(best)
            st = bk[0]
            if o.dma is None:
                f = st + o.dur
                eng_free[o.stream] = f
            else:
                eng_free[o.stream] = st + o.issue
                f0 = max(st + o.issue, dma_free) + o.dur
                dma_free = f0
                f = f0 + 1.8
            fin[best] = f
            done[best] = True
            nsched += 1
            order[o.stream].append(best)
            for s_ in o.succ:
                so = ops[s_]
                so.nleft -= 1
                fl = f + (SAME_HOP_US if (so.stream == o.stream and o.stream not in ("pe", "sp")) else (0.0 if so.stream == o.stream else HOP_US))
                if so.ready < fl:
                    so.ready = fl
                if so.nleft == 0:
                    ready.append(s_)
            while head < n and done[head]:
                head += 1
        self.order = order
        self.est_total = max(fin) if fin else 0.0

    def emit(self, window=600):
        self.schedule(window)
        ops = self.ops
        nc = self.nc
        for nm in self.names:
            cnt = 0
            for i in self.order[nm]:
                o = ops[i]
                if o.dma is None:
                    cnt += 1
                    o.ev = (self.sems[nm], cnt)
            setattr(self, "cnt_" + nm, cnt)
        slot_cnt = {}
        for nm in self.names:
            for i in self.order[nm]:
                o = ops[i]
                if o.dma is not None:
                    c = slot_cnt.get(id(o.slot), 0) + 1
                    slot_cnt[id(o.slot)] = c
                    o.ev = (o.slot.sem, 16 * c)
        hmap = {"pe": "tensor", "act": "scalar", "dve": "vector", "pool": "gpsimd", "sp": "sync"}
        streams = {}
        for nm in self.names:
            ins = []
            waited = {}
            for i in self.order[nm]:
                o = ops[i]
                need = {}
                for d in o.deps:
                    sem, v = ops[d].ev
                    if sem is self.sems[nm] and (nm in self.in_order_safe or not SAME_ENGINE_SYNC) and ops[d].dma is None:
                        continue
                    if need.get(id(sem), (None, 0))[1] < v:
                        need[id(sem)] = (sem, v)
                for sem, v in need.values():
                    if waited.get(id(sem), 0) >= v:
                        continue
                    waited[id(sem)] = v
                    ins.append(("wait", sem, v))
                if o.dma is None:
                    ins.append(("op", o.fn))
                else:
                    ins.append(("dma", o.dma[0], o.dma[1], o.slot.sem))
            if nm == "sp":
                for sl in self.slots:
                    c = slot_cnt.get(id(sl), 0)
                    if c and waited.get(id(sl.sem), 0) < 16 * c:
                        ins.append(("wait", sl.sem, 16 * c))
                for o2 in self.names:
                    c = getattr(self, "cnt_" + o2)
                    if o2 != "sp" and c:
                        ins.append(("wait", self.sems[o2], c))
            streams[nm] = ins
        with nc.Block() as block:
            for nm in self.names:
                ins = streams[nm]
                if not ins:
                    continue

                def body(e, ins=ins, sem=self.sems[nm]):
                    for it in ins:
                        k = it[0]
                        if k == "wait":
                            e.wait_ge(it[1], it[2])
                        elif k == "op":
                            last = it[1](e)
                            last.then_inc(sem, 1)
                        else:
                            e.dma_start(out=it[1], in_=it[2]).then_inc(it[3], 16)

                getattr(block, hmap[nm])(body)
        self.es.close()


def V(t, p0, pn, off, dims):
    Fsz = 1
    for s in t.shape[1:]:
        Fsz *= s
    return bass.AP(t, p0 * Fsz + off, [[Fsz, pn]] + [[s, n] for s, n in dims])


def DA(h, off, dims):
    return bass.AP(h, off, [[s, n] for s, n in dims])


def t5_bucket_np(rel):
    n = -rel
    half, max_exact = 16, 8
    ret = np.where(n < 0, half, 0)
    n = np.abs(n)
    nf = np.maximum(n, 1).astype(np.float32)
    large = max_exact + (np.log(nf / np.float32(max_exact)) / np.float32(math.log(128 / max_exact))
                         * np.float32(half - max_exact)).astype(np.int32)
    large = np.minimum(large, half - 1)
    return ret + np.where(n < max_exact, n, large)


def onehot_table():
    q = np.arange(64)[:, None]
    oh = np.zeros((32, 3, 64, 64), np.float32)
    for blk in range(3):
        s = np.arange(64)[None, :] + 64 * blk
        bk = t5_bucket_np(s - 128 - q)
        for b in range(32):
            oh[b, blk] = (bk == b)
    return oh.reshape(32, 3 * 64 * 64)


def build_program(NPS, PLEN, NSS, DEPTH=2):
    NSEQ = NPS + NSS
    NPT = PLEN // NT
    assert NSS * 64 == NT or NSS == 0
    nc = bass.Bass("TRN2", target_bir_lowering=False)

    def din(name, shape):
        return nc.dram_tensor(name, list(shape), F32, kind="ExternalInput")

    def dout(name, shape):
        return nc.dram_tensor(name, list(shape), F32, kind="ExternalOutput")

    xp = din("xp", [NPS * PLEN, D]); xs = din("xs", [max(NSS, 1) * 64, D])
    ck = din("ck", [DEPTH, max(NSS, 1), 128, 256]); cv = din("cv", [DEPTH, max(NSS, 1), 128, 256])
    sconv = din("sconv", [DEPTH, max(NSS, 1) * 3, 3072]); sssm = din("sssm", [DEPTH, max(NSS, 1), 2048, 128])
    cvec = din("cvec", [NSEQ, D])
    rel_bias = din("rel_bias", [32, 16]); onehot = din("onehot", [32, 3 * 4096])
    ada_w = din("ada_w", [DEPTH, D, 6 * D]); ada_b = din("ada_b", [DEPTH, 6 * D])
    norm_mix_g = din("norm_mix_g", [DEPTH, D]); norm_ffn_g = din("norm_ffn_g", [DEPTH, D])
    w_in = din("w_in", [DEPTH, D, INC]); conv_w = din("conv_w", [DEPTH, 4, 3072]); conv_b = din("conv_b", [DEPTH, 3072])
    dt_bias = din("dt_bias", [DEPTH, 32]); a_log = din("a_log", [DEPTH, 32]); d_skip = din("d_skip", [DEPTH, 32])
    ssm_norm_g = din("ssm_norm_g", [DEPTH, 2048]); q_norm_g = din("q_norm_g", [DEPTH, 64]); k_norm_g = din("k_norm_g", [DEPTH, 64])
    sinks = din("sinks", [DEPTH, 16])
    w_br_ssm = din("w_br_ssm", [DEPTH, 2048, D]); w_br_attn = din("w_br_attn", [DEPTH, D, D]); w_out = din("w_out", [DEPTH, D, D])
    w_gate_up = din("w_gate_up", [DEPTH, D, 2 * DFF]); w_down = din("w_down", [DEPTH, DFF, D])
    yp = dout("yp", [NPS * PLEN, D]); ys = dout("ys", [max(NSS, 1) * 64, D])
    ncp = dout("ncp", [DEPTH, NPS, 3, 3072]); ncs = dout("ncs", [DEPTH, max(NSS, 1) * 3, 3072])
    nsp = dout("nsp", [DEPTH, NPS, 2048, 128]); nss = dout("nss", [DEPTH, max(NSS, 1), 2048, 128])
    nkp = dout("nkp", [DEPTH, NPS, 128, 256]); nks = dout("nks", [DEPTH, max(NSS, 1), 64, 256])
    nvp = dout("nvp", [DEPTH, NPS, 128, 256]); nvs = dout("nvs", [DEPTH, max(NSS, 1), 64, 256])

    P = Prog(nc)
    toks = {}

    def T(name):
        if name not in toks:
            toks[name] = Tok(name)
        return toks[name]

    def TL(name, n):
        return [T(f"{name}{i}") for i in range(n)]

    PSF = P.psum("psf", [128, 4096], F32)
    PSB = PSF.bitcast(BF16)
    pstok = TL("psb", 8)
    for t_ in pstok:
        t_.excl = True
    ps_live = {}
    ps_rr = {1: 0, 2: 0, 4: 0}

    def ps_alloc(n, nreads):
        slots_ = list(range(0, 8, n))
        k = len(slots_)
        for t_ in range(k):
            b = slots_[(ps_rr[n] + t_) % k]
            if all((b + i) not in ps_live for i in range(n)):
                ps_rr[n] = (ps_rr[n] + t_ + 1) % k
                a = {"banks": list(range(b, b + n)), "rem": nreads}
                for i in range(n):
                    ps_live[b + i] = a
                return b, pstok[b:b + n]
        raise RuntimeError(f"PSUM exhausted (need {n}); live={sorted(ps_live)}")

    def ps_single(nreads=1):
        return ps_alloc(1, nreads)

    def ps_pair(nreads=1):
        return ps_alloc(2, nreads)

    def ps_quad(nreads=1):
        return ps_alloc(4, nreads)

    _orig_op = P.op
    pstok_ids = {id(t): i for i, t in enumerate(pstok)}

    def _op(eng, fn, reads=(), writes=()):
        ev = _orig_op(eng, fn, reads=reads, writes=writes)
        seen = []
        for t in reads:
            bi = pstok_ids.get(id(t))
            if bi is None or bi not in ps_live:
                continue
            a = ps_live[bi]
            if any(a is x for x in seen):
                continue
            seen.append(a)
            a["rem"] -= 1
            if a["rem"] == 0:
                for bb in a["banks"]:
                    del ps_live[bb]
        return ev
    P.op = _op

    def pf(b, pn, off, dims, p0=0):
        return V(PSF, p0, pn, b * 512 + off, dims)

    def pb(b, pn, off, dims, p0=0):
        return V(PSB, p0, pn, b * 1024 + off, dims)

    def sb(name, f, dt=F32):
        return P.sbuf(name, [128, f], dt)

    identf = sb("identf", 128); identb = sb("identb", 128, BF16)
    onesb = sb("onesb", 128, BF16); onesf = sb("onesf", 128)
    Umat = sb("Umat", 64); SUmat = sb("SUmat", 64); Umb = sb("Umb", 64, BF16)
    CONST = sb("CONST", 4)
    xT = sb("xT", KD * NT)
    hT = sb("hT", KD * NT, BF16)
    RSTD = sb("RSTD", NT)
    ZT = sb("ZT", 16 * NT, BF16); QTr = sb("QTr", 8 * NT, BF16)
    PH1 = sb("PH1", 24 * NT, BF16)
    XC = PH1; BT_OFF = 16 * NT; CT_OFF = 20 * NT
    MIXT = ZT
    ACTT = PH1
    CS = [sb(f"CS{i}", 4 * 67) for i in range(2)]
    CACC = [sb(f"CACC{i}", NT) for i in range(2)]
    CIN = sb("CIN", DEPTH * 24 * 12)
    KT = sb("KT", DEPTH * 4 * 6 * 64, BF16)
    VBf = sb("VB", DEPTH * 6 * 4 * 65, BF16)
    KP = sb("KP", 4 * 128, BF16)
    VP = sb("VP", 2 * 4 * 65, BF16)
    DTX = sb("DTX", 128); DTA = sb("DTA", 128); DTE = sb("DTE", 128); DTv = sb("DTv", 128); DAv = sb("DAv", 128)
    R1 = sb("R1", 2048)
    NTMP = R1
    XDTS = R1.bitcast(BF16)
    DEC = sb("DEC", 2048, BF16)
    SQ = DEC
    CBM = sb("CBM", 256, BF16)
    XTs = sb("XTs", 2048, BF16); XDT = sb("XDT", 2048, BF16)
    YN = XDT
    BTOK = sb("BTOK", 512, BF16)
    Tb = sb("Tb", 2048)
    SQQ = sb("SQQ", 1024, BF16)
    EXs = [sb(f"EX{i}", 96) for i in range(2)]
    SS4 = sb("SS4", 4); RS4 = sb("RS4", 4)
    ST = sb("ST", DEPTH * 2048); STB = sb("STB", 2048, BF16)
    YT = sb("YT", 16 * NT, BF16); OT = sb("OT", 8 * NT, BF16)
    KO = [sb(f"KO{i}", 256) for i in range(1)]; VO = [sb(f"VO{i}", 256) for i in range(1)]
    KNB = sb("KNB", 256, BF16); SQK = sb("SQK", 256); SSK = sb("SSK", 4); RK = sb("RK", 4)
    SSQ = sb("SSQ", 16); RQ = sb("RQ", 16)
    QN = sb("QN", 1024, BF16); QTc = sb("QTc", 1024, BF16)
    ON = QN
    PT = [sb(f"PT{i}", 1024, BF16) for i in range(1)]
    DEN = sb("DEN", 16); RDEN = sb("RDEN", 16)
    BIAS = sb("BIAS", 3 * 1024, BF16)
    GSs = sb("GSs", NT); GAs = sb("GAs", NT); MXs = sb("MXs", NT)
    SGs = [sb(f"SGs{i}", NT) for i in range(2)]
    NWB = 4
    WB = [sb(f"WB{i}", 4096, BF16) for i in range(NWB)]
    WBS = sb("WBS", 4096, BF16)
    WDT = sb("WDT", DEPTH * 8 * 32, BF16)
    SCR = sb("SCR", 2048)
    ROWS = SCR; CROW = SCR
    MODT = sb("MODT", DEPTH * 48 * NSEQ)
    GMUL = sb("GMUL", DEPTH * 2 * 8 * NSEQ)
    ADAB = sb("ADAB", DEPTH * 48)
    GN = sb("GN", DEPTH * 8 * 2)
    CWB = sb("CWB", DEPTH * 24 * 5)
    GSSM = sb("GSSM", DEPTH * 16)
    DTB = sb("DTB", DEPTH * 32); ABC = sb("ABC", DEPTH * 32); DSK = sb("DSK", DEPTH * 32)
    QG8 = sb("QG8", DEPTH * 64); KG = sb("KG", DEPTH * 64); ESINK = sb("ESINK", DEPTH * 16)
    DI = sb("DI", 32 * 64, BF16)
    RELB = sb("RELB", 16, BF16); RELF = sb("RELF", 16)
    SCT = sb("SCT", 8 * NSEQ, BF16)

    t_const = T("const")
    wtok = TL("wb", NWB)
    wslot = [P.slot(f"wb{i}") for i in range(NWB)]
    wctr = [0]
    s_in = P.slot("in"); s_in2 = P.slot("in2"); s_pool = P.slot("poolmisc")
    s_out = [P.slot(f"out{i}") for i in range(4)]
    octr = [0]

    def out_slot():
        s = s_out[octr[0] % 4]; octr[0] += 1
        return s

    NGRP = 96
    wscr = nc.dram_tensor("wscr", [NGRP * 128, 4096], BF16, kind="Internal")
    wkeys = {}
    wslot_sp = [P.slot(f"wbsp{i}") for i in range(NWB)]
    wsave_slot = [P.slot(f"wsave{i}") for i in range(2)]
    wsave_ctr = [0]

    def _wfetch(key, buf, tok, slot_pool, slot_sp, parts):
        if key is None or key not in wkeys:
            for pi_, (doff, ddims, sap) in enumerate(parts):
                P.dma("pool", V(buf, 0, 128, doff, ddims), sap, slot_pool, writes=[tok], cont=(pi_ > 0))
            if key is not None:
                gi = len(wkeys)
                assert gi < NGRP
                wkeys[key] = gi
                ss = wsave_slot[wsave_ctr[0] % 2]; wsave_ctr[0] += 1
                P.dma("sp", DA(wscr, gi * 128 * 4096, [(4096, 128), (1, 4096)]), V(buf, 0, 128, 0, [(1, 4096)]), ss, reads=[tok])
        else:
            gi = wkeys[key]
            P.dma("sp", V(buf, 0, 128, 0, [(1, 4096)]), DA(wscr, gi * 128 * 4096, [(4096, 128), (1, 4096)]), slot_sp, writes=[tok])

    def wload(dst_dims, src_ap, key=None):
        i = wctr[0] % NWB; wctr[0] += 1
        _wfetch(key, WB[i], wtok[i], wslot[i], wslot_sp[i], [(0, dst_dims, src_ap)])
        return WB[i], wtok[i]

    wbs_slot = P.slot("wbs"); wbs_slot_sp = P.slot("wbssp")

    def wload_bs(l, col0):
        parts = [(part * 8 * 256, [(256, 8), (1, 256)], DA(w_br_ssm, l * 2048 * D + part * 8 * 128 * D + col0, [(D, 128), (128 * D, 8), (1, 256)])) for part in range(2)]
        _wfetch(("bs", l, col0), WBS, T("WBS"), wbs_slot, wbs_slot_sp, parts)
        return WBS, T("WBS")

    def wload_down(l, j):
        i = wctr[0] % NWB; wctr[0] += 1
        parts = []
        k0 = 0
        for nk in (8, 8, 6):
            parts.append((k0 * 128, [(128, nk), (1, 128)], DA(w_down, l * DFF * D + k0 * 128 * D + j * 128, [(D, 128), (128 * D, nk), (1, 128)])))
            k0 += nk
        _wfetch(("down", l, j), WB[i], wtok[i], wslot[i], wslot_sp[i], parts)
        return WB[i], wtok[i]

    cw_ = [t_const, T("VBall"), T("VPall"), T("CIN0"), T("CIN1")]
    for fn in [
        lambda e: e.memset(identf[:], 0.0),
        lambda e: e.affine_select(out=identf[:], in_=identf[:], pattern=[[-1, 128]], compare_op=ALU.not_equal, fill=1.0, base=0, channel_multiplier=1),
        lambda e: e.memset(onesf[:], 1.0),
        lambda e: e.memset(Umat[0:64, :], 1.0),
        lambda e: e.affine_select(out=Umat[0:64, :], in_=Umat[0:64, :], pattern=[[1, 64]], compare_op=ALU.is_ge, fill=0.0, base=0, channel_multiplier=-1),
        lambda e: e.memset(SUmat[0:64, :], 1.0),
        lambda e: e.affine_select(out=SUmat[0:64, :], in_=SUmat[0:64, :], pattern=[[-1, 64]], compare_op=ALU.is_gt, fill=0.0, base=0, channel_multiplier=1),
        lambda e: e.memset(CONST[:, 0:1], EPS),
        lambda e: e.memset(CONST[:, 1:2], 1.0),
        lambda e: e.memset(CONST[:, 2:3], 0.0),
        lambda e: e.memset(VBf[:], 1.0),
        lambda e: e.memset(VP[:], 1.0),
        lambda e: e.memset(CIN[:], 0.0),
        lambda e: e.tensor_copy(out=onesb[:], in_=onesf[:]),
        lambda e: e.tensor_copy(out=Umb[0:64, :], in_=Umat[0:64, :]),
        lambda e: e.tensor_copy(out=identb[:], in_=identf[:]),
    ]:
        P.op("pool", fn, writes=cw_)
    ktok = [TL(f"KT{l}_", 6) for l in range(DEPTH)]
    vtok = [TL(f"VB{l}_", 6) for l in range(DEPTH)]
    for l in range(DEPTH):
        for t in vtok[l]:
            t.w = T("VBall").w
    T("VP").w = T("VPall").w

    t_rows = T("SCR")

    def rows_to_fm(src_ap_rows, R, C, dst_fn, extra_reads=()):
        assert R == 1
        h, off0 = src_ap_rows
        for c0 in range(0, C, 2048):
            cw = min(2048, C - c0)
            P.dma("sp", V(ROWS, 0, 1, 0, [(1, cw)]), DA(h, off0 + c0, [(C, 1), (1, cw)]), s_in, writes=[t_rows])
            nj = cw // 128
            b, bt = ps_single()

            def tr(e, nj=nj, b=b):
                for jj in range(nj):
                    last = e.transpose(pf(b, 128, jj, [(1, 1)]), V(ROWS, 0, 1, jj * 128, [(1, 128)]), V(identf, 0, 1, 0, [(1, 1)]))
                return last
            P.op("pe", tr, reads=[t_rows, t_const], writes=bt)
            dst, dtoks = dst_fn(c0 // 128, nj)
            P.op("dve", lambda e, b=b, nj=nj, dst=dst: e.tensor_copy(out=dst, in_=pf(b, 128, 0, [(1, nj), (1, 1)])), reads=bt, writes=dtoks)

    def bc_load(dst_ap, h, off, n, tok):
        P.dma("sp", dst_ap, DA(h, off, [(0, 64), (1, n)]), s_in2, writes=[tok])

    t_lc = T("layerconst")
    for l in range(DEPTH):
        for j0 in range(0, 24, 12):
            P.dma("sp", V(ROWS, 0, 4, 0, [(1, 1536)]), DA(conv_w, l * 4 * 3072 + j0 * 128, [(3072, 4), (1, 1536)]), s_in, writes=[t_rows])
            P.dma("sp", V(ROWS, 4, 1, 0, [(1, 1536)]), DA(conv_b, l * 3072 + j0 * 128, [(3072, 1), (1, 1536)]), s_in, writes=[t_rows])
            b, bt = ps_single()

            def tr(e, j0=j0, b=b):
                for jj in range(12):
                    last = e.transpose(pf(b, 128, jj * 5, [(1, 5)]), V(ROWS, 0, 5, jj * 128, [(1, 128)]), V(identf, 0, 5, 0, [(1, 5)]))
                return last
            P.op("pe", tr, reads=[t_rows, t_const], writes=bt)
            P.op("dve", lambda e, b=b, j0=j0, l=l: e.tensor_copy(out=V(CWB, 0, 128, l * 120 + j0 * 5, [(1, 60)]), in_=pf(b, 128, 0, [(1, 60)])), reads=bt, writes=[t_lc])
        rows_to_fm((norm_mix_g, l * D), 1, D, lambda j0, nj, l=l: (V(GN, 0, 128, l * 16 + j0 * 2, [(2, nj), (1, 1)]), [t_lc]))
        rows_to_fm((norm_ffn_g, l * D), 1, D, lambda j0, nj, l=l: (V(GN, 0, 128, l * 16 + j0 * 2 + 1, [(2, nj), (1, 1)]), [t_lc]))
        rows_to_fm((ssm_norm_g, l * 2048), 1, 2048, lambda j0, nj, l=l: (V(GSSM, 0, 128, l * 16 + j0, [(1, nj), (1, 1)]), [t_lc]))
        rows_to_fm((ada_b, l * 6144), 1, 6144, lambda j0, nj, l=l: (V(ADAB, 0, 128, l * 48 + j0, [(1, nj), (1, 1)]), [t_lc]))
        bc_load(V(DTB, 0, 64, l * 32, [(1, 32)]), dt_bias, l * 32, 32, t_lc)
        bc_load(V(ABC, 0, 64, l * 32, [(1, 32)]), a_log, l * 32, 32, t_lc)
        bc_load(V(DSK, 0, 64, l * 32, [(1, 32)]), d_skip, l * 32, 32, t_lc)
        bc_load(V(QG8, 0, 64, l * 64, [(1, 64)]), q_norm_g, l * 64, 64, t_lc)
        bc_load(V(KG, 0, 64, l * 64, [(1, 64)]), k_norm_g, l * 64, 64, t_lc)
        bc_load(V(ESINK, 0, 64, l * 16, [(1, 16)]), sinks, l * 16, 16, t_lc)
        P.dma("pool", V(WDT, 0, 128, l * 256, [(32, 8), (1, 32)]), DA(w_in, l * D * INC + C_DT, [(INC, 128), (128 * INC, 8), (1, 32)]), s_pool, writes=[t_lc])
    P.op("act", lambda e: e.activation(out=V(ABC, 0, 64, 0, [(1, DEPTH * 32)]), in_=V(ABC, 0, 64, 0, [(1, DEPTH * 32)]), func=AF.Exp), reads=[t_lc], writes=[t_lc])
    P.op("dve", lambda e: e.tensor_scalar(out=V(ABC, 0, 64, 0, [(1, DEPTH * 32)]), in0=V(ABC, 0, 64, 0, [(1, DEPTH * 32)]), scalar1=-1.0, scalar2=None, op0=ALU.mult), reads=[t_lc], writes=[t_lc])
    P.op("act", lambda e: e.activation(out=V(ESINK, 0, 64, 0, [(1, DEPTH * 16)]), in_=V(ESINK, 0, 64, 0, [(1, DEPTH * 16)]), func=AF.Exp), reads=[t_lc], writes=[t_lc])
    P.op("dve", lambda e: e.tensor_scalar(out=V(QG8, 0, 64, 0, [(1, DEPTH * 64)]), in0=V(QG8, 0, 64, 0, [(1, DEPTH * 64)]), scalar1=0.125, scalar2=None, op0=ALU.mult), reads=[t_lc], writes=[t_lc])
    P.dma("sp", V(RELF, 0, 32, 0, [(1, 16)]), DA(rel_bias, 0, [(16, 32), (1, 16)]), s_in2, writes=[T("relf")])
    P.op("dve", lambda e: e.tensor_copy(out=V(RELB, 0, 32, 0, [(1, 16)]), in_=V(RELF, 0, 32, 0, [(1, 16)])), reads=[T("relf")], writes=[T("relb")])
    for blk in range(3):
        i = wctr[0] % NWB; wctr[0] += 1
        P.dma("pool", V(WB[i], 0, 32, 0, [(1, 4096)]), DA(onehot, blk * 4096, [(3 * 4096, 32), (1, 4096)]), wslot[i], writes=[wtok[i]])
        b, bt = ps_pair()

        def ohmm(e, i=i, b=b):
            for q in range(64):
                last = e.matmul(pf(b, 64, q * 16, [(1, 16)]), lhsT=V(WB[i], 0, 32, q * 64, [(1, 64)]), rhs=V(RELB, 0, 32, 0, [(1, 16)]), start=True, stop=True)
            return last
        P.op("pe", ohmm, reads=[wtok[i], T("relb")], writes=bt)
        P.op("act", lambda e, b=b, blk=blk: e.activation(out=V(BIAS, 0, 64, blk * 1024, [(64, 16), (1, 64)]), in_=pf(b, 64, 0, [(1, 16), (16, 64)]), func=AF.Exp), reads=bt, writes=[T("bias")])
    t_scr = T("SCR")
    P.dma("sp", V(CROW, 0, NSEQ, 0, [(1, 1024)]), DA(cvec, 0, [(D, NSEQ), (1, D)]), s_in, writes=[T("SCR")])
    P.op("act", lambda e: e.activation(out=V(CROW, 0, NSEQ, 0, [(1, 1024)]), in_=V(CROW, 0, NSEQ, 0, [(1, 1024)]), func=AF.Silu), reads=[T("SCR")], writes=[T("SCR")])
    b, bt = ps_single()

    def trc(e, b=b):
        for k in range(8):
            last = e.transpose(pf(b, 128, k * NSEQ, [(1, NSEQ)]), V(CROW, 0, NSEQ, k * 128, [(1, 128)]), V(identf, 0, NSEQ, 0, [(1, NSEQ)]))
        return last
    P.op("pe", trc, reads=[T("SCR"), t_const], writes=bt)
    P.op("dve", lambda e, b=b: e.tensor_copy(out=V(SCT, 0, 128, 0, [(1, 8 * NSEQ)]), in_=pf(b, 128, 0, [(1, 8 * NSEQ)])), reads=bt, writes=[T("sct")])
    for l in range(DEPTH):
        b, bt = ps_single()
        for g in range(12):
            wbuf, wt = wload([(512, 8), (1, 512)], DA(ada_w, l * D * 6144 + g * 512, [(6144, 128), (128 * 6144, 8), (1, 512)]))

            def adamm(e, wbuf=wbuf, g=g, b=b):
                for jj in range(4):
                    j = g * 4 + jj
                    for k in range(8):
                        last = e.matmul(pf(b, 128, j * NSEQ, [(1, NSEQ)]), lhsT=V(wbuf, 0, 128, k * 512 + jj * 128, [(1, 128)]),
                                        rhs=V(SCT, 0, 128, k * NSEQ, [(1, NSEQ)]), start=(k == 0), stop=(k == 7))
                return last
            P.op("pe", adamm, reads=[wt, T("sct")], writes=bt)
        P.op("dve", lambda e, b=b, l=l: e.tensor_tensor(out=V(MODT, 0, 128, l * 48 * NSEQ, [(NSEQ, 48), (1, NSEQ)]), in0=pf(b, 128, 0, [(NSEQ, 48), (1, NSEQ)]),
                                                         in1=V(ADAB, 0, 128, l * 48, [(1, 48), (0, NSEQ)]), op=ALU.add), reads=bt + [t_lc], writes=[T("modt")])
        for mf in range(2):
            sc0 = 8 if mf == 0 else 32
            P.op("dve", lambda e, l=l, mf=mf, sc0=sc0: e.scalar_tensor_tensor(
                out=V(GMUL, 0, 128, (l * 2 + mf) * 8 * NSEQ, [(NSEQ, 8), (1, NSEQ)]),
                in0=V(MODT, 0, 128, (l * 48 + sc0) * NSEQ, [(NSEQ, 8), (1, NSEQ)]), scalar=1.0,
                in1=V(GN, 0, 128, l * 16 + mf, [(2, 8), (0, NSEQ)]), op0=ALU.add, op1=ALU.mult), reads=[T("modt"), t_lc], writes=[T("gmul")])

    def modv(l, which, k, seq):
        return V(MODT, 0, 128, (l * 48 + which * 8 + k) * NSEQ + seq, [(1, 1)])

    cpar = [0]
    exctr = [0]
    uctr = [0]
    import os as _os
    _kstop = int(_os.environ.get("K_STOP", "-1"))
    _kctr = [0]

    class StopBuild(Exception):
        pass

    def ckpt(name):
        if _kstop >= 0:
            print("CKPT", _kctr[0], name, flush=True)
            if _kctr[0] == _kstop:
                raise StopBuild()
        _kctr[0] += 1

    def acttok(j):
        return T(f"XC{j}") if j < 16 else (T(f"BT{j - 16}") if j < 20 else T(f"CT{j - 20}"))

    def do_tile(kind, seq, kt):
        if kind == "p":
            nseg, L, seq0 = 1, NT, seq
            xsrc, ydst, row0 = xp, yp, seq * PLEN + kt * NT
            last_tile = (kt == NPT - 1)
        else:
            nseg, L, seq0 = NSS, 64, NPS
            xsrc, ydst, row0 = xs, ys, 0
            last_tile = True
        txT = TL("xT", 8)
        for sub in range(2):
            P.dma("sp", V(SCR, 0, 128, 0, [(1, 1024)]), DA(xsrc, (row0 + sub * 128) * D, [(D, 128), (1, D)]), s_in, writes=[t_scr])
            b, bt = ps_pair()

            def trx(e, b=b):
                for k in range(8):
                    last = e.transpose(pf(b, 128, k * 128, [(1, 128)]), V(SCR, 0, 128, k * 128, [(1, 128)]), identf[:, :])
                return last
            P.op("pe", trx, reads=[t_scr, t_const], writes=bt)
            P.op("dve", lambda e, b=b, sub=sub: e.tensor_copy(out=V(xT, 0, 128, sub * 128, [(NT, 8), (1, 128)]), in_=pf(b, 128, 0, [(128, 8), (1, 128)])), reads=bt, writes=txT)

        ckpt("xload")

        def norm_to_hT(l, mf):
            P.op("act", lambda e: e.activation(out=SQ[:, :], in_=xT[:, :], func=AF.Square), reads=txT, writes=[T("DEC_0"), T("DEC_1")])
            b, bt = ps_single()

            def ssmm(e, b=b):
                for k in range(8):
                    last = e.matmul(pf(b, 128, 0, [(1, NT)]), lhsT=onesb[:, :], rhs=V(SQ, 0, 128, k * NT, [(1, NT)]), start=(k == 0), stop=(k == 7))
                return last
            P.op("pe", ssmm, reads=[T("DEC_0"), T("DEC_1"), t_const], writes=bt)
            P.op("act", lambda e, b=b: e.activation(out=RSTD[:, :], in_=pf(b, 128, 0, [(1, NT)]), func=AF.Sqrt, bias=CONST[:, 0:1], scale=1.0 / D), reads=bt + [t_const], writes=[T("RSTD")])
            P.op("dve", lambda e: e.reciprocal(out=RSTD[:, :], in_=RSTD[:, :]), reads=[T("RSTD")], writes=[T("RSTD")])
            for k in range(8):
                tr1 = T(f"R1_{k // 4}")
                for sg in range(nseg):
                    o_ = k * NT + sg * L
                    P.op("dve", lambda e, k=k, sg=sg, o_=o_: e.scalar_tensor_tensor(
                        out=V(NTMP, 0, 128, o_, [(1, L)]), in0=V(xT, 0, 128, o_, [(1, L)]),
                        scalar=V(GMUL, 0, 128, ((l * 2 + mf) * 8 + k) * NSEQ + seq0 + sg, [(1, 1)]),
                        in1=V(RSTD, 0, 128, sg * L, [(1, L)]), op0=ALU.mult, op1=ALU.mult),
                        reads=[txT[k], T("RSTD"), T("gmul")], writes=[tr1])
                    P.op("act", lambda e, k=k, sg=sg, o_=o_: e.activation(
                        out=V(hT, 0, 128, o_, [(1, L)]), in_=V(NTMP, 0, 128, o_, [(1, L)]), func=AF.Identity,
                        bias=modv(l, 0 if mf == 0 else 3, k, seq0 + sg), scale=1.0),
                        reads=[tr1, T("modt")], writes=[T(f"hT{k}")])

        def fm_proj(wbuf, wt, kc, col0, rhs_t, rhs_tok, nk, nreads=1):
            b, bt = ps_single(nreads)

            def mm(e, b=b):
                for k in range(nk):
                    last = e.matmul(pf(b, 128, 0, [(1, NT)]), lhsT=V(wbuf, 0, 128, k * kc + col0, [(1, 128)]), rhs=V(rhs_t, 0, 128, k * NT, [(1, NT)]),
                                    start=(k == 0), stop=(k == nk - 1))
                return last
            P.op("pe", mm, reads=[wt] + rhs_tok, writes=bt)
            return b, bt

        for l in range(DEPTH):
            win0 = l * D * INC

            def win_ap(c0, w):
                return DA(w_in, win0 + c0, [(INC, 128), (128 * INC, 8), (1, w)])

            if kind == "p" and kt == 0:
                P.op("pool", lambda e, l=l: e.memset(V(CIN, 0, 128, l * 288, [(12, 24), (1, 3)]), 0.0), writes=[T(f"CIN{l}")])
            if kind == "s":
                R = 3 * NSS
                for j0 in range(0, 24, 12):
                    P.dma("sp", V(SCR, 0, R, 0, [(1, 1536)]), DA(sconv, l * R * 3072 + j0 * 128, [(3072, R), (1, 1536)]), s_in, writes=[t_scr])
                    b, bt = ps_single()

                    def trs(e, j0=j0, b=b, R=R):
                        for jj in range(12):
                            last = e.transpose(pf(b, 128, jj * R, [(1, R)]), V(SCR, 0, R, jj * 128, [(1, 128)]), V(identf, 0, R, 0, [(1, R)]))
                        return last
                    P.op("pe", trs, reads=[t_scr, t_const], writes=bt)
                    P.op("dve", lambda e, b=b, j0=j0, l=l, R=R: e.tensor_copy(out=V(CIN, 0, 128, l * 288 + j0 * 12, [(12, 12), (1, R)]), in_=pf(b, 128, 0, [(R, 12), (1, R)])),
                         reads=bt, writes=[T(f"CIN{l}")])
            P.op("dve", lambda e, l=l: e.tensor_tensor(out=V(DI, 0, 64, 0, [(64, 32), (1, 64)]), in0=V(DSK, 0, 64, l * 32, [(1, 32), (0, 64)]),
                                                        in1=V(identf, 0, 64, 0, [(0, 32), (1, 64)]), op=ALU.mult), reads=[t_lc, t_const], writes=[T("DI")])
            if kind == "p" and kt > 0:
                P.op("act", lambda e, l=l: e.activation(out=STB[:, :], in_=V(ST, 0, 128, l * 2048, [(1, 2048)]), func=AF.Copy), reads=[T(f"ST{l}_0"), T(f"ST{l}_1")], writes=[T("STB_0"), T("STB_1")])
            ckpt("layerstart")
            norm_to_hT(l, 0)
            thT = TL("hT", 8)
            ckpt("norm1")
            for g6 in range(6):
                wbuf, wt = wload([(512, 8), (1, 512)], win_ap(C_XBC + g6 * 512, 512), key=("xbc", l, g6))
                for jj in range(4):
                    j = g6 * 4 + jj
                    b, bt = fm_proj(wbuf, wt, 512, jj * 128, hT, thT, 8)
                    ci = cpar[0] % 2; cpar[0] += 1
                    S, ACC = CS[ci], CACC[ci]
                    tS, tA = T(f"CS{ci}"), T(f"CACC{ci}")
                    tcin = T(f"CIN{l}")
                    P.op("pool", lambda e, S=S, j=j, l=l: e.tensor_copy(out=V(S, 0, 128, 0, [(L + 3, nseg), (1, 3)]), in_=V(CIN, 0, 128, l * 288 + j * 12, [(3, nseg), (1, 3)])), reads=[tcin], writes=[tS])
                    P.op("act", lambda e, S=S, b=b: e.activation(out=V(S, 0, 128, 3, [(L + 3, nseg), (1, L)]), in_=pf(b, 128, 0, [(L, nseg), (1, L)]), func=AF.Copy), reads=bt, writes=[tS])
                    P.op("pool", lambda e, S=S, j=j, l=l: e.tensor_copy(out=V(CIN, 0, 128, l * 288 + j * 12, [(3, nseg), (1, 3)]), in_=V(S, 0, 128, L, [(L + 3, nseg), (1, 3)])), reads=[tS], writes=[tcin])

                    cwv = lambda k, j=j, l=l: V(CWB, 0, 128, l * 120 + j * 5 + k, [(1, 1)])
                    P.op("act", lambda e, S=S, ACC=ACC, cwv=cwv: e.activation(out=V(ACC, 0, 128, 0, [(L, nseg), (1, L)]), in_=V(S, 0, 128, 0, [(L + 3, nseg), (1, L)]),
                                                                       func=AF.Copy, scale=cwv(0)), reads=[tS, t_lc], writes=[tA])
                    for k in range(1, 4):
                        P.op("dve", lambda e, S=S, ACC=ACC, cwv=cwv, k=k: e.scalar_tensor_tensor(out=V(ACC, 0, 128, 0, [(L, nseg), (1, L)]), in0=V(S, 0, 128, k, [(L + 3, nseg), (1, L)]),
                                                                                       scalar=cwv(k), in1=V(ACC, 0, 128, 0, [(L, nseg), (1, L)]), op0=ALU.mult, op1=ALU.add),
                             reads=[tS, t_lc, tA], writes=[tA])
                    if j < 16:
                        dst, dtok = V(XC, 0, 128, j * NT, [(1, NT)]), T(f"XC{j}")
                    elif j < 20:
                        dst, dtok = V(PH1, 0, 128, BT_OFF + (j - 16) * NT, [(1, NT)]), T(f"BT{j - 16}")
                    else:
                        dst, dtok = V(PH1, 0, 128, CT_OFF + (j - 20) * NT, [(1, NT)]), T(f"CT{j - 20}")
                    P.op("act", lambda e, ACC=ACC, dst=dst, j=j, l=l: e.activation(out=dst, in_=ACC[:, :], func=AF.Silu, bias=V(CWB, 0, 128, l * 120 + j * 5 + 4, [(1, 1)]), scale=1.0),
                         reads=[tA, t_lc], writes=[dtok])
            ckpt("xbc")
            if last_tile:
                R = 3 * nseg
                for half in range(2):
                    b, bt = ps_quad()

                    def trc2(e, half=half, b=b, R=R, l=l):
                        for jj in range(12):
                            j = half * 12 + jj
                            last = e.transpose(pf(b, R, jj * 128, [(1, 128)]), V(CIN, 0, 128, l * 288 + j * 12, [(1, R)]), identf[:, :])
                        return last
                    P.op("pe", trc2, reads=[T(f"CIN{l}"), t_const], writes=bt)
                    P.op("act", lambda e, b=b, R=R: e.activation(out=V(SCR, 0, R, 0, [(1, 1536)]), in_=pf(b, R, 0, [(1, 1536)]), func=AF.Copy), reads=bt, writes=[t_scr])
                    if kind == "p":
                        dst = DA(ncp, ((l * NPS + seq) * 3) * 3072 + half * 1536, [(3072, 3), (1, 1536)])
                    else:
                        dst = DA(ncs, (l * 3 * NSS) * 3072 + half * 1536, [(3072, 3 * NSS), (1, 1536)])
                    P.dma("sp", dst, V(SCR, 0, R, 0, [(1, 1536)]), out_slot(), reads=[t_scr])
            ckpt("newconv")
            for g4 in range(4):
                wbuf, wt = wload([(512, 8), (1, 512)], win_ap(C_Z + g4 * 512, 512), key=("z", l, g4))
                for jj in range(4):
                    j = g4 * 4 + jj
                    b, bt = fm_proj(wbuf, wt, 512, jj * 128, hT, thT, 8)
                    P.op("act", lambda e, b=b, j=j: e.activation(out=V(ZT, 0, 128, j * NT, [(1, NT)]), in_=pf(b, 128, 0, [(1, NT)]), func=AF.Silu), reads=bt, writes=[T(f"ZT{j}")])
            for g2 in range(2):
                wbuf, wt = wload([(512, 8), (1, 512)], win_ap(C_Q + g2 * 512, 512), key=("q", l, g2))
                for jj in range(4):
                    j = g2 * 4 + jj
                    b, bt = fm_proj(wbuf, wt, 512, jj * 128, hT, thT, 8)
                    P.op("dve", lambda e, b=b, j=j: e.tensor_copy(out=V(QTr, 0, 128, j * NT, [(1, NT)]), in_=pf(b, 128, 0, [(1, NT)])), reads=bt, writes=[T(f"QTr{j}")])
            ckpt("zq")
            bdt, bdtt = ps_single()

            def dtmm(e, b=bdt, l=l):
                for c in range(NCH):
                    for k in range(8):
                        last = e.matmul(pf(b, 64, c * 32, [(1, 32)]), lhsT=V(hT, 0, 128, k * NT + c * 64, [(1, 64)]), rhs=V(WDT, 0, 128, l * 256 + k * 32, [(1, 32)]),
                                        start=(k == 0), stop=(k == 7))
                return last
            P.op("pe", dtmm, reads=thT + [t_lc], writes=bdtt)
            tdt = T("dtv")
            P.op("dve", lambda e, b=bdt, l=l: e.tensor_tensor(out=V(DTX, 0, 64, 0, [(32, NCH), (1, 32)]), in0=pf(b, 64, 0, [(32, NCH), (1, 32)]),
                                                               in1=V(DTB, 0, 64, l * 32, [(0, NCH), (1, 32)]), op=ALU.add), reads=bdtt + [t_lc], writes=[tdt])
            P.op("act", lambda e: e.activation(out=DTA[0:64, :], in_=DTX[0:64, :], func=AF.Abs), reads=[tdt], writes=[tdt])
            P.op("act", lambda e: e.activation(out=DTE[0:64, :], in_=DTA[0:64, :], func=AF.Exp, scale=-1.0), reads=[tdt], writes=[tdt])
            P.op("act", lambda e: e.activation(out=DTE[0:64, :], in_=DTE[0:64, :], func=AF.Ln, bias=CONST[0:64, 1:2], scale=1.0), reads=[tdt, t_const], writes=[tdt])
            P.op("dve", lambda e: e.scalar_tensor_tensor(out=DTv[0:64, :], in0=DTX[0:64, :], scalar=0.0, in1=DTE[0:64, :], op0=ALU.max, op1=ALU.add), reads=[tdt], writes=[tdt])
            P.op("dve", lambda e, l=l: e.tensor_tensor(out=V(DAv, 0, 64, 0, [(32, NCH), (1, 32)]), in0=V(DTv, 0, 64, 0, [(32, NCH), (1, 32)]),
                                                        in1=V(ABC, 0, 64, l * 32, [(0, NCH), (1, 32)]), op=ALU.mult), reads=[tdt, t_lc], writes=[tdt])
            ckpt("dt")
            wbuf, wt = wload([(512, 8), (1, 512)], win_ap(C_K, 512), key=("kv", l))
            for c in range(NCH):
                cidx = kt * NCH + c if kind == "p" else 0
                blk = 2 + c
                b, bt = ps_single(3)

                def kvmm(e, b=b, c=c, wbuf=wbuf):
                    for k in range(8):
                        last = e.matmul(pf(b, 64, 0, [(1, 512)]), lhsT=V(hT, 0, 128, k * NT + c * 64, [(1, 64)]), rhs=V(wbuf, 0, 128, k * 512, [(1, 512)]),
                                        start=(k == 0), stop=(k == 7))
                    return last
                P.op("pe", kvmm, reads=thT + [wt], writes=bt)
                oi = 0
                KOb, VOb = KO[oi], VO[oi]
                tko, tvo = T(f"KO{oi}"), T(f"VO{oi}")
                tkk = T("ktmp")
                P.op("act", lambda e, b=b: e.activation(out=SQK[0:64, :], in_=pf(b, 64, 0, [(1, 256)]), func=AF.Square), reads=bt, writes=[tkk])
                P.op("dve", lambda e: e.tensor_reduce(out=SSK[0:64, :], in_=V(SQK, 0, 64, 0, [(64, 4), (1, 64)]), axis=AX.X, op=ALU.add), reads=[tkk], writes=[tkk])
                P.op("act", lambda e: e.activation(out=RK[0:64, :], in_=SSK[0:64, :], func=AF.Sqrt, bias=CONST[0:64, 0:1], scale=1.0 / 64), reads=[tkk, t_const], writes=[tkk])
                P.op("dve", lambda e: e.reciprocal(out=RK[0:64, :], in_=RK[0:64, :]), reads=[tkk], writes=[tkk])
                P.op("dve", lambda e, b=b, KOb=KOb: e.tensor_tensor(out=V(KOb, 0, 64, 0, [(64, 4), (1, 64)]), in0=pf(b, 64, 0, [(64, 4), (1, 64)]),
                                                                     in1=V(RK, 0, 64, 0, [(1, 4), (0, 64)]), op=ALU.mult), reads=bt + [tkk], writes=[tko])
                P.op("dve", lambda e, KOb=KOb, l=l: e.tensor_tensor(out=V(KOb, 0, 64, 0, [(64, 4), (1, 64)]), in0=V(KOb, 0, 64, 0, [(64, 4), (1, 64)]),
                                                                     in1=V(KG, 0, 64, l * 64, [(0, 4), (1, 64)]), op=ALU.mult), reads=[tko, t_lc], writes=[tko])
                P.op("act", lambda e, KOb=KOb: e.activation(out=KNB[0:64, :], in_=KOb[0:64, :], func=AF.Copy), reads=[tko], writes=[T("KNB")])
                b2, bt2 = ps_single()

                def trkk(e, b2=b2):
                    for kv in range(4):
                        last = e.transpose(pb(b2, 64, kv * 64, [(1, 64)]), V(KNB, 0, 64, kv * 64, [(1, 64)]), V(identb, 0, 64, 0, [(1, 64)]))
                    return last
                P.op("pe", trkk, reads=[T("KNB"), t_const], writes=bt2)
                P.op("dve", lambda e, b2=b2, l=l, blk=blk: e.tensor_copy(out=V(KT, 0, 64, l * 1536 + blk * 64, [(384, 4), (1, 64)]), in_=pb(b2, 64, 0, [(64, 4), (1, 64)])),
                     reads=bt2, writes=[ktok[l][blk]])
                P.op("act", lambda e, b=b, VOb=VOb: e.activation(out=VOb[0:64, :], in_=pf(b, 64, 256, [(1, 256)]), func=AF.Copy), reads=bt, writes=[tvo])
                P.op("dve", lambda e, VOb=VOb, l=l, blk=blk: e.tensor_copy(out=V(VBf, 0, 64, l * 1560 + blk * 260, [(65, 4), (1, 64)]), in_=V(VOb, 0, 64, 0, [(64, 4), (1, 64)])),
                     reads=[tvo], writes=[vtok[l][blk]])
                if kind == "s":
                    P.dma("sp", DA(nks, (l * NSS + c) * 64 * 256, [(256, 64), (1, 256)]), KOb[0:64, :], out_slot(), reads=[tko])
                    P.dma("sp", DA(nvs, (l * NSS + c) * 64 * 256, [(256, 64), (1, 256)]), VOb[0:64, :], out_slot(), reads=[tvo])
                elif cidx >= PLEN // 64 - 2:
                    r0 = (cidx - (PLEN // 64 - 2)) * 64
                    P.dma("sp", DA(nkp, ((l * NPS + seq) * 128 + r0) * 256, [(256, 64), (1, 256)]), KOb[0:64, :], out_slot(), reads=[tko])
                    P.dma("sp", DA(nvp, ((l * NPS + seq) * 128 + r0) * 256, [(256, 64), (1, 256)]), VOb[0:64, :], out_slot(), reads=[tvo])

            ckpt("kv")
            tST, tSTB = [T(f"ST{l}_0"), T(f"ST{l}_1")], [T("STB_0"), T("STB_1")]
            for c in range(NCH):
                cs = c * 64
                cidx = kt * NCH + c if kind == "p" else 0
                first = (kind == "p" and cidx == 0)
                if kind == "s":
                    for part in range(2):
                        P.dma("sp", V(SCR, 0, 128, part * 1024, [(128, 8), (1, 128)]), DA(sssm, (l * NSS + c) * 2048 * 128 + part * 1024 * 128, [(128, 128), (128 * 128, 8), (1, 128)]),
                              s_in, writes=[t_scr], cont=(part > 0))
                    b, bt = ps_quad()

                    def trst(e, b=b):
                        for j in range(16):
                            last = e.transpose(pf(b, 128, j * 128, [(1, 128)]), V(SCR, 0, 128, j * 128, [(1, 128)]), identf[:, :])
                        return last
                    P.op("pe", trst, reads=[t_scr, t_const], writes=bt)
                    P.op("dve", lambda e, b=b, l=l: e.tensor_copy(out=V(ST, 0, 128, l * 2048, [(1, 2048)]), in_=pf(b, 128, 0, [(1, 2048)])), reads=bt, writes=tST)
                    P.op("act", lambda e, l=l: e.activation(out=STB[:, :], in_=V(ST, 0, 128, l * 2048, [(1, 2048)]), func=AF.Copy), reads=tST, writes=tSTB)
                ckpt("stateload")
                exi = exctr[0] % 2; exctr[0] += 1
                EX = EXs[exi]; tEX = T(f"EX{exi}")
                bs, bst = ps_single(2)

                def smallmm(e, b=bs, c=c):
                    da = V(DAv, 0, 64, c * 32, [(1, 32)])
                    e.matmul(pf(b, 64, 0, [(1, 32)]), lhsT=Umat[0:64, :], rhs=da, start=True, stop=True)
                    e.matmul(pf(b, 64, 32, [(1, 32)]), lhsT=SUmat[0:64, :], rhs=da, start=True, stop=True)
                    return e.matmul(pf(b, 128, 64, [(1, 32)]), lhsT=V(onesf, 0, 64, 0, [(1, 128)]), rhs=da, start=True, stop=True)
                P.op("pe", smallmm, reads=[tdt, t_const], writes=bst)
                P.op("act", lambda e, b=bs, EX=EX: e.activation(out=EX[0:64, 0:64], in_=pf(b, 64, 0, [(1, 64)]), func=AF.Exp), reads=bst, writes=[tEX])
                P.op("act", lambda e, b=bs, EX=EX: e.activation(out=EX[:, 64:96], in_=pf(b, 128, 64, [(1, 32)]), func=AF.Exp), reads=bst, writes=[tEX])
                for hh in range(2):
                    u = uctr[0] % 2; uctr[0] += 1
                    o1 = u * 1024
                    h0, g0 = hh * 16, hh * 2
                    tR1, tDEC, tXTs, tXDT, tTb, tCBM, tBTOK = [T(f"{n}{u}") for n in ("R1_", "DEC_", "XTs_", "XDT_", "Tb_", "CBM_", "BTOK_")]
                    tSTh, tSTBh = T(f"ST{l}_{hh}"), T(f"STB_{hh}")
                    P.op("pool", lambda e, c=c, o1=o1, h0=h0: e.tensor_tensor(out=V(R1, 0, 64, o1, [(64, 16), (1, 64)]), in0=V(DAv, 0, 64, c * 32 + h0, [(1, 16), (0, 64)]),
                                                                           in1=V(Umat, 0, 64, 0, [(0, 16), (1, 64)]), op=ALU.mult), reads=[tdt, t_const], writes=[tR1])
                    bq, bqt = ps_pair()

                    def segmm(e, b=bq, o1=o1):
                        for i in range(2):
                            last = e.matmul(pf(b, 64, i * 512, [(1, 512)]), lhsT=SUmat[0:64, :], rhs=V(R1, 0, 64, o1 + i * 512, [(1, 512)]), start=True, stop=True)
                        return last
                    P.op("pe", segmm, reads=[tR1, t_const], writes=bqt)
                    P.op("act", lambda e, b=bq, o1=o1: e.activation(out=V(DEC, 0, 64, o1, [(1, 1024)]), in_=pf(b, 64, 0, [(1, 1024)]), func=AF.Exp), reads=bqt, writes=[tDEC])
                    bc_, bct = ps_single()

                    def cbmm(e, b=bc_, cs=cs, g0=g0):
                        for gg in range(2):
                            g = g0 + gg
                            last = e.matmul(pf(b, 64, gg * 64, [(1, 64)]), lhsT=V(PH1, 0, 128, BT_OFF + g * NT + cs, [(1, 64)]), rhs=V(PH1, 0, 128, CT_OFF + g * NT + cs, [(1, 64)]), start=True, stop=True)
                        return last
                    P.op("pe", cbmm, reads=[T(f"BT{g0}"), T(f"BT{g0 + 1}"), T(f"CT{g0}"), T(f"CT{g0 + 1}")], writes=bct)
                    P.op("dve", lambda e, b=bc_, u=u: e.tensor_tensor(out=V(CBM, 0, 64, u * 128, [(64, 2), (1, 64)]), in0=pf(b, 64, 0, [(64, 2), (1, 64)]),
                                                                       in1=V(Umb, 0, 64, 0, [(0, 2), (1, 64)]), op=ALU.mult), reads=bct + [t_const], writes=[tCBM])
                    P.op("dve", lambda e, o1=o1, u=u: e.tensor_tensor(out=V(DEC, 0, 64, o1, [(512, 2), (64, 8), (1, 64)]), in0=V(DEC, 0, 64, o1, [(512, 2), (64, 8), (1, 64)]),
                                                                       in1=V(CBM, 0, 64, u * 128, [(64, 2), (0, 8), (1, 64)]), op=ALU.mult), reads=[tDEC, tCBM], writes=[tDEC])
                    bx, bxt = ps_single(2)

                    def trxx(e, b=bx, cs=cs, hh=hh):
                        for j in range(8):
                            last = e.transpose(pb(b, 64, j * 128, [(1, 128)]), V(XC, 0, 128, (hh * 8 + j) * NT + cs, [(1, 64)]), identb[:, :])
                        return last
                    P.op("pe", trxx, reads=[T(f"XC{hh * 8 + j}") for j in range(8)] + [t_const], writes=bxt)
                    P.op("act", lambda e, b=bx, o1=o1: e.activation(out=V(XTs, 0, 64, o1, [(1, 1024)]), in_=pb(b, 64, 0, [(1, 1024)]), func=AF.Copy), reads=bxt, writes=[tXTs])
                    P.op("dve", lambda e, b=bx, c=c, o1=o1, h0=h0: e.tensor_tensor(out=V(XDT, 0, 64, o1, [(64, 16), (1, 64)]), in0=pb(b, 64, 0, [(64, 16), (1, 64)]),
                                                                                 in1=V(DTv, 0, 64, c * 32 + h0, [(1, 16), (0, 64)]), op=ALU.mult), reads=bxt + [tdt], writes=[tXDT])
                    P.op("pool", lambda e, o1=o1, h0=h0, EX=EX: e.tensor_tensor(out=V(XDTS, 0, 64, 2 * o1, [(64, 16), (1, 64)]), in0=V(XDT, 0, 64, o1, [(64, 16), (1, 64)]),
                                                                             in1=V(EX, 0, 64, 32 + h0, [(1, 16), (0, 64)]), op=ALU.mult), reads=[tXDT, tEX], writes=[tR1])
                    bb, bbt = ps_single()

                    def trb(e, b=bb, cs=cs, g0=g0):
                        for gg in range(2):
                            last = e.transpose(pb(b, 64, gg * 128, [(1, 128)]), V(PH1, 0, 128, BT_OFF + (g0 + gg) * NT + cs, [(1, 64)]), identb[:, :])
                        return last
                    P.op("pe", trb, reads=[T(f"BT{g0}"), T(f"BT{g0 + 1}"), t_const], writes=bbt)
                    P.op("act", lambda e, b=bb, u=u: e.activation(out=V(BTOK, 0, 64, u * 256, [(1, 256)]), in_=pb(b, 64, 0, [(1, 256)]), func=AF.Copy), reads=bbt, writes=[tBTOK])
                    by, byt = ps_pair()

                    def ymm(e, b=by, o1=o1, h0=h0):
                        for h in range(16):
                            e.matmul(pf(b, 64, h * 64, [(1, 64)]), lhsT=V(DEC, 0, 64, o1 + h * 64, [(1, 64)]), rhs=V(XDT, 0, 64, o1 + h * 64, [(1, 64)]), start=True, stop=False)
                            last = e.matmul(pf(b, 64, h * 64, [(1, 64)]), lhsT=V(DI, 0, 64, (h0 + h) * 64, [(1, 64)]), rhs=V(XTs, 0, 64, o1 + h * 64, [(1, 64)]), start=False, stop=True)
                        return last
                    P.op("pe", ymm, reads=[tDEC, tXDT, tXTs, T("DI")], writes=byt)
                    if not first:
                        bp, bpt = ps_pair()

                        def ypmm(e, b=bp, cs=cs, g0=g0):
                            for gg in range(2):
                                g = g0 + gg
                                last = e.matmul(pf(b, 64, gg * 512, [(1, 512)]), lhsT=V(PH1, 0, 128, CT_OFF + g * NT + cs, [(1, 64)]), rhs=V(STB, 0, 128, g * 512, [(1, 512)]), start=True, stop=True)
                            return last
                        P.op("pe", ypmm, reads=[T(f"CT{g0}"), T(f"CT{g0 + 1}"), tSTBh], writes=bpt)
                        P.op("dve", lambda e, b=bp, o1=o1, h0=h0, EX=EX: e.tensor_tensor(out=V(Tb, 0, 64, o1, [(64, 16), (1, 64)]), in0=pf(b, 64, 0, [(64, 16), (1, 64)]),
                                                                                      in1=V(EX, 0, 64, h0, [(1, 16), (0, 64)]), op=ALU.mult), reads=bpt + [tEX], writes=[tTb])
                        P.op("dve", lambda e, b=by, o1=o1: e.tensor_tensor(out=V(Tb, 0, 64, o1, [(1, 1024)]), in0=pf(b, 64, 0, [(1, 1024)]), in1=V(Tb, 0, 64, o1, [(1, 1024)]), op=ALU.add),
                             reads=byt + [tTb], writes=[tTb])
                    else:
                        P.op("dve", lambda e, b=by, o1=o1: e.tensor_copy(out=V(Tb, 0, 64, o1, [(1, 1024)]), in_=pf(b, 64, 0, [(1, 1024)])), reads=byt, writes=[tTb])
                    bs2, bs2t = ps_pair()

                    def stmm(e, b=bs2, o1=o1, u=u):
                        for gg in range(2):
                            last = e.matmul(pf(b, 128, gg * 512, [(1, 512)]), lhsT=V(BTOK, 0, 64, u * 256 + gg * 128, [(1, 128)]), rhs=V(XDTS, 0, 64, 2 * o1 + gg * 512, [(1, 512)]), start=True, stop=True)
                        return last
                    P.op("pe", stmm, reads=[tBTOK, tR1], writes=bs2t)
                    so = l * 2048 + hh * 1024
                    if first:
                        P.op("dve", lambda e, b=bs2, so=so: e.tensor_copy(out=V(ST, 0, 128, so, [(1, 1024)]), in_=pf(b, 128, 0, [(1, 1024)])), reads=bs2t, writes=[tSTh])
                    else:
                        P.op("pool", lambda e, so=so, h0=h0, EX=EX: e.tensor_tensor(out=V(ST, 0, 128, so, [(64, 16), (1, 64)]), in0=V(ST, 0, 128, so, [(64, 16), (1, 64)]),
                                                                                 in1=V(EX, 0, 128, 64 + h0, [(1, 16), (0, 64)]), op=ALU.mult), reads=[tSTh, tEX], writes=[tSTh])
                        P.op("dve", lambda e, b=bs2, so=so: e.tensor_tensor(out=V(ST, 0, 128, so, [(1, 1024)]), in0=V(ST, 0, 128, so, [(1, 1024)]),
                                                                             in1=pf(b, 128, 0, [(1, 1024)]), op=ALU.add), reads=bs2t + [tSTh], writes=[tSTh])
                    P.op("act", lambda e, so=so, hh=hh: e.activation(out=V(STB, 0, 128, hh * 1024, [(1, 1024)]), in_=V(ST, 0, 128, so, [(1, 1024)]), func=AF.Copy), reads=[tSTh], writes=[tSTBh])
                    bz, bzt = ps_single()

                    def trz(e, b=bz, cs=cs, hh=hh):
                        for j in range(8):
                            last = e.transpose(pb(b, 64, j * 128, [(1, 128)]), V(ZT, 0, 128, (hh * 8 + j) * NT + cs, [(1, 64)]), identb[:, :])
                        return last
                    P.op("pe", trz, reads=[T(f"ZT{hh * 8 + j}") for j in range(8)] + [t_const], writes=bzt)
                    P.op("dve", lambda e, b=bz, o1=o1: e.tensor_tensor(out=V(Tb, 0, 64, o1, [(1, 1024)]), in0=V(Tb, 0, 64, o1, [(1, 1024)]), in1=pb(b, 64, 0, [(1, 1024)]), op=ALU.mult),
                         reads=bzt + [tTb], writes=[tTb])
                    P.op("act", lambda e, o1=o1: e.activation(out=V(R1, 0, 64, o1, [(1, 1024)]), in_=V(Tb, 0, 64, o1, [(1, 1024)]), func=AF.Square), reads=[tTb], writes=[tR1])
                    tss = T(f"ss4_{u}")
                    P.op("dve", lambda e, o1=o1, u=u: e.tensor_reduce(out=V(SS4, 0, 64, u * 2, [(1, 2)]), in_=V(R1, 0, 64, o1, [(512, 2), (1, 512)]), axis=AX.X, op=ALU.add), reads=[tR1], writes=[tss])
                    P.op("act", lambda e, u=u: e.activation(out=V(RS4, 0, 64, u * 2, [(1, 2)]), in_=V(SS4, 0, 64, u * 2, [(1, 2)]), func=AF.Sqrt, bias=CONST[0:64, 0:1], scale=1.0 / 512), reads=[tss, t_const], writes=[tss])
                    P.op("dve", lambda e, u=u: e.reciprocal(out=V(RS4, 0, 64, u * 2, [(1, 2)]), in_=V(RS4, 0, 64, u * 2, [(1, 2)])), reads=[tss], writes=[tss])
                    P.op("dve", lambda e, o1=o1, u=u: e.tensor_tensor(out=V(YN, 0, 64, o1, [(512, 2), (1, 512)]), in0=V(Tb, 0, 64, o1, [(512, 2), (1, 512)]),
                                                                       in1=V(RS4, 0, 64, u * 2, [(1, 2), (0, 512)]), op=ALU.mult), reads=[tTb, tss], writes=[tXDT])
                    bt_, btt = ps_single()

                    def try_(e, b=bt_, o1=o1):
                        for j in range(8):
                            last = e.transpose(pb(b, 128, j * 64, [(1, 64)]), V(YN, 0, 64, o1 + j * 128, [(1, 128)]), V(identb, 0, 64, 0, [(1, 64)]))
                        return last
                    P.op("pe", try_, reads=[tXDT, t_const], writes=btt)
                    P.op("dve", lambda e, b=bt_, cs=cs, l=l, hh=hh: e.tensor_tensor(out=V(YT, 0, 128, hh * 8 * NT + cs, [(NT, 8), (1, 64)]), in0=pb(b, 128, 0, [(64, 8), (1, 64)]),
                                                                                  in1=V(GSSM, 0, 128, l * 16 + hh * 8, [(1, 8), (0, 64)]), op=ALU.mult), reads=btt + [t_lc], writes=[T(f"YT{c}")])
                final_state = (kind == "s") or (kind == "p" and cidx == PLEN // 64 - 1)
                if final_state:
                    b, bt = ps_quad()

                    def trso(e, b=b, l=l):
                        for j in range(16):
                            last = e.transpose(pf(b, 128, j * 128, [(1, 128)]), V(ST, 0, 128, l * 2048 + j * 128, [(1, 128)]), identf[:, :])
                        return last
                    P.op("pe", trso, reads=[T(f"ST{l}_0"), T(f"ST{l}_1"), t_const], writes=bt)
                    P.op("act", lambda e, b=b: e.activation(out=V(SCR, 0, 128, 0, [(1, 2048)]), in_=pf(b, 128, 0, [(1, 2048)]), func=AF.Copy), reads=bt, writes=[t_scr])
                    osl = out_slot()
                    for part in range(2):
                        if kind == "s":
                            dst = DA(nss, (l * NSS + c) * 2048 * 128 + part * 1024 * 128, [(128, 128), (128 * 128, 8), (1, 128)])
                        else:
                            dst = DA(nsp, (l * NPS + seq) * 2048 * 128 + part * 1024 * 128, [(128, 128), (128 * 128, 8), (1, 128)])
                        P.dma("sp", dst, V(SCR, 0, 128, part * 1024, [(128, 8), (1, 128)]), osl, reads=[t_scr], cont=(part > 0))
                bq_, bqt_ = ps_single(2)

                def trq(e, b=bq_, cs=cs):
                    for j in range(8):
                        last = e.transpose(pb(b, 64, j * 128, [(1, 128)]), V(QTr, 0, 128, j * NT + cs, [(1, 64)]), identb[:, :])
                    return last
                P.op("pe", trq, reads=TL("QTr", 8) + [t_const], writes=bqt_)
                tq = T("SQQ")
                P.op("act", lambda e, b=bq_: e.activation(out=SQQ[0:64, 0:1024], in_=pb(b, 64, 0, [(1, 1024)]), func=AF.Square), reads=bqt_, writes=[tq])
                P.op("dve", lambda e: e.tensor_reduce(out=SSQ[0:64, :], in_=V(SQQ, 0, 64, 0, [(64, 16), (1, 64)]), axis=AX.X, op=ALU.add), reads=[tq], writes=[tq])
                P.op("act", lambda e: e.activation(out=RQ[0:64, :], in_=SSQ[0:64, :], func=AF.Sqrt, bias=CONST[0:64, 0:1], scale=1.0 / 64), reads=[tq, t_const], writes=[tq])
                P.op("dve", lambda e: e.reciprocal(out=RQ[0:64, :], in_=RQ[0:64, :]), reads=[tq], writes=[tq])
                P.op("dve", lambda e, b=bq_: e.tensor_tensor(out=V(QN, 0, 64, 0, [(64, 16), (1, 64)]), in0=pb(b, 64, 0, [(64, 16), (1, 64)]),
                                                              in1=V(RQ, 0, 64, 0, [(1, 16), (0, 64)]), op=ALU.mult), reads=bqt_ + [tq], writes=[T("QN")])
                P.op("dve", lambda e, l=l: e.tensor_tensor(out=V(QN, 0, 64, 0, [(64, 16), (1, 64)]), in0=V(QN, 0, 64, 0, [(64, 16), (1, 64)]),
                                                            in1=V(QG8, 0, 64, l * 64, [(0, 16), (1, 64)]), op=ALU.mult), reads=[T("QN"), t_lc], writes=[T("QN")])
                bq2, bq2t = ps_single()

                def trq2(e, b=bq2):
                    for h in range(16):
                        last = e.transpose(pb(b, 64, h * 64, [(1, 64)]), V(QN, 0, 64, h * 64, [(1, 64)]), V(identb, 0, 64, 0, [(1, 64)]))
                    return last
                P.op("pe", trq2, reads=[T("QN"), t_const], writes=bq2t)
                P.op("act", lambda e, b=bq2: e.activation(out=QTc[0:64, :], in_=pb(b, 64, 0, [(1, 1024)]), func=AF.Copy), reads=bq2t, writes=[T("QTc")])
                if kind == "s":
                    P.dma("sp", V(SCR, 0, 128, 0, [(1, 256)]), DA(ck, (l * NSS + c) * 128 * 256, [(256, 128), (1, 256)]), s_in, writes=[t_scr])
                    P.op("act", lambda e: e.activation(out=V(SCR, 0, 128, 512, [(1, 128)]).bitcast(BF16), in_=V(SCR, 0, 128, 0, [(1, 256)]), func=AF.Copy), reads=[t_scr], writes=[t_scr])
                    bk_, bkt_ = ps_single()

                    def trk(e, b=bk_):
                        src = V(SCR, 0, 128, 512, [(1, 128)]).bitcast(BF16)
                        for kv in range(4):
                            last = e.transpose(pb(b, 64, kv * 128, [(1, 128)]), src[:, kv * 64:(kv + 1) * 64], identb[:, :])
                        return last
                    P.op("pe", trk, reads=[t_scr, t_const], writes=bkt_)
                    P.op("dve", lambda e, b=bk_: e.tensor_copy(out=V(KP, 0, 64, 0, [(128, 4), (1, 128)]), in_=pb(b, 64, 0, [(128, 4), (1, 128)])), reads=bkt_, writes=[T("KP")])
                    P.dma("sp", V(SCR, 0, 64, 1024, [(256, 2), (1, 256)]), DA(cv, (l * NSS + c) * 128 * 256, [(256, 64), (64 * 256, 2), (1, 256)]), s_in, writes=[t_scr])
                    P.op("dve", lambda e: e.tensor_copy(out=V(VP, 0, 64, 0, [(260, 2), (65, 4), (1, 64)]), in_=V(SCR, 0, 64, 1024, [(256, 2), (64, 4), (1, 64)])), reads=[t_scr], writes=[T("VP")])
                ckpt("att_q")
                blocks = []
                if kind == "p":
                    for pos in range(3):
                        if cidx + pos - 2 < 0:
                            continue
                        blk = c + pos
                        blocks.append((pos,
                                       lambda kv, blk=blk, l=l: V(KT, 0, 64, l * 1536 + kv * 384 + blk * 64, [(1, 64)]),
                                       lambda kv, blk=blk, l=l: V(VBf, 0, 64, l * 1560 + blk * 260 + kv * 65, [(1, 65)]),
                                       [ktok[l][blk], vtok[l][blk]]))
                else:
                    for pos in range(2):
                        blocks.append((pos,
                                       lambda kv, pos=pos: V(KP, 0, 64, kv * 128 + pos * 64, [(1, 64)]),
                                       lambda kv, pos=pos: V(VP, 0, 64, pos * 260 + kv * 65, [(1, 65)]),
                                       [T("KP"), T("VP")]))
                    blk = 2 + c
                    blocks.append((2,
                                   lambda kv, blk=blk, l=l: V(KT, 0, 64, l * 1536 + kv * 384 + blk * 64, [(1, 64)]),
                                   lambda kv, blk=blk, l=l: V(VBf, 0, 64, l * 1560 + blk * 260 + kv * 65, [(1, 65)]),
                                   [ktok[l][blk], vtok[l][blk]]))
                bo, bot = ps_quad(2)
                nb = len(blocks)
                for bi, (pos, kfn, vfn, btoks) in enumerate(blocks):
                    bsc, bsct = ps_pair()

                    def scmm(e, b=bsc, kfn=kfn, pos=pos):
                        for kv in range(4):
                            last = e.matmul(pf(b, 64, kv * 256, [(1, 256)]), lhsT=kfn(kv), rhs=V(QTc, 0, 64, kv * 256, [(1, 256)]), start=True, stop=True)
                        return last
                    P.op("pe", scmm, reads=[btoks[0], T("QTc")], writes=bsct)
                    pi = 0
                    P.op("act", lambda e, b=bsc, pi=pi: e.activation(out=PT[pi][0:64, :], in_=pf(b, 64, 0, [(1, 1024)]), func=AF.Exp), reads=bsct, writes=[T(f"PT{pi}")])
                    P.op("dve", lambda e, pi=pi, pos=pos: e.tensor_tensor(out=PT[pi][0:64, :], in0=PT[pi][0:64, :], in1=V(BIAS, 0, 64, pos * 1024, [(1, 1024)]), op=ALU.mult),
                         reads=[T(f"PT{pi}"), T("bias")], writes=[T(f"PT{pi}")])

                    def omm(e, b=bo, vfn=vfn, pi=pi, bi=bi, nb=nb):
                        for h in range(16):
                            last = e.matmul(pf(b, 64, h * 128, [(1, 65)]), lhsT=V(PT[pi], 0, 64, h * 64, [(1, 64)]), rhs=vfn(h // 4), start=(bi == 0 and h % 4 == 0), stop=(bi == nb - 1), skip_group_check=True)
                        return last
                    P.op("pe", omm, reads=[T(f"PT{pi}"), btoks[1]], writes=bot)
                tden = T("den")
                P.op("dve", lambda e, b=bo, l=l: e.tensor_tensor(out=DEN[0:64, :], in0=pf(b, 64, 64, [(128, 16)]), in1=V(ESINK, 0, 64, l * 16, [(1, 16)]), op=ALU.add), reads=bot + [t_lc], writes=[tden])
                P.op("dve", lambda e: e.reciprocal(out=RDEN[0:64, :], in_=DEN[0:64, :]), reads=[tden], writes=[tden])
                P.op("dve", lambda e, b=bo: e.tensor_tensor(out=V(ON, 0, 64, 0, [(64, 16), (1, 64)]), in0=pf(b, 64, 0, [(128, 16), (1, 64)]),
                                                             in1=V(RDEN, 0, 64, 0, [(1, 16), (0, 64)]), op=ALU.mult), reads=bot + [tden], writes=[T("QN")])
                bo2, bo2t = ps_single()

                def tro(e, b=bo2):
                    for j in range(8):
                        last = e.transpose(pb(b, 128, j * 64, [(1, 64)]), V(ON, 0, 64, j * 128, [(1, 128)]), V(identb, 0, 64, 0, [(1, 64)]))
                    return last
                P.op("pe", tro, reads=[T("QN"), t_const], writes=bo2t)
                P.op("act", lambda e, b=bo2, cs=cs: e.activation(out=V(OT, 0, 128, cs, [(NT, 8), (1, 64)]), in_=pb(b, 128, 0, [(64, 8), (1, 64)]), func=AF.Copy), reads=bo2t, writes=[T(f"OT{c}")])

            ckpt("att_done")
            if kind == "p" and not last_tile:
                P.op("pool", lambda e, l=l: e.tensor_copy(out=V(KT, 0, 64, l * 1536, [(384, 4), (1, 128)]), in_=V(KT, 0, 64, l * 1536 + 256, [(384, 4), (1, 128)])),
                     reads=ktok[l][4:6], writes=ktok[l][0:2])
                P.op("pool", lambda e, l=l: e.tensor_copy(out=V(VBf, 0, 64, l * 1560, [(1, 520)]), in_=V(VBf, 0, 64, l * 1560 + 4 * 260, [(1, 520)])),
                     reads=vtok[l][4:6], writes=vtok[l][0:2])

            tYT, tOT = TL("YT", NCH), TL("OT", NCH)
            for J in range(2):
                wgs, wgst = wload([(512, 8), (1, 512)], win_ap(C_GS + J * 512, 512), key=("gs", l, J))
                wga, wgat = wload([(512, 8), (1, 512)], win_ap(C_GA + J * 512, 512), key=("ga", l, J))
                wba, wbat = wload([(512, 8), (1, 512)], DA(w_br_attn, l * D * D + J * 512, [(D, 128), (128 * D, 8), (1, 512)]), key=("ba", l, J))
                for half in range(2):
                    wbs, wbst = wload_bs(l, J * 512 + half * 256)
                    for j2 in range(2):
                        jj = half * 2 + j2
                        j = J * 4 + jj
                        b, bt = fm_proj(wgs, wgst, 512, jj * 128, hT, thT, 8)
                        P.op("act", lambda e, b=b: e.activation(out=GSs[:, :], in_=pf(b, 128, 0, [(1, NT)]), func=AF.Sigmoid), reads=bt, writes=[T("GSs")])
                        b, bt = fm_proj(wga, wgat, 512, jj * 128, hT, thT, 8)
                        P.op("act", lambda e, b=b: e.activation(out=GAs[:, :], in_=pf(b, 128, 0, [(1, NT)]), func=AF.Sigmoid), reads=bt, writes=[T("GAs")])
                        b, bt = fm_proj(wbs, wbst, 256, j2 * 128, YT, tYT, 16)
                        P.op("dve", lambda e, b=b: e.tensor_tensor(out=MXs[:, :], in0=pf(b, 128, 0, [(1, NT)]), in1=GSs[:, :], op=ALU.mult), reads=bt + [T("GSs")], writes=[T("MXs")])
                        b, bt = fm_proj(wba, wbat, 512, jj * 128, OT, tOT, 8)
                        P.op("dve", lambda e, b=b: e.tensor_tensor(out=GAs[:, :], in0=pf(b, 128, 0, [(1, NT)]), in1=GAs[:, :], op=ALU.mult), reads=bt + [T("GAs")], writes=[T("GAs")])
                        P.op("dve", lambda e, j=j: e.tensor_tensor(out=V(MIXT, 0, 128, j * NT, [(1, NT)]), in0=MXs[:, :], in1=GAs[:, :], op=ALU.add),
                             reads=[T("MXs"), T("GAs")], writes=[T(f"ZT{j}")])
            tMIX = TL("ZT", 8)
            for J in range(2):
                wo, wot = wload([(512, 8), (1, 512)], DA(w_out, l * D * D + J * 512, [(D, 128), (128 * D, 8), (1, 512)]), key=("wo", l, J))
                for jj in range(4):
                    j = J * 4 + jj
                    b, bt = fm_proj(wo, wot, 512, jj * 128, MIXT, tMIX, 8, nreads=nseg)
                    for sg in range(nseg):
                        P.op("dve", lambda e, b=b, j=j, sg=sg, l=l: e.scalar_tensor_tensor(out=V(xT, 0, 128, j * NT + sg * L, [(1, L)]), in0=pf(b, 128, sg * L, [(1, L)]),
                                                                                        scalar=modv(l, 2, j, seq0 + sg), in1=V(xT, 0, 128, j * NT + sg * L, [(1, L)]), op0=ALU.mult, op1=ALU.add),
                             reads=bt + [T("modt")], writes=[txT[j]])

            ckpt("phase2")
            norm_to_hT(l, 1)
            wgu0 = l * D * 2 * DFF
            for G in range(6):
                ncg = 4 if G < 5 else 2
                wg, wgt = wload([(512, 8), (1, ncg * 128)], DA(w_gate_up, wgu0 + G * 512, [(2 * DFF, 128), (128 * 2 * DFF, 8), (1, ncg * 128)]), key=("wg", l, G))
                wu, wut = wload([(512, 8), (1, ncg * 128)], DA(w_gate_up, wgu0 + DFF + G * 512, [(2 * DFF, 128), (128 * 2 * DFF, 8), (1, ncg * 128)]), key=("wu", l, G))
                for jj in range(ncg):
                    j = G * 4 + jj
                    b, bt = fm_proj(wg, wgt, 512, jj * 128, hT, thT, 8)
                    si = j % 2
                    P.op("act", lambda e, b=b, si=si: e.activation(out=SGs[si][:, :], in_=pf(b, 128, 0, [(1, NT)]), func=AF.Silu), reads=bt, writes=[T(f"SGs{si}")])
                    b, bt = fm_proj(wu, wut, 512, jj * 128, hT, thT, 8)
                    P.op("dve", lambda e, b=b, si=si, j=j: e.tensor_tensor(out=V(ACTT, 0, 128, j * NT, [(1, NT)]), in0=pf(b, 128, 0, [(1, NT)]), in1=SGs[si][:, :], op=ALU.mult),
                         reads=bt + [T(f"SGs{si}")], writes=[acttok(j)])
            tACT = [acttok(j) for j in range(22)]
            for j in range(8):
                wd, wdt = wload_down(l, j)
                b, bt = fm_proj(wd, wdt, 128, 0, ACTT, tACT, 22, nreads=nseg)
                for sg in range(nseg):
                    P.op("dve", lambda e, b=b, j=j, sg=sg, l=l: e.scalar_tensor_tensor(out=V(xT, 0, 128, j * NT + sg * L, [(1, L)]), in0=pf(b, 128, sg * L, [(1, L)]),
                                                                                    scalar=modv(l, 5, j, seq0 + sg), in1=V(xT, 0, 128, j * NT + sg * L, [(1, L)]), op0=ALU.mult, op1=ALU.add),
                         reads=bt + [T("modt")], writes=[txT[j]])

        ckpt("ffn")
        for sub in range(2):
            b, bt = ps_pair()

            def try2(e, b=b, sub=sub):
                for k in range(8):
                    last = e.transpose(pf(b, 128, k * 128, [(1, 128)]), V(xT, 0, 128, k * NT + sub * 128, [(1, 128)]), identf[:, :])
                return last
            P.op("pe", try2, reads=txT + [t_const], writes=bt)
            P.op("act", lambda e, b=b: e.activation(out=V(SCR, 0, 128, 0, [(1, 1024)]), in_=pf(b, 128, 0, [(1, 1024)]), func=AF.Copy), reads=bt, writes=[t_scr])
            P.dma("sp", DA(ydst, (row0 + sub * 128) * D, [(D, 128), (1, D)]), V(SCR, 0, 128, 0, [(1, 1024)]), out_slot(), reads=[t_scr])

    try:
        ckpt("consts")
        if NSS:
            do_tile("s", 0, 0)
        for s in range(NPS):
            for kt in range(NPT):
                do_tile("p", s, kt)
    except StopBuild:
        pass
    import os as _os2
    P.emit(window=int(_os2.environ.get("K_WINDOW", "600")))
    return nc


_W_NAMES = ["rel_bias", "ada_w", "ada_b", "norm_mix_g", "norm_ffn_g", "w_in", "conv_w", "conv_b", "dt_bias", "a_log",
            "d_skip", "ssm_norm_g", "q_norm_g", "k_norm_g", "sinks", "w_br_ssm", "w_br_attn", "w_out", "w_gate_up", "w_down"]
_PROG_CACHE = {}


def run_cores(inputs, n_cores, NPS, PLEN, NSS):
    f = lambda a: np.ascontiguousarray(np.asarray(a, dtype=np.float32))
    key = (NPS, PLEN, NSS)
    nc = build_program(NPS, PLEN, NSS)
    oh = onehot_table()
    in_maps = []
    for i in range(n_cores):
        m = {n: f(inputs[n]) for n in _W_NAMES}
        m["onehot"] = oh
        m["xp"] = f(inputs["x_prompt"][i * NPS:(i + 1) * NPS]).reshape(NPS * PLEN, D)
        m["xs"] = f(inputs["x_sample"][i * NSS:(i + 1) * NSS]).reshape(NSS * 64, D)
        m["ck"] = f(inputs["cache_k"][:, i * NSS:(i + 1) * NSS]).reshape(2, NSS, 128, 256)
        m["cv"] = f(inputs["cache_v"][:, i * NSS:(i + 1) * NSS]).reshape(2, NSS, 128, 256)
        m["sconv"] = f(inputs["state_conv"][:, i * NSS:(i + 1) * NSS]).reshape(2, NSS * 3, 3072)
        m["sssm"] = f(inputs["state_ssm"][:, i * NSS:(i + 1) * NSS]).reshape(2, NSS, 2048, 128)
        m["cvec"] = np.concatenate([f(inputs["c_prompt"][i * NPS:(i + 1) * NPS]), f(inputs["c_sample"][i * NSS:(i + 1) * NSS])], axis=0)
        in_maps.append(m)
    res = run_bass_kernel_spmd(nc, in_maps, core_ids=list(range(n_cores)))
    R = res.results
    cat = lambda name, shp, ax: np.concatenate([np.asarray(r[name], dtype=np.float32).reshape(shp) for r in R], axis=ax)
    out = (
        cat("yp", (NPS, PLEN, D), 0), cat("ys", (NSS, 64, D), 0),
        cat("ncp", (2, NPS, 3, 3072), 1), cat("ncs", (2, NSS, 3, 3072), 1),
        cat("nsp", (2, NPS, 32, 64, 128), 1), cat("nss", (2, NSS, 32, 64, 128), 1),
        cat("nkp", (2, NPS, 128, 4, 64), 1), cat("nks", (2, NSS, 64, 4, 64), 1),
        cat("nvp", (2, NPS, 128, 4, 64), 1), cat("nvs", (2, NSS, 64, 4, 64), 1),
    )
    return out


def kernel(**inputs):
    return run_cores(inputs, 8, 2, 2048, 4)
```

```python
import math
import numpy as np
from contextlib import ExitStack
import concourse.bass as bass
import concourse.mybir as mybir
from concourse.bass_utils import run_bass_kernel_spmd

F32 = mybir.dt.float32
BF16 = mybir.dt.bfloat16
AF = mybir.ActivationFunctionType
ALU = mybir.AluOpType
AX = mybir.AxisListType

SAME_ENGINE_SYNC = True
SAME_HOP_US = 0.25
HOP_US = 0.7
EPS = 1e-6
D = 1024
KD = 8
NT = 256
NCH = 4
INC = 8736
C_Z, C_XBC, C_DT, C_Q, C_K, C_GS, C_GA = 0, 2048, 5120, 5152, 6176, 6688, 7712
DFF = 2816


class Tok:
    __slots__ = ("w", "r", "name", "excl")

    def __init__(self, name=""):
        self.w = None
        self.r = []
        self.name = name
        self.excl = False


class Slot:
    def __init__(self, sem, name=""):
        self.sem = sem
        self.name = name
        self.last = None
        self.queue = None


class Op:
    __slots__ = ("id", "stream", "fn", "dma", "slot", "deps", "dur", "issue", "ev", "succ", "nleft", "ready")


class _Rec:
    def __init__(self, stream):
        self.stream = stream
        self.t = 0.0

    def then_inc(self, *a, **k):
        return self

    def __getattr__(self, name):
        def f(*a, **k):
            def fsz(ap):
                n = 1
                for d in list(ap.shape)[1:]:
                    n *= d
                return n
            st = self.stream
            if name == "matmul":
                rhs = k.get("rhs", a[2] if len(a) > 2 else None)
                n = fsz(rhs)
                mult = 4.0 if rhs.dtype == F32 else 1.0
                self.t += max(n, 64) * mult / 2050.0 + 0.012
            elif name == "transpose":
                src = k.get("in_", a[1] if len(a) > 1 else None)
                mult = 4.0 if src.dtype == F32 else 1.0
                self.t += max(fsz(src), 64) * mult / 2050.0 + 0.03
            else:
                out = k.get("out", a[0] if a else None)
                n = fsz(out) if out is not None else 64
                if name == "reciprocal":
                    n *= 6
                if st == "dve":
                    self.t += n / 960.0 + 0.08
                elif st == "act":
                    self.t += n / 1400.0 + 0.22
                else:
                    self.t += n / 500.0 + 0.15
            return self
        return f


def _os_env_inorder():
    import os
    return os.environ.get("K_INORDER", "pe,sp").split(",")


class Prog:
    def __init__(self, nc):
        self.nc = nc
        self.es = ExitStack()
        self.names = ("pe", "act", "dve", "pool", "sp")
        self.sems = {nm: self.es.enter_context(nc.semaphore("sem_" + nm)) for nm in self.names}
        self.in_order_safe = set(_os_env_inorder())
        self.slots = []
        self.ops = []

    def sbuf(self, name, shape, dtype):
        return self.es.enter_context(self.nc.sbuf_tensor(name, list(shape), dtype))

    def psum(self, name, shape, dtype):
        return self.es.enter_context(self.nc.psum_tensor(name, list(shape), dtype))

    def slot(self, name):
        sem = self.es.enter_context(self.nc.semaphore("dq_" + name))
        s = Slot(sem, name)
        self.slots.append(s)
        return s

    def _mk(self, stream, reads, writes):
        o = Op()
        o.id = len(self.ops)
        o.stream = stream
        o.fn = None
        o.dma = None
        o.slot = None
        deps = set()
        for t in reads:
            if t.w is not None:
                deps.add(t.w)
            if t.excl:
                deps.update(t.r)
        for t in writes:
            if t.w is not None:
                deps.add(t.w)
            deps.update(t.r)
        o.deps = deps
        for t in reads:
            t.r.append(o.id)
        for t in writes:
            t.w = o.id
            t.r = []
        self.ops.append(o)
        return o

    def op(self, eng, fn, reads=(), writes=()):
        o = self._mk(eng, reads, writes)
        o.fn = fn
        rec = _Rec(eng)
        fn(rec)
        o.dur = rec.t
        o.issue = o.dur
        return o.id

    def dma(self, q, out, in_, slot, reads=(), writes=(), cont=False):
        o = self._mk(q, reads, writes)
        o.dma = (out, in_)
        o.slot = slot
        assert slot.queue in (None, q)
        slot.queue = q
        if slot.last is not None:
            if cont:
                o.deps.update(self.ops[slot.last].deps)
                o.deps.discard(o.id)
            else:
                o.deps.add(slot.last)
        slot.last = o.id
        n = 1
        for d in list(out.shape):
            n *= d
        nbytes = n * (4 if in_.dtype == F32 else 2)
        o.issue = 0.06 if q == "sp" else 1.0
        o.dur = nbytes / 350e3 + 0.2
        return o.id

    def schedule(self, window=600):
        ops = self.ops
        n = len(ops)
        for o in ops:
            o.succ = []
            o.nleft = 0
            o.ready = 0.0
        for o in ops:
            o.deps.discard(o.id)
            for d in o.deps:
                ops[d].succ.append(o.id)
            o.nleft = len(o.deps)
        eng_free = {nm: 0.0 for nm in self.names}
        dma_free = 0.0
        order = {nm: [] for nm in self.names}
        done = [False] * n
        ready = [o.id for o in ops if o.nleft == 0]
        head = 0
        fin = [0.0] * n
        nsched = 0
        while nsched < n:
            best = None
            bk = None
            lim = head + window
            for i in ready:
                if i >= lim:
                    continue
                o = ops[i]
                st = max(eng_free[o.stream], o.ready)
                k = (st, i)
                if bk is None or k < bk:
                    bk = k
                    best = i
            if best is None:
                best = min(ready)
                o = ops[best]
                bk = (max(eng_free[o.stream], o.ready), best)
            o = ops[best]
            ready.remove(best)
            st = bk[0]
            if o.dma is None:
                f = st + o.dur
                eng_free[o.stream] = f
            else:
                eng_free[o.stream] = st + o.issue
                f0 = max(st + o.issue, dma_free) + o.dur
                dma_free = f0
                f = f0 + 1.8
            fin[best] = f
            done[best] = True
            nsched += 1
            order[o.stream].append(best)
            for s_ in o.succ:
                so = ops[s_]
                so.nleft -= 1
                fl = f + (SAME_HOP_US if (so.stream == o.stream and o.stream not in ("pe", "sp")) else (0.0 if so.stream == o.stream else HOP_US))
                if so.ready < fl:
                    so.ready = fl
                if so.nleft == 0:
                    ready.append(s_)
            while head < n and done[head]:
                head += 1
        self.order = order
        self.est_total = max(fin) if fin else 0.0

    def emit(self, window=600):
        self.schedule(window)
        ops = self.ops
        nc = self.nc
        for nm in self.names:
            cnt = 0
            for i in self.order[nm]:
                o = ops[i]
                if o.dma is None:
                    cnt += 1
                    o.ev = (self.sems[nm], cnt)
            setattr(self, "cnt_" + nm, cnt)
        slot_cnt = {}
        for nm in self.names:
            for i in self.order[nm]:
                o = ops[i]
                if o.dma is not None:
                    c = slot_cnt.get(id(o.slot), 0) + 1
                    slot_cnt[id(o.slot)] = c
                    o.ev = (o.slot.sem, 16 * c)
        hmap = {"pe": "tensor", "act": "scalar", "dve": "vector", "pool": "gpsimd", "sp": "sync"}
        streams = {}
        for nm in self.names:
            ins = []
            waited = {}
            for i in self.order[nm]:
                o = ops[i]
                need = {}
                for d in o.deps:
                    sem, v = ops[d].ev
                    if sem is self.sems[nm] and (nm in self.in_order_safe or not SAME_ENGINE_SYNC) and ops[d].dma is None:
                        continue
                    if need.get(id(sem), (None, 0))[1] < v:
                        need[id(sem)] = (sem, v)
                for sem, v in need.values():
                    if waited.get(id(sem), 0) >= v:
                        continue
                    waited[id(sem)] = v
                    ins.append(("wait", sem, v))
                if o.dma is None:
                    ins.append(("op", o.fn))
                else:
                    ins.append(("dma", o.dma[0], o.dma[1], o.slot.sem))
            if nm == "sp":
                for sl in self.slots:
                    c = slot_cnt.get(id(sl), 0)
                    if c and waited.get(id(sl.sem), 0) < 16 * c:
                        ins.append(("wait", sl.sem, 16 * c))
                for o2 in self.names:
                    c = getattr(self, "cnt_" + o2)
                    if o2 != "sp" and c:
                        ins.append(("wait", self.sems[o2], c))
            streams[nm] = ins
        with nc.Block() as block:
            for nm in self.names:
                ins = streams[nm]
                if not ins:
                    continue

                def body(e, ins=ins, sem=self.sems[nm]):
                    for it in ins:
                        k = it[0]
                        if k == "wait":
                            e.wait_ge(it[1], it[2])
                        elif k == "op":
                            last = it[1](e)
                            last.then_inc(sem, 1)
                        else:
                            e.dma_start(out=it[1], in_=it[2]).then_inc(it[3], 16)

                getattr(block, hmap[nm])(body)
        self.es.close()


def V(t, p0, pn, off, dims):
    Fsz = 1
    for s in t.shape[1:]:
        Fsz *= s
    return bass.AP(t, p0 * Fsz + off, [[Fsz, pn]] + [[s, n] for s, n in dims])


def DA(h, off, dims):
    return bass.AP(h, off, [[s, n] for s, n in dims])


def t5_bucket_np(rel):
    n = -rel
    half, max_exact = 16, 8
    ret = np.where(n < 0, half, 0)
    n = np.abs(n)
    nf = np.maximum(n, 1).astype(np.float32)
    large = max_exact + (np.log(nf / np.float32(max_exact)) / np.float32(math.log(128 / max_exact))
                         * np.float32(half - max_exact)).astype(np.int32)
    large = np.minimum(large, half - 1)
    return ret + np.where(n < max_exact, n, large)


def onehot_table():
    q = np.arange(64)[:, None]
    oh = np.zeros((32, 3, 64, 64), np.float32)
    for blk in range(3):
        s = np.arange(64)[None, :] + 64 * blk
        bk = t5_bucket_np(s - 128 - q)
        for b in range(32):
            oh[b, blk] = (bk == b)
    return oh.reshape(32, 3 * 64 * 64)


def build_program(NPS, PLEN, NSS, DEPTH=2):
    NSEQ = NPS + NSS
    NPT = PLEN // NT
    assert NSS * 64 == NT or NSS == 0
    nc = bass.Bass("TRN2", target_bir_lowering=False)

    def din(name, shape):
        return nc.dram_tensor(name, list(shape), F32, kind="ExternalInput")

    def dout(name, shape):
        return nc.dram_tensor(name, list(shape), F32, kind="ExternalOutput")

    xp = din("xp", [NPS * PLEN, D]); xs = din("xs", [max(NSS, 1) * 64, D])
    ck = din("ck", [DEPTH, max(NSS, 1), 128, 256]); cv = din("cv", [DEPTH, max(NSS, 1), 128, 256])
    sconv = din("sconv", [DEPTH, max(NSS, 1) * 3, 3072]); sssm = din("sssm", [DEPTH, max(NSS, 1), 2048, 128])
    cvec = din("cvec", [NSEQ, D])
    rel_bias = din("rel_bias", [32, 16]); onehot = din("onehot", [32, 3 * 4096])
    ada_w = din("ada_w", [DEPTH, D, 6 * D]); ada_b = din("ada_b", [DEPTH, 6 * D])
    norm_mix_g = din("norm_mix_g", [DEPTH, D]); norm_ffn_g = din("norm_ffn_g", [DEPTH, D])
    w_in = din("w_in", [DEPTH, D, INC]); conv_w = din("conv_w", [DEPTH, 4, 3072]); conv_b = din("conv_b", [DEPTH, 3072])
    dt_bias = din("dt_bias", [DEPTH, 32]); a_log = din("a_log", [DEPTH, 32]); d_skip = din("d_skip", [DEPTH, 32])
    ssm_norm_g = din("ssm_norm_g", [DEPTH, 2048]); q_norm_g = din("q_norm_g", [DEPTH, 64]); k_norm_g = din("k_norm_g", [DEPTH, 64])
    sinks = din("sinks", [DEPTH, 16])
    w_br_ssm = din("w_br_ssm", [DEPTH, 2048, D]); w_br_attn = din("w_br_attn", [DEPTH, D, D]); w_out = din("w_out", [DEPTH, D, D])
    w_gate_up = din("w_gate_up", [DEPTH, D, 2 * DFF]); w_down = din("w_down", [DEPTH, DFF, D])
    yp = dout("yp", [NPS * PLEN, D]); ys = dout("ys", [max(NSS, 1) * 64, D])
    ncp = dout("ncp", [DEPTH, NPS, 3, 3072]); ncs = dout("ncs", [DEPTH, max(NSS, 1) * 3, 3072])
    nsp = dout("nsp", [DEPTH, NPS, 2048, 128]); nss = dout("nss", [DEPTH, max(NSS, 1), 2048, 128])
    nkp = dout("nkp", [DEPTH, NPS, 128, 256]); nks = dout("nks", [DEPTH, max(NSS, 1), 64, 256])
    nvp = dout("nvp", [DEPTH, NPS, 128, 256]); nvs = dout("nvs", [DEPTH, max(NSS, 1), 64, 256])

    P = Prog(nc)
    toks = {}

    def T(name):
        if name not in toks:
            toks[name] = Tok(name)
        return toks[name]

    def TL(name, n):
        return [T(f"{name}{i}") for i in range(n)]

    PSF = P.psum("psf", [128, 4096], F32)
    PSB = PSF.bitcast(BF16)
    pstok = TL("psb", 8)
    for t_ in pstok:
        t_.excl = True
    ps_live = {}
    ps_rr = {1: 0, 2: 0, 4: 0}

    def ps_alloc(n, nreads):
        slots_ = list(range(0, 8, n))
        k = len(slots_)
        for t_ in range(k):
            b = slots_[(ps_rr[n] + t_) % k]
            if all((b + i) not in ps_live for i in range(n)):
                ps_rr[n] = (ps_rr[n] + t_ + 1) % k
                a = {"banks": list(range(b, b + n)), "rem": nreads}
                for i in range(n):
                    ps_live[b + i] = a
                return b, pstok[b:b + n]
        raise RuntimeError(f"PSUM exhausted (need {n}); live={sorted(ps_live)}")

    def ps_single(nreads=1):
        return ps_alloc(1, nreads)

    def ps_pair(nreads=1):
        return ps_alloc(2, nreads)

    def ps_quad(nreads=1):
        return ps_alloc(4, nreads)

    _orig_op = P.op
    pstok_ids = {id(t): i for i, t in enumerate(pstok)}

    def _op(eng, fn, reads=(), writes=()):
        ev = _orig_op(eng, fn, reads=reads, writes=writes)
        seen = []
        for t in reads:
            bi = pstok_ids.get(id(t))
            if bi is None or bi not in ps_live:
                continue
            a = ps_live[bi]
            if any(a is x for x in seen):
                continue
            seen.append(a)
            a["rem"] -= 1
            if a["rem"] == 0:
                for bb in a["banks"]:
                    del ps_live[bb]
        return ev
    P.op = _op

    def pf(b, pn, off, dims, p0=0):
        return V(PSF, p0, pn, b * 512 + off, dims)

    def pb(b, pn, off, dims, p0=0):
        return V(PSB, p0, pn, b * 1024 + off, dims)

    def sb(name, f, dt=F32):
        return P.sbuf(name, [128, f], dt)

    identf = sb("identf", 128); identb = sb("identb", 128, BF16)
    onesb = sb("onesb", 128, BF16); onesf = sb("onesf", 128)
    Umat = sb("Umat", 64); SUmat = sb("SUmat", 64); Umb = sb("Umb", 64, BF16)
    CONST = sb("CONST", 4)
    xT = sb("xT", KD * NT)
    hT = sb("hT", KD * NT, BF16)
    RSTD = sb("RSTD", NT)
    ZT = sb("ZT", 16 * NT, BF16); QTr = sb("QTr", 8 * NT, BF16)
    PH1 = sb("PH1", 24 * NT, BF16)
    XC = PH1; BT_OFF = 16 * NT; CT_OFF = 20 * NT
    MIXT = ZT
    ACTT = PH1
    CS = [sb(f"CS{i}", 4 * 67) for i in range(2)]
    CACC = [sb(f"CACC{i}", NT) for i in range(2)]
    CIN = sb("CIN", DEPTH * 24 * 12)
    KT = sb("KT", DEPTH * 4 * 6 * 64, BF16)
    VBf = sb("VB", DEPTH * 6 * 4 * 65, BF16)
    KP = sb("KP", 4 * 128, BF16)
    VP = sb("VP", 2 * 4 * 65, BF16)
    DTX = sb("DTX", 128); DTA = sb("DTA", 128); DTE = sb("DTE", 128); DTv = sb("DTv", 128); DAv = sb("DAv", 128)
    R1 = sb("R1", 2048)
    NTMP = R1
    XDTS = R1.bitcast(BF16)
    DEC = sb("DEC", 2048, BF16)
    SQ = DEC
    CBM = sb("CBM", 256, BF16)
    XTs = sb("XTs", 2048, BF16); XDT = sb("XDT", 2048, BF16)
    YN = XDT
    BTOK = sb("BTOK", 512, BF16)
    Tb = sb("Tb", 2048)
    SQQ = sb("SQQ", 1024, BF16)
    EXs = [sb(f"EX{i}", 96) for i in range(2)]
    SS4 = sb("SS4", 4); RS4 = sb("RS4", 4)
    ST = sb("ST", DEPTH * 2048); STB = sb("STB", 2048, BF16)
    YT = sb("YT", 16 * NT, BF16); OT = sb("OT", 8 * NT, BF16)
    KO = [sb(f"KO{i}", 256) for i in range(1)]; VO = [sb(f"VO{i}", 256) for i in range(1)]
    KNB = sb("KNB", 256, BF16); SQK = sb("SQK", 256); SSK = sb("SSK", 4); RK = sb("RK", 4)
    SSQ = sb("SSQ", 16); RQ = sb("RQ", 16)
    QN = sb("QN", 1024, BF16); QTc = sb("QTc", 1024, BF16)
    ON = QN
    PT = [sb(f"PT{i}", 1024, BF16) for i in range(1)]
    DEN = sb("DEN", 16); RDEN = sb("RDEN", 16)
    BIAS = sb("BIAS", 3 * 1024, BF16)
    GSs = sb("GSs", NT); GAs = sb("GAs", NT); MXs = sb("MXs", NT)
    SGs = [sb(f"SGs{i}", NT) for i in range(2)]
    NWB = 4
    WB = [sb(f"WB{i}", 4096, BF16) for i in range(NWB)]
    WBS = sb("WBS", 4096, BF16)
    WDT = sb("WDT", DEPTH * 8 * 32, BF16)
    SCR = sb("SCR", 2048)
    ROWS = SCR; CROW = SCR
    MODT = sb("MODT", DEPTH * 48 * NSEQ)
    GMUL = sb("GMUL", DEPTH * 2 * 8 * NSEQ)
    ADAB = sb("ADAB", DEPTH * 48)
    GN = sb("GN", DEPTH * 8 * 2)
    CWB = sb("CWB", DEPTH * 24 * 5)
    GSSM = sb("GSSM", DEPTH * 16)
    DTB = sb("DTB", DEPTH * 32); ABC = sb("ABC", DEPTH * 32); DSK = sb("DSK", DEPTH * 32)
    QG8 = sb("QG8", DEPTH * 64); KG = sb("KG", DEPTH * 64); ESINK = sb("ESINK", DEPTH * 16)
    DI = sb("DI", 32 * 64, BF16)
    RELB = sb("RELB", 16, BF16); RELF = sb("RELF", 16)
    SCT = sb("SCT", 8 * NSEQ, BF16)

    t_const = T("const")
    wtok = TL("wb", NWB)
    wslot = [P.slot(f"wb{i}") for i in range(NWB)]
    wctr = [0]
    s_in = P.slot("in"); s_in2 = P.slot("in2"); s_pool = P.slot("poolmisc")
    s_out = [P.slot(f"out{i}") for i in range(4)]
    octr = [0]

    def out_slot():
        s = s_out[octr[0] % 4]; octr[0] += 1
        return s

    NGRP = 96
    wscr = nc.dram_tensor("wscr", [NGRP * 128, 4096], BF16, kind="Internal")
    wkeys = {}
    wslot_sp = [P.slot(f"wbsp{i}") for i in range(NWB)]
    wsave_slot = [P.slot(f"wsave{i}") for i in range(2)]
    wsave_ctr = [0]

    def _wfetch(key, buf, tok, slot_pool, slot_sp, parts):
        if key is None or key not in wkeys:
            for pi_, (doff, ddims, sap) in enumerate(parts):
                P.dma("pool", V(buf, 0, 128, doff, ddims), sap, slot_pool, writes=[tok], cont=(pi_ > 0))
            if key is not None:
                gi = len(wkeys)
                assert gi < NGRP
                wkeys[key] = gi
                ss = wsave_slot[wsave_ctr[0] % 2]; wsave_ctr[0] += 1
                P.dma("sp", DA(wscr, gi * 128 * 4096, [(4096, 128), (1, 4096)]), V(buf, 0, 128, 0, [(1, 4096)]), ss, reads=[tok])
        else:
            gi = wkeys[key]
            P.dma("sp", V(buf, 0, 128, 0, [(1, 4096)]), DA(wscr, gi * 128 * 4096, [(4096, 128), (1, 4096)]), slot_sp, writes=[tok])

    def wload(dst_dims, src_ap, key=None):
        i = wctr[0] % NWB; wctr[0] += 1
        _wfetch(key, WB[i], wtok[i], wslot[i], wslot_sp[i], [(0, dst_dims, src_ap)])
        return WB[i], wtok[i]

    wbs_slot = P.slot("wbs"); wbs_slot_sp = P.slot("wbssp")

    def wload_bs(l, col0):
        parts = [(part * 8 * 256, [(256, 8), (1, 256)], DA(w_br_ssm, l * 2048 * D + part * 8 * 128 * D + col0, [(D, 128), (128 * D, 8), (1, 256)])) for part in range(2)]
        _wfetch(("bs", l, col0), WBS, T("WBS"), wbs_slot, wbs_slot_sp, parts)
        return WBS, T("WBS")

    def wload_down(l, j):
        i = wctr[0] % NWB; wctr[0] += 1
        parts = []
        k0 = 0
        for nk in (8, 8, 6):
            parts.append((k0 * 128, [(128, nk), (1, 128)], DA(w_down, l * DFF * D + k0 * 128 * D + j * 128, [(D, 128), (128 * D, nk), (1, 128)])))
            k0 += nk
        _wfetch(("down", l, j), WB[i], wtok[i], wslot[i], wslot_sp[i], parts)
        return WB[i], wtok[i]

    cw_ = [t_const, T("VBall"), T("VPall"), T("CIN0"), T("CIN1")]
    for fn in [
        lambda e: e.memset(identf[:], 0.0),
        lambda e: e.affine_select(out=identf[:], in_=identf[:], pattern=[[-1, 128]], compare_op=ALU.not_equal, fill=1.0, base=0, channel_multiplier=1),
        lambda e: e.memset(onesf[:], 1.0),
        lambda e: e.memset(Umat[0:64, :], 1.0),
        lambda e: e.affine_select(out=Umat[0:64, :], in_=Umat[0:64, :], pattern=[[1, 64]], compare_op=ALU.is_ge, fill=0.0, base=0, channel_multiplier=-1),
        lambda e: e.memset(SUmat[0:64, :], 1.0),
        lambda e: e.affine_select(out=SUmat[0:64, :], in_=SUmat[0:64, :], pattern=[[-1, 64]], compare_op=ALU.is_gt, fill=0.0, base=0, channel_multiplier=1),
        lambda e: e.memset(CONST[:, 0:1], EPS),
        lambda e: e.memset(CONST[:, 1:2], 1.0),
        lambda e: e.memset(CONST[:, 2:3], 0.0),
        lambda e: e.memset(VBf[:], 1.0),
        lambda e: e.memset(VP[:], 1.0),
        lambda e: e.memset(CIN[:], 0.0),
        lambda e: e.tensor_copy(out=onesb[:], in_=onesf[:]),
        lambda e: e.tensor_copy(out=Umb[0:64, :], in_=Umat[0:64, :]),
        lambda e: e.tensor_copy(out=identb[:], in_=identf[:]),
    ]:
        P.op("pool", fn, writes=cw_)
    ktok = [TL(f"KT{l}_", 6) for l in range(DEPTH)]
    vtok = [TL(f"VB{l}_", 6) for l in range(DEPTH)]
    for l in range(DEPTH):
        for t in vtok[l]:
            t.w = T("VBall").w
    T("VP").w = T("VPall").w

    t_rows = T("SCR")

    def rows_to_fm(src_ap_rows, R, C, dst_fn, extra_reads=()):
        assert R == 1
        h, off0 = src_ap_rows
        for c0 in range(0, C, 2048):
            cw = min(2048, C - c0)
            P.dma("sp", V(ROWS, 0, 1, 0, [(1, cw)]), DA(h, off0 + c0, [(C, 1), (1, cw)]), s_in, writes=[t_rows])
            nj = cw // 128
            b, bt = ps_single()

            def tr(e, nj=nj, b=b):
                for jj in range(nj):
                    last = e.transpose(pf(b, 128, jj, [(1, 1)]), V(ROWS, 0, 1, jj * 128, [(1, 128)]), V(identf, 0, 1, 0, [(1, 1)]))
                return last
            P.op("pe", tr, reads=[t_rows, t_const], writes=bt)
            dst, dtoks = dst_fn(c0 // 128, nj)
            P.op("dve", lambda e, b=b, nj=nj, dst=dst: e.tensor_copy(out=dst, in_=pf(b, 128, 0, [(1, nj), (1, 1)])), reads=bt, writes=dtoks)

    def bc_load(dst_ap, h, off, n, tok):
        P.dma("sp", dst_ap, DA(h, off, [(0, 64), (1, n)]), s_in2, writes=[tok])

    t_lc = T("layerconst")
    for l in range(DEPTH):
        for j0 in range(0, 24, 12):
            P.dma("sp", V(ROWS, 0, 4, 0, [(1, 1536)]), DA(conv_w, l * 4 * 3072 + j0 * 128, [(3072, 4), (1, 1536)]), s_in, writes=[t_rows])
            P.dma("sp", V(ROWS, 4, 1, 0, [(1, 1536)]), DA(conv_b, l * 3072 + j0 * 128, [(3072, 1), (1, 1536)]), s_in, writes=[t_rows])
            b, bt = ps_single()

            def tr(e, j0=j0, b=b):
                for jj in range(12):
                    last = e.transpose(pf(b, 128, jj * 5, [(1, 5)]), V(ROWS, 0, 5, jj * 128, [(1, 128)]), V(identf, 0, 5, 0, [(1, 5)]))
                return last
            P.op("pe", tr, reads=[t_rows, t_const], writes=bt)
            P.op("dve", lambda e, b=b, j0=j0, l=l: e.tensor_copy(out=V(CWB, 0, 128, l * 120 + j0 * 5, [(1, 60)]), in_=pf(b, 128, 0, [(1, 60)])), reads=bt, writes=[t_lc])
        rows_to_fm((norm_mix_g, l * D), 1, D, lambda j0, nj, l=l: (V(GN, 0, 128, l * 16 + j0 * 2, [(2, nj), (1, 1)]), [t_lc]))
        rows_to_fm((norm_ffn_g, l * D), 1, D, lambda j0, nj, l=l: (V(GN, 0, 128, l * 16 + j0 * 2 + 1, [(2, nj), (1, 1)]), [t_lc]))
        rows_to_fm((ssm_norm_g, l * 2048), 1, 2048, lambda j0, nj, l=l: (V(GSSM, 0, 128, l * 16 + j0, [(1, nj), (1, 1)]), [t_lc]))
        rows_to_fm((ada_b, l * 6144), 1, 6144, lambda j0, nj, l=l: (V(ADAB, 0, 128, l * 48 + j0, [(1, nj), (1, 1)]), [t_lc]))
        bc_load(V(DTB, 0, 64, l * 32, [(1, 32)]), dt_bias, l * 32, 32, t_lc)
        bc_load(V(ABC, 0, 64, l * 32, [(1, 32)]), a_log, l * 32, 32, t_lc)
        bc_load(V(DSK, 0, 64, l * 32, [(1, 32)]), d_skip, l * 32, 32, t_lc)
        bc_load(V(QG8, 0, 64, l * 64, [(1, 64)]), q_norm_g, l * 64, 64, t_lc)
        bc_load(V(KG, 0, 64, l * 64, [(1, 64)]), k_norm_g, l * 64, 64, t_lc)
        bc_load(V(ESINK, 0, 64, l * 16, [(1, 16)]), sinks, l * 16, 16, t_lc)
        P.dma("pool", V(WDT, 0, 128, l * 256, [(32, 8), (1, 32)]), DA(w_in, l * D * INC + C_DT, [(INC, 128), (128 * INC, 8), (1, 32)]), s_pool, writes=[t_lc])
    P.op("act", lambda e: e.activation(out=V(ABC, 0, 64, 0, [(1, DEPTH * 32)]), in_=V(ABC, 0, 64, 0, [(1, DEPTH * 32)]), func=AF.Exp), reads=[t_lc], writes=[t_lc])
    P.op("dve", lambda e: e.tensor_scalar(out=V(ABC, 0, 64, 0, [(1, DEPTH * 32)]), in0=V(ABC, 0, 64, 0, [(1, DEPTH * 32)]), scalar1=-1.0, scalar2=None, op0=ALU.mult), reads=[t_lc], writes=[t_lc])
    P.op("act", lambda e: e.activation(out=V(ESINK, 0, 64, 0, [(1, DEPTH * 16)]), in_=V(ESINK, 0, 64, 0, [(1, DEPTH * 16)]), func=AF.Exp), reads=[t_lc], writes=[t_lc])
    P.op("dve", lambda e: e.tensor_scalar(out=V(QG8, 0, 64, 0, [(1, DEPTH * 64)]), in0=V(QG8, 0, 64, 0, [(1, DEPTH * 64)]), scalar1=0.125, scalar2=None, op0=ALU.mult), reads=[t_lc], writes=[t_lc])
    P.dma("sp", V(RELF, 0, 32, 0, [(1, 16)]), DA(rel_bias, 0, [(16, 32), (1, 16)]), s_in2, writes=[T("relf")])
    P.op("dve", lambda e: e.tensor_copy(out=V(RELB, 0, 32, 0, [(1, 16)]), in_=V(RELF, 0, 32, 0, [(1, 16)])), reads=[T("relf")], writes=[T("relb")])
    for blk in range(3):
        i = wctr[0] % NWB; wctr[0] += 1
        P.dma("pool", V(WB[i], 0, 32, 0, [(1, 4096)]), DA(onehot, blk * 4096, [(3 * 4096, 32), (1, 4096)]), wslot[i], writes=[wtok[i]])
        b, bt = ps_pair()

        def ohmm(e, i=i, b=b):
            for q in range(64):
                last = e.matmul(pf(b, 64, q * 16, [(1, 16)]), lhsT=V(WB[i], 0, 32, q * 64, [(1, 64)]), rhs=V(RELB, 0, 32, 0, [(1, 16)]), start=True, stop=True)
            return last
        P.op("pe", ohmm, reads=[wtok[i], T("relb")], writes=bt)
        P.op("dve", lambda e, b=b, blk=blk: e.tensor_copy(out=V(BIAS, 0, 64, blk * 1024, [(64, 16), (1, 64)]), in_=pf(b, 64, 0, [(1, 16), (16, 64)])), reads=bt, writes=[T("bias")])
    t_scr = T("SCR")
    P.dma("sp", V(CROW, 0, NSEQ, 0, [(1, 1024)]), DA(cvec, 0, [(D, NSEQ), (1, D)]), s_in, writes=[T("SCR")])
    P.op("act", lambda e: e.activation(out=V(CROW, 0, NSEQ, 0, [(1, 1024)]), in_=V(CROW, 0, NSEQ, 0, [(1, 1024)]), func=AF.Silu), reads=[T("SCR")], writes=[T("SCR")])
    b, bt = ps_single()

    def trc(e, b=b):
        for k in range(8):
            last = e.transpose(pf(b, 128, k * NSEQ, [(1, NSEQ)]), V(CROW, 0, NSEQ, k * 128, [(1, 128)]), V(identf, 0, NSEQ, 0, [(1, NSEQ)]))
        return last
    P.op("pe", trc, reads=[T("SCR"), t_const], writes=bt)
    P.op("dve", lambda e, b=b: e.tensor_copy(out=V(SCT, 0, 128, 0, [(1, 8 * NSEQ)]), in_=pf(b, 128, 0, [(1, 8 * NSEQ)])), reads=bt, writes=[T("sct")])
    for l in range(DEPTH):
        b, bt = ps_single()
        for g in range(12):
            wbuf, wt = wload([(512, 8), (1, 512)], DA(ada_w, l * D * 6144 + g * 512, [(6144, 128), (128 * 6144, 8), (1, 512)]))

            def adamm(e, wbuf=wbuf, g=g, b=b):
                for jj in range(4):
                    j = g * 4 + jj
                    for k in range(8):
                        last = e.matmul(pf(b, 128, j * NSEQ, [(1, NSEQ)]), lhsT=V(wbuf, 0, 128, k * 512 + jj * 128, [(1, 128)]),
                                        rhs=V(SCT, 0, 128, k * NSEQ, [(1, NSEQ)]), start=(k == 0), stop=(k == 7))
                return last
            P.op("pe", adamm, reads=[wt, T("sct")], writes=bt)
        P.op("dve", lambda e, b=b, l=l: e.tensor_tensor(out=V(MODT, 0, 128, l * 48 * NSEQ, [(NSEQ, 48), (1, NSEQ)]), in0=pf(b, 128, 0, [(NSEQ, 48), (1, NSEQ)]),
                                                         in1=V(ADAB, 0, 128, l * 48, [(1, 48), (0, NSEQ)]), op=ALU.add), reads=bt + [t_lc], writes=[T("modt")])
        for mf in range(2):
            sc0 = 8 if mf == 0 else 32
            P.op("dve", lambda e, l=l, mf=mf, sc0=sc0: e.scalar_tensor_tensor(
                out=V(GMUL, 0, 128, (l * 2 + mf) * 8 * NSEQ, [(NSEQ, 8), (1, NSEQ)]),
                in0=V(MODT, 0, 128, (l * 48 + sc0) * NSEQ, [(NSEQ, 8), (1, NSEQ)]), scalar=1.0,
                in1=V(GN, 0, 128, l * 16 + mf, [(2, 8), (0, NSEQ)]), op0=ALU.add, op1=ALU.mult), reads=[T("modt"), t_lc], writes=[T("gmul")])

    def modv(l, which, k, seq):
        return V(MODT, 0, 128, (l * 48 + which * 8 + k) * NSEQ + seq, [(1, 1)])

    cpar = [0]
    exctr = [0]
    uctr = [0]
    import os as _os
    _kstop = int(_os.environ.get("K_STOP", "-1"))
    _kctr = [0]

    class StopBuild(Exception):
        pass

    def ckpt(name):
        if _kstop >= 0:
            print("CKPT", _kctr[0], name, flush=True)
            if _kctr[0] == _kstop:
                raise StopBuild()
        _kctr[0] += 1

    def acttok(j):
        return T(f"XC{j}") if j < 16 else (T(f"BT{j - 16}") if j < 20 else T(f"CT{j - 20}"))

    def do_tile(kind, seq, kt):
        if kind == "p":
            nseg, L, seq0 = 1, NT, seq
            xsrc, ydst, row0 = xp, yp, seq * PLEN + kt * NT
            last_tile = (kt == NPT - 1)
        else:
            nseg, L, seq0 = NSS, 64, NPS
            xsrc, ydst, row0 = xs, ys, 0
            last_tile = True
        txT = TL("xT", 8)
        for sub in range(2):
            P.dma("sp", V(SCR, 0, 128, 0, [(1, 1024)]), DA(xsrc, (row0 + sub * 128) * D, [(D, 128), (1, D)]), s_in, writes=[t_scr])
            b, bt = ps_pair()

            def trx(e, b=b):
                for k in range(8):
                    last = e.transpose(pf(b, 128, k * 128, [(1, 128)]), V(SCR, 0, 128, k * 128, [(1, 128)]), identf[:, :])
                return last
            P.op("pe", trx, reads=[t_scr, t_const], writes=bt)
            P.op("dve", lambda e, b=b, sub=sub: e.tensor_copy(out=V(xT, 0, 128, sub * 128, [(NT, 8), (1, 128)]), in_=pf(b, 128, 0, [(128, 8), (1, 128)])), reads=bt, writes=txT)

        ckpt("xload")

        def norm_to_hT(l, mf):
            P.op("act", lambda e: e.activation(out=SQ[:, :], in_=xT[:, :], func=AF.Square), reads=txT, writes=[T("DEC_0"), T("DEC_1")])
            b, bt = ps_single()

            def ssmm(e, b=b):
                for k in range(8):
                    last = e.matmul(pf(b, 128, 0, [(1, NT)]), lhsT=onesb[:, :], rhs=V(SQ, 0, 128, k * NT, [(1, NT)]), start=(k == 0), stop=(k == 7))
                return last
            P.op("pe", ssmm, reads=[T("DEC_0"), T("DEC_1"), t_const], writes=bt)
            P.op("act", lambda e, b=b: e.activation(out=RSTD[:, :], in_=pf(b, 128, 0, [(1, NT)]), func=AF.Sqrt, bias=CONST[:, 0:1], scale=1.0 / D), reads=bt + [t_const], writes=[T("RSTD")])
            P.op("dve", lambda e: e.reciprocal(out=RSTD[:, :], in_=RSTD[:, :]), reads=[T("RSTD")], writes=[T("RSTD")])
            for k in range(8):
                tr1 = T(f"R1_{k // 4}")
                for sg in range(nseg):
                    o_ = k * NT + sg * L
                    P.op("dve", lambda e, k=k, sg=sg, o_=o_: e.scalar_tensor_tensor(
                        out=V(NTMP, 0, 128, o_, [(1, L)]), in0=V(xT, 0, 128, o_, [(1, L)]),
                        scalar=V(GMUL, 0, 128, ((l * 2 + mf) * 8 + k) * NSEQ + seq0 + sg, [(1, 1)]),
                        in1=V(RSTD, 0, 128, sg * L, [(1, L)]), op0=ALU.mult, op1=ALU.mult),
                        reads=[txT[k], T("RSTD"), T("gmul")], writes=[tr1])
                    P.op("act", lambda e, k=k, sg=sg, o_=o_: e.activation(
                        out=V(hT, 0, 128, o_, [(1, L)]), in_=V(NTMP, 0, 128, o_, [(1, L)]), func=AF.Identity,
                        bias=modv(l, 0 if mf == 0 else 3, k, seq0 + sg), scale=1.0),
                        reads=[tr1, T("modt")], writes=[T(f"hT{k}")])

        def fm_proj(wbuf, wt, kc, col0, rhs_t, rhs_tok, nk, nreads=1):
            b, bt = ps_single(nreads)

            def mm(e, b=b):
                for k in range(nk):
                    last = e.matmul(pf(b, 128, 0, [(1, NT)]), lhsT=V(wbuf, 0, 128, k * kc + col0, [(1, 128)]), rhs=V(rhs_t, 0, 128, k * NT, [(1, NT)]),
                                    start=(k == 0), stop=(k == nk - 1))
                return last
            P.op("pe", mm, reads=[wt] + rhs_tok, writes=bt)
            return b, bt

        for l in range(DEPTH):
            win0 = l * D * INC

            def win_ap(c0, w):
                return DA(w_in, win0 + c0, [(INC, 128), (128 * INC, 8), (1, w)])

            if kind == "p" and kt == 0:
                P.op("pool", lambda e, l=l: e.memset(V(CIN, 0, 128, l * 288, [(12, 24), (1, 3)]), 0.0), writes=[T(f"CIN{l}")])
            if kind == "s":
                R = 3 * NSS
                for j0 in range(0, 24, 12):
                    P.dma("sp", V(SCR, 0, R, 0, [(1, 1536)]), DA(sconv, l * R * 3072 + j0 * 128, [(3072, R), (1, 1536)]), s_in, writes=[t_scr])
                    b, bt = ps_single()

                    def trs(e, j0=j0, b=b, R=R):
                        for jj in range(12):
                            last = e.transpose(pf(b, 128, jj * R, [(1, R)]), V(SCR, 0, R, jj * 128, [(1, 128)]), V(identf, 0, R, 0, [(1, R)]))
                        return last
                    P.op("pe", trs, reads=[t_scr, t_const], writes=bt)
                    P.op("dve", lambda e, b=b, j0=j0, l=l, R=R: e.tensor_copy(out=V(CIN, 0, 128, l * 288 + j0 * 12, [(12, 12), (1, R)]), in_=pf(b, 128, 0, [(R, 12), (1, R)])),
                         reads=bt, writes=[T(f"CIN{l}")])
            P.op("dve", lambda e, l=l: e.tensor_tensor(out=V(DI, 0, 64, 0, [(64, 32), (1, 64)]), in0=V(DSK, 0, 64, l * 32, [(1, 32), (0, 64)]),
                                                        in1=V(identf, 0, 64, 0, [(0, 32), (1, 64)]), op=ALU.mult), reads=[t_lc, t_const], writes=[T("DI")])
            if kind == "p" and kt > 0:
                P.op("act", lambda e, l=l: e.activation(out=STB[:, :], in_=V(ST, 0, 128, l * 2048, [(1, 2048)]), func=AF.Copy), reads=[T(f"ST{l}_0"), T(f"ST{l}_1")], writes=[T("STB_0"), T("STB_1")])
            ckpt("layerstart")
            norm_to_hT(l, 0)
            thT = TL("hT", 8)
            ckpt("norm1")
            for g6 in range(6):
                wbuf, wt = wload([(512, 8), (1, 512)], win_ap(C_XBC + g6 * 512, 512), key=("xbc", l, g6))
                for jj in range(4):
                    j = g6 * 4 + jj
                    b, bt = fm_proj(wbuf, wt, 512, jj * 128, hT, thT, 8)
                    ci = cpar[0] % 2; cpar[0] += 1
                    S, ACC = CS[ci], CACC[ci]
                    tS, tA = T(f"CS{ci}"), T(f"CACC{ci}")
                    tcin = T(f"CIN{l}")
                    P.op("pool", lambda e, S=S, j=j, l=l: e.tensor_copy(out=V(S, 0, 128, 0, [(L + 3, nseg), (1, 3)]), in_=V(CIN, 0, 128, l * 288 + j * 12, [(3, nseg), (1, 3)])), reads=[tcin], writes=[tS])
                    P.op("act", lambda e, S=S, b=b: e.activation(out=V(S, 0, 128, 3, [(L + 3, nseg), (1, L)]), in_=pf(b, 128, 0, [(L, nseg), (1, L)]), func=AF.Copy), reads=bt, writes=[tS])
                    P.op("pool", lambda e, S=S, j=j, l=l: e.tensor_copy(out=V(CIN, 0, 128, l * 288 + j * 12, [(3, nseg), (1, 3)]), in_=V(S, 0, 128, L, [(L + 3, nseg), (1, 3)])), reads=[tS], writes=[tcin])

                    cwv = lambda k, j=j, l=l: V(CWB, 0, 128, l * 120 + j * 5 + k, [(1, 1)])
                    P.op("act", lambda e, S=S, ACC=ACC, cwv=cwv: e.activation(out=V(ACC, 0, 128, 0, [(L, nseg), (1, L)]), in_=V(S, 0, 128, 0, [(L + 3, nseg), (1, L)]),
                                                                       func=AF.Copy, scale=cwv(0)), reads=[tS, t_lc], writes=[tA])
                    for k in range(1, 4):
                        P.op("dve", lambda e, S=S, ACC=ACC, cwv=cwv, k=k: e.scalar_tensor_tensor(out=V(ACC, 0, 128, 0, [(L, nseg), (1, L)]), in0=V(S, 0, 128, k, [(L + 3, nseg), (1, L)]),
                                                                                       scalar=cwv(k), in1=V(ACC, 0, 128, 0, [(L, nseg), (1, L)]), op0=ALU.mult, op1=ALU.add),
                             reads=[tS, t_lc, tA], writes=[tA])
                    if j < 16:
                        dst, dtok = V(XC, 0, 128, j * NT, [(1, NT)]), T(f"XC{j}")
                    elif j < 20:
                        dst, dtok = V(PH1, 0, 128, BT_OFF + (j - 16) * NT, [(1, NT)]), T(f"BT{j - 16}")
                    else:
                        dst, dtok = V(PH1, 0, 128, CT_OFF + (j - 20) * NT, [(1, NT)]), T(f"CT{j - 20}")
                    P.op("act", lambda e, ACC=ACC, dst=dst, j=j, l=l: e.activation(out=dst, in_=ACC[:, :], func=AF.Silu, bias=V(CWB, 0, 128, l * 120 + j * 5 + 4, [(1, 1)]), scale=1.0),
                         reads=[tA, t_lc], writes=[dtok])
            ckpt("xbc")
            if last_tile:
                R = 3 * nseg
                for half in range(2):
                    b, bt = ps_quad()

                    def trc2(e, half=half, b=b, R=R, l=l):
                        for jj in range(12):
                            j = half * 12 + jj
                            last = e.transpose(pf(b, R, jj * 128, [(1, 128)]), V(CIN, 0, 128, l * 288 + j * 12, [(1, R)]), identf[:, :])
                        return last
                    P.op("pe", trc2, reads=[T(f"CIN{l}"), t_const], writes=bt)
                    P.op("act", lambda e, b=b, R=R: e.activation(out=V(SCR, 0, R, 0, [(1, 1536)]), in_=pf(b, R, 0, [(1, 1536)]), func=AF.Copy), reads=bt, writes=[t_scr])
                    if kind == "p":
                        dst = DA(ncp, ((l * NPS + seq) * 3) * 3072 + half * 1536, [(3072, 3), (1, 1536)])
                    else:
                        dst = DA(ncs, (l * 3 * NSS) * 3072 + half * 1536, [(3072, 3 * NSS), (1, 1536)])
                    P.dma("sp", dst, V(SCR, 0, R, 0, [(1, 1536)]), out_slot(), reads=[t_scr])
            ckpt("newconv")
            for g4 in range(4):
                wbuf, wt = wload([(512, 8), (1, 512)], win_ap(C_Z + g4 * 512, 512), key=("z", l, g4))
                for jj in range(4):
                    j = g4 * 4 + jj
                    b, bt = fm_proj(wbuf, wt, 512, jj * 128, hT, thT, 8)
                    P.op("act", lambda e, b=b, j=j: e.activation(out=V(ZT, 0, 128, j * NT, [(1, NT)]), in_=pf(b, 128, 0, [(1, NT)]), func=AF.Silu), reads=bt, writes=[T(f"ZT{j}")])
            for g2 in range(2):
                wbuf, wt = wload([(512, 8), (1, 512)], win_ap(C_Q + g2 * 512, 512), key=("q", l, g2))
                for jj in range(4):
                    j = g2 * 4 + jj
                    b, bt = fm_proj(wbuf, wt, 512, jj * 128, hT, thT, 8)
                    P.op("dve", lambda e, b=b, j=j: e.tensor_copy(out=V(QTr, 0, 128, j * NT, [(1, NT)]), in_=pf(b, 128, 0, [(1, NT)])), reads=bt, writes=[T(f"QTr{j}")])
            ckpt("zq")
            bdt, bdtt = ps_single()

            def dtmm(e, b=bdt, l=l):
                for c in range(NCH):
                    for k in range(8):
                        last = e.matmul(pf(b, 64, c * 32, [(1, 32)]), lhsT=V(hT, 0, 128, k * NT + c * 64, [(1, 64)]), rhs=V(WDT, 0, 128, l * 256 + k * 32, [(1, 32)]),
                                        start=(k == 0), stop=(k == 7))
                return last
            P.op("pe", dtmm, reads=thT + [t_lc], writes=bdtt)
            tdt = T("dtv")
            P.op("dve", lambda e, b=bdt, l=l: e.tensor_tensor(out=V(DTX, 0, 64, 0, [(32, NCH), (1, 32)]), in0=pf(b, 64, 0, [(32, NCH), (1, 32)]),
                                                               in1=V(DTB, 0, 64, l * 32, [(0, NCH), (1, 32)]), op=ALU.add), reads=bdtt + [t_lc], writes=[tdt])
            P.op("act", lambda e: e.activation(out=DTA[0:64, :], in_=DTX[0:64, :], func=AF.Abs), reads=[tdt], writes=[tdt])
            P.op("act", lambda e: e.activation(out=DTE[0:64, :], in_=DTA[0:64, :], func=AF.Exp, scale=-1.0), reads=[tdt], writes=[tdt])
            P.op("act", lambda e: e.activation(out=DTE[0:64, :], in_=DTE[0:64, :], func=AF.Ln, bias=CONST[0:64, 1:2], scale=1.0), reads=[tdt, t_const], writes=[tdt])
            P.op("dve", lambda e: e.scalar_tensor_tensor(out=DTv[0:64, :], in0=DTX[0:64, :], scalar=0.0, in1=DTE[0:64, :], op0=ALU.max, op1=ALU.add), reads=[tdt], writes=[tdt])
            P.op("dve", lambda e, l=l: e.tensor_tensor(out=V(DAv, 0, 64, 0, [(32, NCH), (1, 32)]), in0=V(DTv, 0, 64, 0, [(32, NCH), (1, 32)]),
                                                        in1=V(ABC, 0, 64, l * 32, [(0, NCH), (1, 32)]), op=ALU.mult), reads=[tdt, t_lc], writes=[tdt])
            ckpt("dt")
            wbuf, wt = wload([(512, 8), (1, 512)], win_ap(C_K, 512), key=("kv", l))
            for c in range(NCH):
                cidx = kt * NCH + c if kind == "p" else 0
                blk = 2 + c
                b, bt = ps_single(3)

                def kvmm(e, b=b, c=c, wbuf=wbuf):
                    for k in range(8):
                        last = e.matmul(pf(b, 64, 0, [(1, 512)]), lhsT=V(hT, 0, 128, k * NT + c * 64, [(1, 64)]), rhs=V(wbuf, 0, 128, k * 512, [(1, 512)]),
                                        start=(k == 0), stop=(k == 7))
                    return last
                P.op("pe", kvmm, reads=thT + [wt], writes=bt)
                oi = 0
                KOb, VOb = KO[oi], VO[oi]
                tko, tvo = T(f"KO{oi}"), T(f"VO{oi}")
                tkk = T("ktmp")
                P.op("act", lambda e, b=b: e.activation(out=SQK[0:64, :], in_=pf(b, 64, 0, [(1, 256)]), func=AF.Square), reads=bt, writes=[tkk])
                P.op("dve", lambda e: e.tensor_reduce(out=SSK[0:64, :], in_=V(SQK, 0, 64, 0, [(64, 4), (1, 64)]), axis=AX.X, op=ALU.add), reads=[tkk], writes=[tkk])
                P.op("act", lambda e: e.activation(out=RK[0:64, :], in_=SSK[0:64, :], func=AF.Sqrt, bias=CONST[0:64, 0:1], scale=1.0 / 64), reads=[tkk, t_const], writes=[tkk])
                P.op("dve", lambda e: e.reciprocal(out=RK[0:64, :], in_=RK[0:64, :]), reads=[tkk], writes=[tkk])
                P.op("dve", lambda e, b=b, KOb=KOb: e.tensor_tensor(out=V(KOb, 0, 64, 0, [(64, 4), (1, 64)]), in0=pf(b, 64, 0, [(64, 4), (1, 64)]),
                                                                     in1=V(RK, 0, 64, 0, [(1, 4), (0, 64)]), op=ALU.mult), reads=bt + [tkk], writes=[tko])
                P.op("dve", lambda e, KOb=KOb, l=l: e.tensor_tensor(out=V(KOb, 0, 64, 0, [(64, 4), (1, 64)]), in0=V(KOb, 0, 64, 0, [(64, 4), (1, 64)]),
                                                                     in1=V(KG, 0, 64, l * 64, [(0, 4), (1, 64)]), op=ALU.mult), reads=[tko, t_lc], writes=[tko])
                P.op("act", lambda e, KOb=KOb: e.activation(out=KNB[0:64, :], in_=KOb[0:64, :], func=AF.Copy), reads=[tko], writes=[T("KNB")])
                b2, bt2 = ps_single()

                def trkk(e, b2=b2):
                    for kv in range(4):
                        last = e.transpose(pb(b2, 64, kv * 64, [(1, 64)]), V(KNB, 0, 64, kv * 64, [(1, 64)]), V(identb, 0, 64, 0, [(1, 64)]))
                    return last
                P.op("pe", trkk, reads=[T("KNB"), t_const], writes=bt2)
                P.op("dve", lambda e, b2=b2, l=l, blk=blk: e.tensor_copy(out=V(KT, 0, 64, l * 1536 + blk * 64, [(384, 4), (1, 64)]), in_=pb(b2, 64, 0, [(64, 4), (1, 64)])),
                     reads=bt2, writes=[ktok[l][blk]])
                P.op("act", lambda e, b=b, VOb=VOb: e.activation(out=VOb[0:64, :], in_=pf(b, 64, 256, [(1, 256)]), func=AF.Copy), reads=bt, writes=[tvo])
                P.op("dve", lambda e, VOb=VOb, l=l, blk=blk: e.tensor_copy(out=V(VBf, 0, 64, l * 1560 + blk * 260, [(65, 4), (1, 64)]), in_=V(VOb, 0, 64, 0, [(64, 4), (1, 64)])),
                     reads=[tvo], writes=[vtok[l][blk]])
                if kind == "s":
                    P.dma("sp", DA(nks, (l * NSS + c) * 64 * 256, [(256, 64), (1, 256)]), KOb[0:64, :], out_slot(), reads=[tko])
                    P.dma("sp", DA(nvs, (l * NSS + c) * 64 * 256, [(256, 64), (1, 256)]), VOb[0:64, :], out_slot(), reads=[tvo])
                elif cidx >= PLEN // 64 - 2:
                    r0 = (cidx - (PLEN // 64 - 2)) * 64
                    P.dma("sp", DA(nkp, ((l * NPS + seq) * 128 + r0) * 256, [(256, 64), (1, 256)]), KOb[0:64, :], out_slot(), reads=[tko])
                    P.dma("sp", DA(nvp, ((l * NPS + seq) * 128 + r0) * 256, [(256, 64), (1, 256)]), VOb[0:64, :], out_slot(), reads=[tvo])

            ckpt("kv")
            tST, tSTB = [T(f"ST{l}_0"), T(f"ST{l}_1")], [T("STB_0"), T("STB_1")]
            for c in range(NCH):
                cs = c * 64
                cidx = kt * NCH + c if kind == "p" else 0
                first = (kind == "p" and cidx == 0)
                if kind == "s":
                    for part in range(2):
                        P.dma("sp", V(SCR, 0, 128, part * 1024, [(128, 8), (1, 128)]), DA(sssm, (l * NSS + c) * 2048 * 128 + part * 1024 * 128, [(128, 128), (128 * 128, 8), (1, 128)]),
                              s_in, writes=[t_scr], cont=(part > 0))
                    b, bt = ps_quad()

                    def trst(e, b=b):
                        for j in range(16):
                            last = e.transpose(pf(b, 128, j * 128, [(1, 128)]), V(SCR, 0, 128, j * 128, [(1, 128)]), identf[:, :])
                        return last
                    P.op("pe", trst, reads=[t_scr, t_const], writes=bt)
                    P.op("dve", lambda e, b=b, l=l: e.tensor_copy(out=V(ST, 0, 128, l * 2048, [(1, 2048)]), in_=pf(b, 128, 0, [(1, 2048)])), reads=bt, writes=tST)
                    P.op("act", lambda e, l=l: e.activation(out=STB[:, :], in_=V(ST, 0, 128, l * 2048, [(1, 2048)]), func=AF.Copy), reads=tST, writes=tSTB)
                ckpt("stateload")
                exi = exctr[0] % 2; exctr[0] += 1
                EX = EXs[exi]; tEX = T(f"EX{exi}")
                bs, bst = ps_single(2)

                def smallmm(e, b=bs, c=c):
                    da = V(DAv, 0, 64, c * 32, [(1, 32)])
                    e.matmul(pf(b, 64, 0, [(1, 32)]), lhsT=Umat[0:64, :], rhs=da, start=True, stop=True)
                    e.matmul(pf(b, 64, 32, [(1, 32)]), lhsT=SUmat[0:64, :], rhs=da, start=True, stop=True)
                    return e.matmul(pf(b, 128, 64, [(1, 32)]), lhsT=V(onesf, 0, 64, 0, [(1, 128)]), rhs=da, start=True, stop=True)
                P.op("pe", smallmm, reads=[tdt, t_const], writes=bst)
                P.op("act", lambda e, b=bs, EX=EX: e.activation(out=EX[0:64, 0:64], in_=pf(b, 64, 0, [(1, 64)]), func=AF.Exp), reads=bst, writes=[tEX])
                P.op("act", lambda e, b=bs, EX=EX: e.activation(out=EX[:, 64:96], in_=pf(b, 128, 64, [(1, 32)]), func=AF.Exp), reads=bst, writes=[tEX])
                for hh in range(2):
                    u = uctr[0] % 2; uctr[0] += 1
                    o1 = u * 1024
                    h0, g0 = hh * 16, hh * 2
                    tR1, tDEC, tXTs, tXDT, tTb, tCBM, tBTOK = [T(f"{n}{u}") for n in ("R1_", "DEC_", "XTs_", "XDT_", "Tb_", "CBM_", "BTOK_")]
                    tSTh, tSTBh = T(f"ST{l}_{hh}"), T(f"STB_{hh}")
                    P.op("dve", lambda e, c=c, o1=o1, h0=h0: e.tensor_tensor(out=V(R1, 0, 64, o1, [(64, 16), (1, 64)]), in0=V(DAv, 0, 64, c * 32 + h0, [(1, 16), (0, 64)]),
                                                                           in1=V(Umat, 0, 64, 0, [(0, 16), (1, 64)]), op=ALU.mult), reads=[tdt, t_const], writes=[tR1])
                    bq, bqt = ps_pair()

                    def segmm(e, b=bq, o1=o1):
                        for i in range(2):
                            last = e.matmul(pf(b, 64, i * 512, [(1, 512)]), lhsT=SUmat[0:64, :], rhs=V(R1, 0, 64, o1 + i * 512, [(1, 512)]), start=True, stop=True)
                        return last
                    P.op("pe", segmm, reads=[tR1, t_const], writes=bqt)
                    P.op("act", lambda e, b=bq, o1=o1: e.activation(out=V(DEC, 0, 64, o1, [(1, 1024)]), in_=pf(b, 64, 0, [(1, 1024)]), func=AF.Exp), reads=bqt, writes=[tDEC])
                    bc_, bct = ps_single()

                    def cbmm(e, b=bc_, cs=cs, g0=g0):
                        for gg in range(2):
                            g = g0 + gg
                            last = e.matmul(pf(b, 64, gg * 64, [(1, 64)]), lhsT=V(PH1, 0, 128, BT_OFF + g * NT + cs, [(1, 64)]), rhs=V(PH1, 0, 128, CT_OFF + g * NT + cs, [(1, 64)]), start=True, stop=True)
                        return last
                    P.op("pe", cbmm, reads=[T(f"BT{g0}"), T(f"BT{g0 + 1}"), T(f"CT{g0}"), T(f"CT{g0 + 1}")], writes=bct)
                    P.op("dve", lambda e, b=bc_, u=u: e.tensor_tensor(out=V(CBM, 0, 64, u * 128, [(64, 2), (1, 64)]), in0=pf(b, 64, 0, [(64, 2), (1, 64)]),
                                                                       in1=V(Umb, 0, 64, 0, [(0, 2), (1, 64)]), op=ALU.mult), reads=bct + [t_const], writes=[tCBM])
                    P.op("dve", lambda e, o1=o1, u=u: e.tensor_tensor(out=V(DEC, 0, 64, o1, [(512, 2), (64, 8), (1, 64)]), in0=V(DEC, 0, 64, o1, [(512, 2), (64, 8), (1, 64)]),
                                                                       in1=V(CBM, 0, 64, u * 128, [(64, 2), (0, 8), (1, 64)]), op=ALU.mult), reads=[tDEC, tCBM], writes=[tDEC])
                    bx, bxt = ps_single(2)

                    def trxx(e, b=bx, cs=cs, hh=hh):
                        for j in range(8):
                            last = e.transpose(pb(b, 64, j * 128, [(1, 128)]), V(XC, 0, 128, (hh * 8 + j) * NT + cs, [(1, 64)]), identb[:, :])
                        return last
                    P.op("pe", trxx, reads=[T(f"XC{hh * 8 + j}") for j in range(8)] + [t_const], writes=bxt)
                    P.op("act", lambda e, b=bx, o1=o1: e.activation(out=V(XTs, 0, 64, o1, [(1, 1024)]), in_=pb(b, 64, 0, [(1, 1024)]), func=AF.Copy), reads=bxt, writes=[tXTs])
                    P.op("dve", lambda e, b=bx, c=c, o1=o1, h0=h0: e.tensor_tensor(out=V(XDT, 0, 64, o1, [(64, 16), (1, 64)]), in0=pb(b, 64, 0, [(64, 16), (1, 64)]),
                                                                                 in1=V(DTv, 0, 64, c * 32 + h0, [(1, 16), (0, 64)]), op=ALU.mult), reads=bxt + [tdt], writes=[tXDT])
                    P.op("pool", lambda e, o1=o1, h0=h0, EX=EX: e.tensor_tensor(out=V(XDTS, 0, 64, 2 * o1, [(64, 16), (1, 64)]), in0=V(XDT, 0, 64, o1, [(64, 16), (1, 64)]),
                                                                             in1=V(EX, 0, 64, 32 + h0, [(1, 16), (0, 64)]), op=ALU.mult), reads=[tXDT, tEX], writes=[tR1])
                    bb, bbt = ps_single()

                    def trb(e, b=bb, cs=cs, g0=g0):
                        for gg in range(2):
                            last = e.transpose(pb(b, 64, gg * 128, [(1, 128)]), V(PH1, 0, 128, BT_OFF + (g0 + gg) * NT + cs, [(1, 64)]), identb[:, :])
                        return last
                    P.op("pe", trb, reads=[T(f"BT{g0}"), T(f"BT{g0 + 1}"), t_const], writes=bbt)
                    P.op("act", lambda e, b=bb, u=u: e.activation(out=V(BTOK, 0, 64, u * 256, [(1, 256)]), in_=pb(b, 64, 0, [(1, 256)]), func=AF.Copy), reads=bbt, writes=[tBTOK])
                    by, byt = ps_pair()

                    def ymm(e, b=by, o1=o1, h0=h0):
                        for h in range(16):
                            e.matmul(pf(b, 64, h * 64, [(1, 64)]), lhsT=V(DEC, 0, 64, o1 + h * 64, [(1, 64)]), rhs=V(XDT, 0, 64, o1 + h * 64, [(1, 64)]), start=True, stop=False)
                            last = e.matmul(pf(b, 64, h * 64, [(1, 64)]), lhsT=V(DI, 0, 64, (h0 + h) * 64, [(1, 64)]), rhs=V(XTs, 0, 64, o1 + h * 64, [(1, 64)]), start=False, stop=True)
                        return last
                    P.op("pe", ymm, reads=[tDEC, tXDT, tXTs, T("DI")], writes=byt)
                    if not first:
                        bp, bpt = ps_pair()

                        def ypmm(e, b=bp, cs=cs, g0=g0):
                            for gg in range(2):
                                g = g0 + gg
                                last = e.matmul(pf(b, 64, gg * 512, [(1, 512)]), lhsT=V(PH1, 0, 128, CT_OFF + g * NT + cs, [(1, 64)]), rhs=V(STB, 0, 128, g * 512, [(1, 512)]), start=True, stop=True)
                            return last
                        P.op("pe", ypmm, reads=[T(f"CT{g0}"), T(f"CT{g0 + 1}"), tSTBh], writes=bpt)
                        P.op("dve", lambda e, b=bp, o1=o1, h0=h0, EX=EX: e.tensor_tensor(out=V(Tb, 0, 64, o1, [(64, 16), (1, 64)]), in0=pf(b, 64, 0, [(64, 16), (1, 64)]),
                                                                                      in1=V(EX, 0, 64, h0, [(1, 16), (0, 64)]), op=ALU.mult), reads=bpt + [tEX], writes=[tTb])
                        P.op("dve", lambda e, b=by, o1=o1: e.tensor_tensor(out=V(Tb, 0, 64, o1, [(1, 1024)]), in0=pf(b, 64, 0, [(1, 1024)]), in1=V(Tb, 0, 64, o1, [(1, 1024)]), op=ALU.add),
                             reads=byt + [tTb], writes=[tTb])
                    else:
                        P.op("dve", lambda e, b=by, o1=o1: e.tensor_copy(out=V(Tb, 0, 64, o1, [(1, 1024)]), in_=pf(b, 64, 0, [(1, 1024)])), reads=byt, writes=[tTb])
                    bs2, bs2t = ps_pair()

                    def stmm(e, b=bs2, o1=o1, u=u):
                        for gg in range(2):
                            last = e.matmul(pf(b, 128, gg * 512, [(1, 512)]), lhsT=V(BTOK, 0, 64, u * 256 + gg * 128, [(1, 128)]), rhs=V(XDTS, 0, 64, 2 * o1 + gg * 512, [(1, 512)]), start=True, stop=True)
                        return last
                    P.op("pe", stmm, reads=[tBTOK, tR1], writes=bs2t)
                    so = l * 2048 + hh * 1024
                    if first:
                        P.op("dve", lambda e, b=bs2, so=so: e.tensor_copy(out=V(ST, 0, 128, so, [(1, 1024)]), in_=pf(b, 128, 0, [(1, 1024)])), reads=bs2t, writes=[tSTh])
                    else:
                        P.op("dve", lambda e, so=so, h0=h0, EX=EX: e.tensor_tensor(out=V(ST, 0, 128, so, [(64, 16), (1, 64)]), in0=V(ST, 0, 128, so, [(64, 16), (1, 64)]),
                                                                                 in1=V(EX, 0, 128, 64 + h0, [(1, 16), (0, 64)]), op=ALU.mult), reads=[tSTh, tEX], writes=[tSTh])
                        P.op("dve", lambda e, b=bs2, so=so: e.tensor_tensor(out=V(ST, 0, 128, so, [(1, 1024)]), in0=V(ST, 0, 128, so, [(1, 1024)]),
                                                                             in1=pf(b, 128, 0, [(1, 1024)]), op=ALU.add), reads=bs2t + [tSTh], writes=[tSTh])
                    P.op("act", lambda e, so=so, hh=hh: e.activation(out=V(STB, 0, 128, hh * 1024, [(1, 1024)]), in_=V(ST, 0, 128, so, [(1, 1024)]), func=AF.Copy), reads=[tSTh], writes=[tSTBh])
                    bz, bzt = ps_single()

                    def trz(e, b=bz, cs=cs, hh=hh):
                        for j in range(8):
                            last = e.transpose(pb(b, 64, j * 128, [(1, 128)]), V(ZT, 0, 128, (hh * 8 + j) * NT + cs, [(1, 64)]), identb[:, :])
                        return last
                    P.op("pe", trz, reads=[T(f"ZT{hh * 8 + j}") for j in range(8)] + [t_const], writes=bzt)
                    P.op("dve", lambda e, b=bz, o1=o1: e.tensor_tensor(out=V(Tb, 0, 64, o1, [(1, 1024)]), in0=V(Tb, 0, 64, o1, [(1, 1024)]), in1=pb(b, 64, 0, [(1, 1024)]), op=ALU.mult),
                         reads=bzt + [tTb], writes=[tTb])
                    P.op("act", lambda e, o1=o1: e.activation(out=V(R1, 0, 64, o1, [(1, 1024)]), in_=V(Tb, 0, 64, o1, [(1, 1024)]), func=AF.Square), reads=[tTb], writes=[tR1])
                    tss = T(f"ss4_{u}")
                    P.op("dve", lambda e, o1=o1, u=u: e.tensor_reduce(out=V(SS4, 0, 64, u * 2, [(1, 2)]), in_=V(R1, 0, 64, o1, [(512, 2), (1, 512)]), axis=AX.X, op=ALU.add), reads=[tR1], writes=[tss])
                    P.op("act", lambda e, u=u: e.activation(out=V(RS4, 0, 64, u * 2, [(1, 2)]), in_=V(SS4, 0, 64, u * 2, [(1, 2)]), func=AF.Sqrt, bias=CONST[0:64, 0:1], scale=1.0 / 512), reads=[tss, t_const], writes=[tss])
                    P.op("dve", lambda e, u=u: e.reciprocal(out=V(RS4, 0, 64, u * 2, [(1, 2)]), in_=V(RS4, 0, 64, u * 2, [(1, 2)])), reads=[tss], writes=[tss])
                    P.op("dve", lambda e, o1=o1, u=u: e.tensor_tensor(out=V(YN, 0, 64, o1, [(512, 2), (1, 512)]), in0=V(Tb, 0, 64, o1, [(512, 2), (1, 512)]),
                                                                       in1=V(RS4, 0, 64, u * 2, [(1, 2), (0, 512)]), op=ALU.mult), reads=[tTb, tss], writes=[tXDT])
                    bt_, btt = ps_single()

                    def try_(e, b=bt_, o1=o1):
                        for j in range(8):
                            last = e.transpose(pb(b, 128, j * 64, [(1, 64)]), V(YN, 0, 64, o1 + j * 128, [(1, 128)]), V(identb, 0, 64, 0, [(1, 64)]))
                        return last
                    P.op("pe", try_, reads=[tXDT, t_const], writes=btt)
                    P.op("dve", lambda e, b=bt_, cs=cs, l=l, hh=hh: e.tensor_tensor(out=V(YT, 0, 128, hh * 8 * NT + cs, [(NT, 8), (1, 64)]), in0=pb(b, 128, 0, [(64, 8), (1, 64)]),
                                                                                  in1=V(GSSM, 0, 128, l * 16 + hh * 8, [(1, 8), (0, 64)]), op=ALU.mult), reads=btt + [t_lc], writes=[T(f"YT{c}")])
                final_state = (kind == "s") or (kind == "p" and cidx == PLEN // 64 - 1)
                if final_state:
                    b, bt = ps_quad()

                    def trso(e, b=b, l=l):
                        for j in range(16):
                            last = e.transpose(pf(b, 128, j * 128, [(1, 128)]), V(ST, 0, 128, l * 2048 + j * 128, [(1, 128)]), identf[:, :])
                        return last
                    P.op("pe", trso, reads=[T(f"ST{l}_0"), T(f"ST{l}_1"), t_const], writes=bt)
                    P.op("act", lambda e, b=b: e.activation(out=V(SCR, 0, 128, 0, [(1, 2048)]), in_=pf(b, 128, 0, [(1, 2048)]), func=AF.Copy), reads=bt, writes=[t_scr])
                    osl = out_slot()
                    for part in range(2):
                        if kind == "s":
                            dst = DA(nss, (l * NSS + c) * 2048 * 128 + part * 1024 * 128, [(128, 128), (128 * 128, 8), (1, 128)])
                        else:
                            dst = DA(nsp, (l * NPS + seq) * 2048 * 128 + part * 1024 * 128, [(128, 128), (128 * 128, 8), (1, 128)])
                        P.dma("sp", dst, V(SCR, 0, 128, part * 1024, [(128, 8), (1, 128)]), osl, reads=[t_scr], cont=(part > 0))
                bq_, bqt_ = ps_single(2)

                def trq(e, b=bq_, cs=cs):
                    for j in range(8):
                        last = e.transpose(pb(b, 64, j * 128, [(1, 128)]), V(QTr, 0, 128, j * NT + cs, [(1, 64)]), identb[:, :])
                    return last
                P.op("pe", trq, reads=TL("QTr", 8) + [t_const], writes=bqt_)
                tq = T("SQQ")
                P.op("act", lambda e, b=bq_: e.activation(out=SQQ[0:64, 0:1024], in_=pb(b, 64, 0, [(1, 1024)]), func=AF.Square), reads=bqt_, writes=[tq])
                P.op("dve", lambda e: e.tensor_reduce(out=SSQ[0:64, :], in_=V(SQQ, 0, 64, 0, [(64, 16), (1, 64)]), axis=AX.X, op=ALU.add), reads=[tq], writes=[tq])
                P.op("act", lambda e: e.activation(out=RQ[0:64, :], in_=SSQ[0:64, :], func=AF.Sqrt, bias=CONST[0:64, 0:1], scale=1.0 / 64), reads=[tq, t_const], writes=[tq])
                P.op("dve", lambda e: e.reciprocal(out=RQ[0:64, :], in_=RQ[0:64, :]), reads=[tq], writes=[tq])
                P.op("dve", lambda e, b=bq_: e.tensor_tensor(out=V(QN, 0, 64, 0, [(64, 16), (1, 64)]), in0=pb(b, 64, 0, [(64, 16), (1, 64)]),
                                                              in1=V(RQ, 0, 64, 0, [(1, 16), (0, 64)]), op=ALU.mult), reads=bqt_ + [tq], writes=[T("QN")])
                P.op("dve", lambda e, l=l: e.tensor_tensor(out=V(QN, 0, 64, 0, [(64, 16), (1, 64)]), in0=V(QN, 0, 64, 0, [(64, 16), (1, 64)]),
                                                            in1=V(QG8, 0, 64, l * 64, [(0, 16), (1, 64)]), op=ALU.mult), reads=[T("QN"), t_lc], writes=[T("QN")])
                bq2, bq2t = ps_single()

                def trq2(e, b=bq2):
                    for h in range(16):
                        last = e.transpose(pb(b, 64, h * 64, [(1, 64)]), V(QN, 0, 64, h * 64, [(1, 64)]), V(identb, 0, 64, 0, [(1, 64)]))
                    return last
                P.op("pe", trq2, reads=[T("QN"), t_const], writes=bq2t)
                P.op("act", lambda e, b=bq2: e.activation(out=QTc[0:64, :], in_=pb(b, 64, 0, [(1, 1024)]), func=AF.Copy), reads=bq2t, writes=[T("QTc")])
                if kind == "s":
                    P.dma("sp", V(SCR, 0, 128, 0, [(1, 256)]), DA(ck, (l * NSS + c) * 128 * 256, [(256, 128), (1, 256)]), s_in, writes=[t_scr])
                    P.op("act", lambda e: e.activation(out=V(SCR, 0, 128, 512, [(1, 128)]).bitcast(BF16), in_=V(SCR, 0, 128, 0, [(1, 256)]), func=AF.Copy), reads=[t_scr], writes=[t_scr])
                    bk_, bkt_ = ps_single()

                    def trk(e, b=bk_):
                        src = V(SCR, 0, 128, 512, [(1, 128)]).bitcast(BF16)
                        for kv in range(4):
                            last = e.transpose(pb(b, 64, kv * 128, [(1, 128)]), src[:, kv * 64:(kv + 1) * 64], identb[:, :])
                        return last
                    P.op("pe", trk, reads=[t_scr, t_const], writes=bkt_)
                    P.op("dve", lambda e, b=bk_: e.tensor_copy(out=V(KP, 0, 64, 0, [(128, 4), (1, 128)]), in_=pb(b, 64, 0, [(128, 4), (1, 128)])), reads=bkt_, writes=[T("KP")])
                    P.dma("sp", V(SCR, 0, 64, 1024, [(256, 2), (1, 256)]), DA(cv, (l * NSS + c) * 128 * 256, [(256, 64), (64 * 256, 2), (1, 256)]), s_in, writes=[t_scr])
                    P.op("dve", lambda e: e.tensor_copy(out=V(VP, 0, 64, 0, [(260, 2), (65, 4), (1, 64)]), in_=V(SCR, 0, 64, 1024, [(256, 2), (64, 4), (1, 64)])), reads=[t_scr], writes=[T("VP")])
                ckpt("att_q")
                blocks = []
                if kind == "p":
                    for pos in range(3):
                        if cidx + pos - 2 < 0:
                            continue
                        blk = c + pos
                        blocks.append((pos,
                                       lambda kv, blk=blk, l=l: V(KT, 0, 64, l * 1536 + kv * 384 + blk * 64, [(1, 64)]),
                                       lambda kv, blk=blk, l=l: V(VBf, 0, 64, l * 1560 + blk * 260 + kv * 65, [(1, 65)]),
                                       [ktok[l][blk], vtok[l][blk]]))
                else:
                    for pos in range(2):
                        blocks.append((pos,
                                       lambda kv, pos=pos: V(KP, 0, 64, kv * 128 + pos * 64, [(1, 64)]),
                                       lambda kv, pos=pos: V(VP, 0, 64, pos * 260 + kv * 65, [(1, 65)]),
                                       [T("KP"), T("VP")]))
                    blk = 2 + c
                    blocks.append((2,
                                   lambda kv, blk=blk, l=l: V(KT, 0, 64, l * 1536 + kv * 384 + blk * 64, [(1, 64)]),
                                   lambda kv, blk=blk, l=l: V(VBf, 0, 64, l * 1560 + blk * 260 + kv * 65, [(1, 65)]),
                                   [ktok[l][blk], vtok[l][blk]]))
                bo, bot = ps_quad(2)
                nb = len(blocks)
                for bi, (pos, kfn, vfn, btoks) in enumerate(blocks):
                    bsc, bsct = ps_pair()

                    def scmm(e, b=bsc, kfn=kfn, pos=pos):
                        for kv in range(4):
                            e.matmul(pf(b, 64, kv * 256, [(1, 256)]), lhsT=kfn(kv), rhs=V(QTc, 0, 64, kv * 256, [(1, 256)]), start=True, stop=False)
                            last = e.matmul(pf(b, 64, kv * 256, [(1, 256)]), lhsT=V(identb, 0, 64, 0, [(1, 64)]), rhs=V(BIAS, 0, 64, pos * 1024 + kv * 256, [(1, 256)]), start=False, stop=True)
                        return last
                    P.op("pe", scmm, reads=[btoks[0], T("QTc"), T("bias"), t_const], writes=bsct)
                    pi = 0
                    P.op("act", lambda e, b=bsc, pi=pi: e.activation(out=PT[pi][0:64, :], in_=pf(b, 64, 0, [(1, 1024)]), func=AF.Exp), reads=bsct, writes=[T(f"PT{pi}")])

                    def omm(e, b=bo, vfn=vfn, pi=pi, bi=bi, nb=nb):
                        for h in range(16):
                            last = e.matmul(pf(b, 64, h * 128, [(1, 65)]), lhsT=V(PT[pi], 0, 64, h * 64, [(1, 64)]), rhs=vfn(h // 4), start=(bi == 0 and h % 4 == 0), stop=(bi == nb - 1), skip_group_check=True)
                        return last
                    P.op("pe", omm, reads=[T(f"PT{pi}"), btoks[1]], writes=bot)
                tden = T("den")
                P.op("dve", lambda e, b=bo, l=l: e.tensor_tensor(out=DEN[0:64, :], in0=pf(b, 64, 64, [(128, 16)]), in1=V(ESINK, 0, 64, l * 16, [(1, 16)]), op=ALU.add), reads=bot + [t_lc], writes=[tden])
                P.op("dve", lambda e: e.reciprocal(out=RDEN[0:64, :], in_=DEN[0:64, :]), reads=[tden], writes=[tden])
                P.op("dve", lambda e, b=bo: e.tensor_tensor(out=V(ON, 0, 64, 0, [(64, 16), (1, 64)]), in0=pf(b, 64, 0, [(128, 16), (1, 64)]),
                                                             in1=V(RDEN, 0, 64, 0, [(1, 16), (0, 64)]), op=ALU.mult), reads=bot + [tden], writes=[T("QN")])
                bo2, bo2t = ps_single()

                def tro(e, b=bo2):
                    for j in range(8):
                        last = e.transpose(pb(b, 128, j * 64, [(1, 64)]), V(ON, 0, 64, j * 128, [(1, 128)]), V(identb, 0, 64, 0, [(1, 64)]))
                    return last
                P.op("pe", tro, reads=[T("QN"), t_const], writes=bo2t)
                P.op("act", lambda e, b=bo2, cs=cs: e.activation(out=V(OT, 0, 128, cs, [(NT, 8), (1, 64)]), in_=pb(b, 128, 0, [(64, 8), (1, 64)]), func=AF.Copy), reads=bo2t, writes=[T(f"OT{c}")])

            ckpt("att_done")
            if kind == "p" and not last_tile:
                P.op("pool", lambda e, l=l: e.tensor_copy(out=V(KT, 0, 64, l * 1536, [(384, 4), (1, 128)]), in_=V(KT, 0, 64, l * 1536 + 256, [(384, 4), (1, 128)])),
                     reads=ktok[l][4:6], writes=ktok[l][0:2])
                P.op("pool", lambda e, l=l: e.tensor_copy(out=V(VBf, 0, 64, l * 1560, [(1, 520)]), in_=V(VBf, 0, 64, l * 1560 + 4 * 260, [(1, 520)])),
                     reads=vtok[l][4:6], writes=vtok[l][0:2])

            tYT, tOT = TL("YT", NCH), TL("OT", NCH)
            for J in range(2):
                wgs, wgst = wload([(512, 8), (1, 512)], win_ap(C_GS + J * 512, 512), key=("gs", l, J))
                wga, wgat = wload([(512, 8), (1, 512)], win_ap(C_GA + J * 512, 512), key=("ga", l, J))
                wba, wbat = wload([(512, 8), (1, 512)], DA(w_br_attn, l * D * D + J * 512, [(D, 128), (128 * D, 8), (1, 512)]), key=("ba", l, J))
                for half in range(2):
                    wbs, wbst = wload_bs(l, J * 512 + half * 256)
                    for j2 in range(2):
                        jj = half * 2 + j2
                        j = J * 4 + jj
                        b, bt = fm_proj(wgs, wgst, 512, jj * 128, hT, thT, 8)
                        P.op("act", lambda e, b=b: e.activation(out=GSs[:, :], in_=pf(b, 128, 0, [(1, NT)]), func=AF.Sigmoid), reads=bt, writes=[T("GSs")])
                        b, bt = fm_proj(wga, wgat, 512, jj * 128, hT, thT, 8)
                        P.op("act", lambda e, b=b: e.activation(out=GAs[:, :], in_=pf(b, 128, 0, [(1, NT)]), func=AF.Sigmoid), reads=bt, writes=[T("GAs")])
                        b, bt = fm_proj(wbs, wbst, 256, j2 * 128, YT, tYT, 16)
                        P.op("dve", lambda e, b=b: e.tensor_tensor(out=MXs[:, :], in0=pf(b, 128, 0, [(1, NT)]), in1=GSs[:, :], op=ALU.mult), reads=bt + [T("GSs")], writes=[T("MXs")])
                        b, bt = fm_proj(wba, wbat, 512, jj * 128, OT, tOT, 8)
                        P.op("dve", lambda e, b=b: e.tensor_tensor(out=GAs[:, :], in0=pf(b, 128, 0, [(1, NT)]), in1=GAs[:, :], op=ALU.mult), reads=bt + [T("GAs")], writes=[T("GAs")])
                        P.op("dve", lambda e, j=j: e.tensor_tensor(out=V(MIXT, 0, 128, j * NT, [(1, NT)]), in0=MXs[:, :], in1=GAs[:, :], op=ALU.add),
                             reads=[T("MXs"), T("GAs")], writes=[T(f"ZT{j}")])
            tMIX = TL("ZT", 8)
            for J in range(2):
                wo, wot = wload([(512, 8), (1, 512)], DA(w_out, l * D * D + J * 512, [(D, 128), (128 * D, 8), (1, 512)]), key=("wo", l, J))
                for jj in range(4):
                    j = J * 4 + jj
                    b, bt = fm_proj(wo, wot, 512, jj * 128, MIXT, tMIX, 8, nreads=nseg)
                    for sg in range(nseg):
                        P.op("dve", lambda e, b=b, j=j, sg=sg, l=l: e.scalar_tensor_tensor(out=V(xT, 0, 128, j * NT + sg * L, [(1, L)]), in0=pf(b, 128, sg * L, [(1, L)]),
                                                                                        scalar=modv(l, 2, j, seq0 + sg), in1=V(xT, 0, 128, j * NT + sg * L, [(1, L)]), op0=ALU.mult, op1=ALU.add),
                             reads=bt + [T("modt")], writes=[txT[j]])

            ckpt("phase2")
            norm_to_hT(l, 1)
            wgu0 = l * D * 2 * DFF
            for G in range(6):
                ncg = 4 if G < 5 else 2
                wg, wgt = wload([(512, 8), (1, ncg * 128)], DA(w_gate_up, wgu0 + G * 512, [(2 * DFF, 128), (128 * 2 * DFF, 8), (1, ncg * 128)]), key=("wg", l, G))
                wu, wut = wload([(512, 8), (1, ncg * 128)], DA(w_gate_up, wgu0 + DFF + G * 512, [(2 * DFF, 128), (128 * 2 * DFF, 8), (1, ncg * 128)]), key=("wu", l, G))
                for jj in range(ncg):
                    j = G * 4 + jj
                    b, bt = fm_proj(wg, wgt, 512, jj * 128, hT, thT, 8)
                    si = j % 2
                    P.op("act", lambda e, b=b, si=si: e.activation(out=SGs[si][:, :], in_=pf(b, 128, 0, [(1, NT)]), func=AF.Silu), reads=bt, writes=[T(f"SGs{si}")])
                    b, bt = fm_proj(wu, wut, 512, jj * 128, hT, thT, 8)
                    P.op("dve", lambda e, b=b, si=si, j=j: e.tensor_tensor(out=V(ACTT, 0, 128, j * NT, [(1, NT)]), in0=pf(b, 128, 0, [(1, NT)]), in1=SGs[si][:, :], op=ALU.mult),
                         reads=bt + [T(f"SGs{si}")], writes=[acttok(j)])
            tACT = [acttok(j) for j in range(22)]
            for j in range(8):
                wd, wdt = wload_down(l, j)
                b, bt = fm_proj(wd, wdt, 128, 0, ACTT, tACT, 22, nreads=nseg)
                for sg in range(nseg):
                    P.op("dve", lambda e, b=b, j=j, sg=sg, l=l: e.scalar_tensor_tensor(out=V(xT, 0, 128, j * NT + sg * L, [(1, L)]), in0=pf(b, 128, sg * L, [(1, L)]),
                                                                                    scalar=modv(l, 5, j, seq0 + sg), in1=V(xT, 0, 128, j * NT + sg * L, [(1, L)]), op0=ALU.mult, op1=ALU.add),
                         reads=bt + [T("modt")], writes=[txT[j]])

        ckpt("ffn")
        for sub in range(2):
            b, bt = ps_pair()

            def try2(e, b=b, sub=sub):
                for k in range(8):
                    last = e.transpose(pf(b, 128, k * 128, [(1, 128)]), V(xT, 0, 128, k * NT + sub * 128, [(1, 128)]), identf[:, :])
                return last
            P.op("pe", try2, reads=txT + [t_const], writes=bt)
            P.op("act", lambda e, b=b: e.activation(out=V(SCR, 0, 128, 0, [(1, 1024)]), in_=pf(b, 128, 0, [(1, 1024)]), func=AF.Copy), reads=bt, writes=[t_scr])
            P.dma("sp", DA(ydst, (row0 + sub * 128) * D, [(D, 128), (1, D)]), V(SCR, 0, 128, 0, [(1, 1024)]), out_slot(), reads=[t_scr])

    try:
        ckpt("consts")
        if NSS:
            do_tile("s", 0, 0)
        for s in range(NPS):
            for kt in range(NPT):
                do_tile("p", s, kt)
    except StopBuild:
        pass
    import os as _os2
    P.emit(window=int(_os2.environ.get("K_WINDOW", "600")))
    return nc


_W_NAMES = ["rel_bias", "ada_w", "ada_b", "norm_mix_g", "norm_ffn_g", "w_in", "conv_w", "conv_b", "dt_bias", "a_log",
            "d_skip", "ssm_norm_g", "q_norm_g", "k_norm_g", "sinks", "w_br_ssm", "w_br_attn", "w_out", "w_gate_up", "w_down"]
_PROG_CACHE = {}


def run_cores(inputs, n_cores, NPS, PLEN, NSS):
    f = lambda a: np.ascontiguousarray(np.asarray(a, dtype=np.float32))
    key = (NPS, PLEN, NSS)
    nc = build_program(NPS, PLEN, NSS)
    oh = onehot_table()
    in_maps = []
    for i in range(n_cores):
        m = {n: f(inputs[n]) for n in _W_NAMES}
        m["onehot"] = oh
        m["xp"] = f(inputs["x_prompt"][i * NPS:(i + 1) * NPS]).reshape(NPS * PLEN, D)
        m["xs"] = f(inputs["x_sample"][i * NSS:(i + 1) * NSS]).reshape(NSS * 64, D)
        m["ck"] = f(inputs["cache_k"][:, i * NSS:(i + 1) * NSS]).reshape(2, NSS, 128, 256)
        m["cv"] = f(inputs["cache_v"][:, i * NSS:(i + 1) * NSS]).reshape(2, NSS, 128, 256)
        m["sconv"] = f(inputs["state_conv"][:, i * NSS:(i + 1) * NSS]).reshape(2, NSS * 3, 3072)
        m["sssm"] = f(inputs["state_ssm"][:, i * NSS:(i + 1) * NSS]).reshape(2, NSS, 2048, 128)
        m["cvec"] = np.concatenate([f(inputs["c_prompt"][i * NPS:(i + 1) * NPS]), f(inputs["c_sample"][i * NSS:(i + 1) * NSS])], axis=0)
        in_maps.append(m)
    res = run_bass_kernel_spmd(nc, in_maps, core_ids=list(range(n_cores)))
    R = res.results
    cat = lambda name, shp, ax: np.concatenate([np.asarray(r[name], dtype=np.float32).reshape(shp) for r in R], axis=ax)
    out = (
        cat("yp", (NPS, PLEN, D), 0), cat("ys", (NSS, 64, D), 0),
        cat("ncp", (2, NPS, 3, 3072), 1), cat("ncs", (2, NSS, 3, 3072), 1),
        cat("nsp", (2, NPS, 32, 64, 128), 1), cat("nss", (2, NSS, 32, 64, 128), 1),
        cat("nkp", (2, NPS, 128, 4, 64), 1), cat("nks", (2, NSS, 64, 4, 64), 1),
        cat("nvp", (2, NPS, 128, 4, 64), 1), cat("nvs", (2, NSS, 64, 4, 64), 1),
    )
    return out


def kernel(**inputs):
    return run_cores(inputs, 8, 2, 2048, 4)
```

```python
import math
import numpy as np
from contextlib import ExitStack
import concourse.bass as bass
import concourse.mybir as mybir
from concourse.bass_utils import run_bass_kernel_spmd

F32 = mybir.dt.float32
BF16 = mybir.dt.bfloat16
AF = mybir.ActivationFunctionType
ALU = mybir.AluOpType
AX = mybir.AxisListType

SAME_ENGINE_SYNC = True
SLACK_US = 0.3
SAME_HOP_US = 0.25
HOP_US = 0.7
EPS = 1e-6
D = 1024
KD = 8
NT = 256
NCH = 4
INC = 8736
C_Z, C_XBC, C_DT, C_Q, C_K, C_GS, C_GA = 0, 2048, 5120, 5152, 6176, 6688, 7712
DFF = 2816


class Tok:
    __slots__ = ("w", "r", "name", "excl")

    def __init__(self, name=""):
        self.w = None
        self.r = []
        self.name = name
        self.excl = False


class Slot:
    def __init__(self, sem, name=""):
        self.sem = sem
        self.name = name
        self.last = None
        self.queue = None


class Op:
    __slots__ = ("id", "stream", "fn", "dma", "slot", "deps", "dur", "issue", "ev", "succ", "nleft", "ready")


class _Rec:
    def __init__(self, stream):
        self.stream = stream
        self.t = 0.0

    def then_inc(self, *a, **k):
        return self

    def __getattr__(self, name):
        def f(*a, **k):
            def fsz(ap):
                n = 1
                for d in list(ap.shape)[1:]:
                    n *= d
                return n
            st = self.stream
            if name == "matmul":
                rhs = k.get("rhs", a[2] if len(a) > 2 else None)
                n = fsz(rhs)
                mult = 4.0 if rhs.dtype == F32 else 1.0
                self.t += max(n, 64) * mult / 2050.0 + 0.012
            elif name == "transpose":
                src = k.get("in_", a[1] if len(a) > 1 else None)
                mult = 4.0 if src.dtype == F32 else 1.0
                self.t += max(fsz(src), 64) * mult / 2050.0 + 0.03
            else:
                out = k.get("out", a[0] if a else None)
                n = fsz(out) if out is not None else 64
                if name == "reciprocal":
                    n *= 6
                if st == "dve":
                    self.t += n / 960.0 + 0.08
                elif st == "act":
                    self.t += n / 1400.0 + 0.22
                else:
                    self.t += n / 500.0 + 0.15
            return self
        return f


def _os_env_inorder():
    import os
    return os.environ.get("K_INORDER", "pe,sp").split(",")


class Prog:
    def __init__(self, nc):
        self.nc = nc
        self.es = ExitStack()
        self.names = ("pe", "act", "dve", "pool", "sp")
        self.sems = {nm: self.es.enter_context(nc.semaphore("sem_" + nm)) for nm in self.names}
        self.in_order_safe = set(_os_env_inorder())
        self.slots = []
        self.ops = []

    def sbuf(self, name, shape, dtype):
        return self.es.enter_context(self.nc.sbuf_tensor(name, list(shape), dtype))

    def psum(self, name, shape, dtype):
        return self.es.enter_context(self.nc.psum_tensor(name, list(shape), dtype))

    def slot(self, name):
        sem = self.es.enter_context(self.nc.semaphore("dq_" + name))
        s = Slot(sem, name)
        self.slots.append(s)
        return s

    def _mk(self, stream, reads, writes):
        o = Op()
        o.id = len(self.ops)
        o.stream = stream
        o.fn = None
        o.dma = None
        o.slot = None
        deps = set()
        for t in reads:
            if t.w is not None:
                deps.add(t.w)
            if t.excl:
                deps.update(t.r)
        for t in writes:
            if t.w is not None:
                deps.add(t.w)
            deps.update(t.r)
        o.deps = deps
        for t in reads:
            t.r.append(o.id)
        for t in writes:
            t.w = o.id
            t.r = []
        self.ops.append(o)
        return o

    def op(self, eng, fn, reads=(), writes=()):
        o = self._mk(eng, reads, writes)
        o.fn = fn
        rec = _Rec(eng)
        fn(rec)
        o.dur = rec.t
        o.issue = o.dur
        return o.id

    def dma(self, q, out, in_, slot, reads=(), writes=(), cont=False):
        o = self._mk(q, reads, writes)
        o.dma = (out, in_)
        o.slot = slot
        assert slot.queue in (None, q)
        slot.queue = q
        if slot.last is not None:
            if cont:
                o.deps.update(self.ops[slot.last].deps)
                o.deps.discard(o.id)
            else:
                o.deps.add(slot.last)
        slot.last = o.id
        n = 1
        for d in list(out.shape):
            n *= d
        nbytes = n * (4 if in_.dtype == F32 else 2)
        o.issue = 0.06 if q == "sp" else 1.0
        o.dur = nbytes / 350e3 + 0.2
        return o.id

    def schedule(self, window=600):
        ops = self.ops
        n = len(ops)
        for o in ops:
            o.succ = []
            o.nleft = 0
            o.ready = 0.0
        for o in ops:
            o.deps.discard(o.id)
            for d in o.deps:
                ops[d].succ.append(o.id)
            o.nleft = len(o.deps)
        blev = [0.0] * n
        for i in range(n - 1, -1, -1):
            o = ops[i]
            m = 0.0
            for s_ in o.succ:
                if blev[s_] > m:
                    m = blev[s_]
            blev[i] = m + (o.dur if o.dma is None else o.dur + 1.8) + HOP_US
        eng_free = {nm: 0.0 for nm in self.names}
        dma_free = 0.0
        order = {nm: [] for nm in self.names}
        done = [False] * n
        ready = [o.id for o in ops if o.nleft == 0]
        head = 0
        fin = [0.0] * n
        nsched = 0
        while nsched < n:
            best = None
            bk = None
            lim = head + window
            cands = []
            for i in ready:
                if i >= lim:
                    continue
                o = ops[i]
                st = max(eng_free[o.stream], o.ready)
                cands.append((st, i))
                if bk is None or (st, i) < bk:
                    bk = (st, i)
                    best = i
            if best is not None:
                lim_t = bk[0] + SLACK_US
                bb = None
                for st, i in cands:
                    if st <= lim_t and (bb is None or blev[i] > blev[bb[1]]):
                        bb = (st, i)
                bk, best = bb, bb[1]
            if best is None:
                best = min(ready)
                o = ops[best]
                bk = (max(eng_free[o.stream], o.ready), best)
            o = ops[best]
            ready.remove(best)
            st = bk[0]
            if o.dma is None:
                f = st + o.dur
                eng_free[o.stream] = f
            else:
                eng_free[o.stream] = st + o.issue
                f0 = max(st + o.issue, dma_free) + o.dur
                dma_free = f0
                f = f0 + 1.8
            fin[best] = f
            done[best] = True
            nsched += 1
            order[o.stream].append(best)
            for s_ in o.succ:
                so = ops[s_]
                so.nleft -= 1
                fl = f + (SAME_HOP_US if (so.stream == o.stream and o.stream not in ("pe", "sp")) else (0.0 if so.stream == o.stream else HOP_US))
                if so.ready < fl:
                    so.ready = fl
                if so.nleft == 0:
                    ready.append(s_)
            while head < n and done[head]:
                head += 1
        self.order = order
        self.est_total = max(fin) if fin else 0.0

    def emit(self, window=600):
        self.schedule(window)
        ops = self.ops
        nc = self.nc
        for nm in self.names:
            cnt = 0
            for i in self.order[nm]:
                o = ops[i]
                if o.dma is None:
                    cnt += 1
                    o.ev = (self.sems[nm], cnt)
            setattr(self, "cnt_" + nm, cnt)
        slot_cnt = {}
        for nm in self.names:
            for i in self.order[nm]:
                o = ops[i]
                if o.dma is not None:
                    c = slot_cnt.get(id(o.slot), 0) + 1
                    slot_cnt[id(o.slot)] = c
                    o.ev = (o.slot.sem, 16 * c)
        hmap = {"pe": "tensor", "act": "scalar", "dve": "vector", "pool": "gpsimd", "sp": "sync"}
        streams = {}
        for nm in self.names:
            ins = []
            waited = {}
            for i in self.order[nm]:
                o = ops[i]
                need = {}
                for d in o.deps:
                    sem, v = ops[d].ev
                    if sem is self.sems[nm] and (nm in self.in_order_safe or not SAME_ENGINE_SYNC) and ops[d].dma is None:
                        continue
                    if need.get(id(sem), (None, 0))[1] < v:
                        need[id(sem)] = (sem, v)
                for sem, v in need.values():
                    if waited.get(id(sem), 0) >= v:
                        continue
                    waited[id(sem)] = v
                    ins.append(("wait", sem, v))
                if o.dma is None:
                    ins.append(("op", o.fn))
                else:
                    ins.append(("dma", o.dma[0], o.dma[1], o.slot.sem))
            if nm == "sp":
                for sl in self.slots:
                    c = slot_cnt.get(id(sl), 0)
                    if c and waited.get(id(sl.sem), 0) < 16 * c:
                        ins.append(("wait", sl.sem, 16 * c))
                for o2 in self.names:
                    c = getattr(self, "cnt_" + o2)
                    if o2 != "sp" and c:
                        ins.append(("wait", self.sems[o2], c))
            streams[nm] = ins
        with nc.Block() as block:
            for nm in self.names:
                ins = streams[nm]
                if not ins:
                    continue

                def body(e, ins=ins, sem=self.sems[nm]):
                    for it in ins:
                        k = it[0]
                        if k == "wait":
                            e.wait_ge(it[1], it[2])
                        elif k == "op":
                            last = it[1](e)
                            last.then_inc(sem, 1)
                        else:
                            e.dma_start(out=it[1], in_=it[2]).then_inc(it[3], 16)

                getattr(block, hmap[nm])(body)
        self.es.close()


def V(t, p0, pn, off, dims):
    Fsz = 1
    for s in t.shape[1:]:
        Fsz *= s
    return bass.AP(t, p0 * Fsz + off, [[Fsz, pn]] + [[s, n] for s, n in dims])


def DA(h, off, dims):
    return bass.AP(h, off, [[s, n] for s, n in dims])


def t5_bucket_np(rel):
    n = -rel
    half, max_exact = 16, 8
    ret = np.where(n < 0, half, 0)
    n = np.abs(n)
    nf = np.maximum(n, 1).astype(np.float32)
    large = max_exact + (np.log(nf / np.float32(max_exact)) / np.float32(math.log(128 / max_exact))
                         * np.float32(half - max_exact)).astype(np.int32)
    large = np.minimum(large, half - 1)
    return ret + np.where(n < max_exact, n, large)


def onehot_table():
    q = np.arange(64)[:, None]
    oh = np.zeros((32, 3, 64, 64), np.float32)
    for blk in range(3):
        s = np.arange(64)[None, :] + 64 * blk
        bk = t5_bucket_np(s - 128 - q)
        for b in range(32):
            oh[b, blk] = (bk == b)
    return oh.reshape(32, 3 * 64 * 64)


def build_program(NPS, PLEN, NSS, DEPTH=2):
    NSEQ = NPS + NSS
    NPT = PLEN // NT
    assert NSS * 64 == NT or NSS == 0
    nc = bass.Bass("TRN2", target_bir_lowering=False)

    def din(name, shape):
        return nc.dram_tensor(name, list(shape), F32, kind="ExternalInput")

    def dout(name, shape):
        return nc.dram_tensor(name, list(shape), F32, kind="ExternalOutput")

    xp = din("xp", [NPS * PLEN, D]); xs = din("xs", [max(NSS, 1) * 64, D])
    ck = din("ck", [DEPTH, max(NSS, 1), 128, 256]); cv = din("cv", [DEPTH, max(NSS, 1), 128, 256])
    sconv = din("sconv", [DEPTH, max(NSS, 1) * 3, 3072]); sssm = din("sssm", [DEPTH, max(NSS, 1), 2048, 128])
    cvec = din("cvec", [NSEQ, D])
    rel_bias = din("rel_bias", [32, 16]); onehot = din("onehot", [32, 3 * 4096])
    ada_w = din("ada_w", [DEPTH, D, 6 * D]); ada_b = din("ada_b", [DEPTH, 6 * D])
    norm_mix_g = din("norm_mix_g", [DEPTH, D]); norm_ffn_g = din("norm_ffn_g", [DEPTH, D])
    w_in = din("w_in", [DEPTH, D, INC]); conv_w = din("conv_w", [DEPTH, 4, 3072]); conv_b = din("conv_b", [DEPTH, 3072])
    dt_bias = din("dt_bias", [DEPTH, 32]); a_log = din("a_log", [DEPTH, 32]); d_skip = din("d_skip", [DEPTH, 32])
    ssm_norm_g = din("ssm_norm_g", [DEPTH, 2048]); q_norm_g = din("q_norm_g", [DEPTH, 64]); k_norm_g = din("k_norm_g", [DEPTH, 64])
    sinks = din("sinks", [DEPTH, 16])
    w_br_ssm = din("w_br_ssm", [DEPTH, 2048, D]); w_br_attn = din("w_br_attn", [DEPTH, D, D]); w_out = din("w_out", [DEPTH, D, D])
    w_gate_up = din("w_gate_up", [DEPTH, D, 2 * DFF]); w_down = din("w_down", [DEPTH, DFF, D])
    yp = dout("yp", [NPS * PLEN, D]); ys = dout("ys", [max(NSS, 1) * 64, D])
    ncp = dout("ncp", [DEPTH, NPS, 3, 3072]); ncs = dout("ncs", [DEPTH, max(NSS, 1) * 3, 3072])
    nsp = dout("nsp", [DEPTH, NPS, 2048, 128]); nss = dout("nss", [DEPTH, max(NSS, 1), 2048, 128])
    nkp = dout("nkp", [DEPTH, NPS, 128, 256]); nks = dout("nks", [DEPTH, max(NSS, 1), 64, 256])
    nvp = dout("nvp", [DEPTH, NPS, 128, 256]); nvs = dout("nvs", [DEPTH, max(NSS, 1), 64, 256])

    P = Prog(nc)
    toks = {}

    def T(name):
        if name not in toks:
            toks[name] = Tok(name)
        return toks[name]

    def TL(name, n):
        return [T(f"{name}{i}") for i in range(n)]

    PSF = P.psum("psf", [128, 4096], F32)
    PSB = PSF.bitcast(BF16)
    pstok = TL("psb", 8)
    for t_ in pstok:
        t_.excl = True
    ps_live = {}
    ps_rr = {1: 0, 2: 0, 4: 0}

    def ps_alloc(n, nreads):
        slots_ = list(range(0, 8, n))
        k = len(slots_)
        for t_ in range(k):
            b = slots_[(ps_rr[n] + t_) % k]
            if all((b + i) not in ps_live for i in range(n)):
                ps_rr[n] = (ps_rr[n] + t_ + 1) % k
                a = {"banks": list(range(b, b + n)), "rem": nreads}
                for i in range(n):
                    ps_live[b + i] = a
                return b, pstok[b:b + n]
        raise RuntimeError(f"PSUM exhausted (need {n}); live={sorted(ps_live)}")

    def ps_single(nreads=1):
        return ps_alloc(1, nreads)

    def ps_pair(nreads=1):
        return ps_alloc(2, nreads)

    def ps_quad(nreads=1):
        return ps_alloc(4, nreads)

    _orig_op = P.op
    pstok_ids = {id(t): i for i, t in enumerate(pstok)}

    def _op(eng, fn, reads=(), writes=()):
        ev = _orig_op(eng, fn, reads=reads, writes=writes)
        seen = []
        for t in reads:
            bi = pstok_ids.get(id(t))
            if bi is None or bi not in ps_live:
                continue
            a = ps_live[bi]
            if any(a is x for x in seen):
                continue
            seen.append(a)
            a["rem"] -= 1
            if a["rem"] == 0:
                for bb in a["banks"]:
                    del ps_live[bb]
        return ev
    P.op = _op

    def pf(b, pn, off, dims, p0=0):
        return V(PSF, p0, pn, b * 512 + off, dims)

    def pb(b, pn, off, dims, p0=0):
        return V(PSB, p0, pn, b * 1024 + off, dims)

    def sb(name, f, dt=F32):
        return P.sbuf(name, [128, f], dt)

    identf = sb("identf", 128); identb = sb("identb", 128, BF16)
    onesb = sb("onesb", 128, BF16); onesf = sb("onesf", 128)
    Umat = sb("Umat", 64); SUmat = sb("SUmat", 64); Umb = sb("Umb", 64, BF16)
    CONST = sb("CONST", 4)
    xT = sb("xT", KD * NT)
    hT = sb("hT", KD * NT, BF16)
    RSTD = sb("RSTD", NT)
    ZT = sb("ZT", 16 * NT, BF16); QTr = sb("QTr", 8 * NT, BF16)
    PH1 = sb("PH1", 24 * NT, BF16)
    XC = PH1; BT_OFF = 16 * NT; CT_OFF = 20 * NT
    MIXT = ZT
    ACTT = PH1
    CS = [sb(f"CS{i}", 4 * 67) for i in range(2)]
    CACC = [sb(f"CACC{i}", NT) for i in range(2)]
    CIN = sb("CIN", DEPTH * 24 * 12)
    KT = sb("KT", DEPTH * 4 * 6 * 64, BF16)
    VBf = sb("VB", DEPTH * 6 * 4 * 65, BF16)
    KP = sb("KP", 4 * 128, BF16)
    VP = sb("VP", 2 * 4 * 65, BF16)
    DTX = sb("DTX", 128); DTA = sb("DTA", 128); DTE = sb("DTE", 128); DTv = sb("DTv", 128); DAv = sb("DAv", 128)
    R1 = sb("R1", 2048)
    NTMP = R1
    XDTS = R1.bitcast(BF16)
    DEC = sb("DEC", 2048, BF16)
    SQ = DEC
    CBM = sb("CBM", 256, BF16)
    XTs = sb("XTs", 2048, BF16); XDT = sb("XDT", 2048, BF16)
    YN = XDT
    BTOK = sb("BTOK", 512, BF16)
    Tb = sb("Tb", 2048)
    SQQ = sb("SQQ", 1024, BF16)
    EXs = [sb(f"EX{i}", 96) for i in range(2)]
    SS4 = sb("SS4", 4); RS4 = sb("RS4", 4)
    ST = sb("ST", DEPTH * 2048); STB = sb("STB", 2048, BF16)
    YT = sb("YT", 16 * NT, BF16); OT = sb("OT", 8 * NT, BF16)
    KO = [sb(f"KO{i}", 256) for i in range(1)]; VO = [sb(f"VO{i}", 256) for i in range(1)]
    KNB = sb("KNB", 256, BF16); SQK = sb("SQK", 256); SSK = sb("SSK", 4); RK = sb("RK", 4)
    SSQ = sb("SSQ", 16); RQ = sb("RQ", 16)
    QN = sb("QN", 1024, BF16); QTc = sb("QTc", 1024, BF16)
    ON = QN
    PT = [sb(f"PT{i}", 1024, BF16) for i in range(1)]
    DEN = sb("DEN", 16); RDEN = sb("RDEN", 16)
    BIAS = sb("BIAS", 3 * 1024, BF16)
    GSs = sb("GSs", NT); GAs = sb("GAs", NT); MXs = sb("MXs", NT)
    SGs = [sb(f"SGs{i}", NT) for i in range(2)]
    NWB = 4
    WB = [sb(f"WB{i}", 4096, BF16) for i in range(NWB)]
    WBS = sb("WBS", 4096, BF16)
    WDT = sb("WDT", DEPTH * 8 * 32, BF16)
    SCR = sb("SCR", 2048)
    ROWS = SCR; CROW = SCR
    MODT = sb("MODT", DEPTH * 48 * NSEQ)
    GMUL = sb("GMUL", DEPTH * 2 * 8 * NSEQ)
    ADAB = sb("ADAB", DEPTH * 48)
    GN = sb("GN", DEPTH * 8 * 2)
    CWB = sb("CWB", DEPTH * 24 * 5)
    GSSM = sb("GSSM", DEPTH * 16)
    DTB = sb("DTB", DEPTH * 32); ABC = sb("ABC", DEPTH * 32); DSK = sb("DSK", DEPTH * 32)
    QG8 = sb("QG8", DEPTH * 64); KG = sb("KG", DEPTH * 64); ESINK = sb("ESINK", DEPTH * 16)
    DI = sb("DI", 32 * 64, BF16)
    RELB = sb("RELB", 16, BF16); RELF = sb("RELF", 16)
    SCT = sb("SCT", 8 * NSEQ, BF16)

    t_const = T("const")
    wtok = TL("wb", NWB)
    wslot = [P.slot(f"wb{i}") for i in range(NWB)]
    wctr = [0]
    s_in = P.slot("in"); s_in2 = P.slot("in2"); s_pool = P.slot("poolmisc")
    s_out = [P.slot(f"out{i}") for i in range(4)]
    octr = [0]

    def out_slot():
        s = s_out[octr[0] % 4]; octr[0] += 1
        return s

    NGRP = 96
    wscr = nc.dram_tensor("wscr", [NGRP * 128, 4096], BF16, kind="Internal")
    wkeys = {}
    wslot_sp = [P.slot(f"wbsp{i}") for i in range(NWB)]
    wsave_slot = [P.slot(f"wsave{i}") for i in range(2)]
    wsave_ctr = [0]

    def _wfetch(key, buf, tok, slot_pool, slot_sp, parts):
        if key is None or key not in wkeys:
            for pi_, (doff, ddims, sap) in enumerate(parts):
                P.dma("pool", V(buf, 0, 128, doff, ddims), sap, slot_pool, writes=[tok], cont=(pi_ > 0))
            if key is not None:
                gi = len(wkeys)
                assert gi < NGRP
                wkeys[key] = gi
                ss = wsave_slot[wsave_ctr[0] % 2]; wsave_ctr[0] += 1
                P.dma("sp", DA(wscr, gi * 128 * 4096, [(4096, 128), (1, 4096)]), V(buf, 0, 128, 0, [(1, 4096)]), ss, reads=[tok])
        else:
            gi = wkeys[key]
            P.dma("sp", V(buf, 0, 128, 0, [(1, 4096)]), DA(wscr, gi * 128 * 4096, [(4096, 128), (1, 4096)]), slot_sp, writes=[tok])

    def wload(dst_dims, src_ap, key=None):
        i = wctr[0] % NWB; wctr[0] += 1
        _wfetch(key, WB[i], wtok[i], wslot[i], wslot_sp[i], [(0, dst_dims, src_ap)])
        return WB[i], wtok[i]

    wbs_slot = P.slot("wbs"); wbs_slot_sp = P.slot("wbssp")

    def wload_bs(l, col0):
        parts = [(part * 8 * 256, [(256, 8), (1, 256)], DA(w_br_ssm, l * 2048 * D + part * 8 * 128 * D + col0, [(D, 128), (128 * D, 8), (1, 256)])) for part in range(2)]
        _wfetch(("bs", l, col0), WBS, T("WBS"), wbs_slot, wbs_slot_sp, parts)
        return WBS, T("WBS")

    def wload_down(l, j):
        i = wctr[0] % NWB; wctr[0] += 1
        parts = []
        k0 = 0
        for nk in (8, 8, 6):
            parts.append((k0 * 128, [(128, nk), (1, 128)], DA(w_down, l * DFF * D + k0 * 128 * D + j * 128, [(D, 128), (128 * D, nk), (1, 128)])))
            k0 += nk
        _wfetch(("down", l, j), WB[i], wtok[i], wslot[i], wslot_sp[i], parts)
        return WB[i], wtok[i]

    cw_ = [t_const, T("VBall"), T("VPall"), T("CIN0"), T("CIN1")]
    for fn in [
        lambda e: e.memset(identf[:], 0.0),
        lambda e: e.affine_select(out=identf[:], in_=identf[:], pattern=[[-1, 128]], compare_op=ALU.not_equal, fill=1.0, base=0, channel_multiplier=1),
        lambda e: e.memset(onesf[:], 1.0),
        lambda e: e.memset(Umat[0:64, :], 1.0),
        lambda e: e.affine_select(out=Umat[0:64, :], in_=Umat[0:64, :], pattern=[[1, 64]], compare_op=ALU.is_ge, fill=0.0, base=0, channel_multiplier=-1),
        lambda e: e.memset(SUmat[0:64, :], 1.0),
        lambda e: e.affine_select(out=SUmat[0:64, :], in_=SUmat[0:64, :], pattern=[[-1, 64]], compare_op=ALU.is_gt, fill=0.0, base=0, channel_multiplier=1),
        lambda e: e.memset(CONST[:, 0:1], EPS),
        lambda e: e.memset(CONST[:, 1:2], 1.0),
        lambda e: e.memset(CONST[:, 2:3], 0.0),
        lambda e: e.memset(VBf[:], 1.0),
        lambda e: e.memset(VP[:], 1.0),
        lambda e: e.memset(CIN[:], 0.0),
        lambda e: e.tensor_copy(out=onesb[:], in_=onesf[:]),
        lambda e: e.tensor_copy(out=Umb[0:64, :], in_=Umat[0:64, :]),
        lambda e: e.tensor_copy(out=identb[:], in_=identf[:]),
    ]:
        P.op("pool", fn, writes=cw_)
    ktok = [TL(f"KT{l}_", 6) for l in range(DEPTH)]
    vtok = [TL(f"VB{l}_", 6) for l in range(DEPTH)]
    for l in range(DEPTH):
        for t in vtok[l]:
            t.w = T("VBall").w
    T("VP").w = T("VPall").w

    t_rows = T("SCR")

    def rows_to_fm(src_ap_rows, R, C, dst_fn, extra_reads=()):
        assert R == 1
        h, off0 = src_ap_rows
        for c0 in range(0, C, 2048):
            cw = min(2048, C - c0)
            P.dma("sp", V(ROWS, 0, 1, 0, [(1, cw)]), DA(h, off0 + c0, [(C, 1), (1, cw)]), s_in, writes=[t_rows])
            nj = cw // 128
            b, bt = ps_single()

            def tr(e, nj=nj, b=b):
                for jj in range(nj):
                    last = e.transpose(pf(b, 128, jj, [(1, 1)]), V(ROWS, 0, 1, jj * 128, [(1, 128)]), V(identf, 0, 1, 0, [(1, 1)]))
                return last
            P.op("pe", tr, reads=[t_rows, t_const], writes=bt)
            dst, dtoks = dst_fn(c0 // 128, nj)
            P.op("dve", lambda e, b=b, nj=nj, dst=dst: e.tensor_copy(out=dst, in_=pf(b, 128, 0, [(1, nj), (1, 1)])), reads=bt, writes=dtoks)

    def bc_load(dst_ap, h, off, n, tok):
        P.dma("sp", dst_ap, DA(h, off, [(0, 64), (1, n)]), s_in2, writes=[tok])

    t_lc = T("layerconst")
    for l in range(DEPTH):
        for j0 in range(0, 24, 12):
            P.dma("sp", V(ROWS, 0, 4, 0, [(1, 1536)]), DA(conv_w, l * 4 * 3072 + j0 * 128, [(3072, 4), (1, 1536)]), s_in, writes=[t_rows])
            P.dma("sp", V(ROWS, 4, 1, 0, [(1, 1536)]), DA(conv_b, l * 3072 + j0 * 128, [(3072, 1), (1, 1536)]), s_in, writes=[t_rows])
            b, bt = ps_single()

            def tr(e, j0=j0, b=b):
                for jj in range(12):
                    last = e.transpose(pf(b, 128, jj * 5, [(1, 5)]), V(ROWS, 0, 5, jj * 128, [(1, 128)]), V(identf, 0, 5, 0, [(1, 5)]))
                return last
            P.op("pe", tr, reads=[t_rows, t_const], writes=bt)
            P.op("dve", lambda e, b=b, j0=j0, l=l: e.tensor_copy(out=V(CWB, 0, 128, l * 120 + j0 * 5, [(1, 60)]), in_=pf(b, 128, 0, [(1, 60)])), reads=bt, writes=[t_lc])
        rows_to_fm((norm_mix_g, l * D), 1, D, lambda j0, nj, l=l: (V(GN, 0, 128, l * 16 + j0 * 2, [(2, nj), (1, 1)]), [t_lc]))
        rows_to_fm((norm_ffn_g, l * D), 1, D, lambda j0, nj, l=l: (V(GN, 0, 128, l * 16 + j0 * 2 + 1, [(2, nj), (1, 1)]), [t_lc]))
        rows_to_fm((ssm_norm_g, l * 2048), 1, 2048, lambda j0, nj, l=l: (V(GSSM, 0, 128, l * 16 + j0, [(1, nj), (1, 1)]), [t_lc]))
        rows_to_fm((ada_b, l * 6144), 1, 6144, lambda j0, nj, l=l: (V(ADAB, 0, 128, l * 48 + j0, [(1, nj), (1, 1)]), [t_lc]))
        bc_load(V(DTB, 0, 64, l * 32, [(1, 32)]), dt_bias, l * 32, 32, t_lc)
        bc_load(V(ABC, 0, 64, l * 32, [(1, 32)]), a_log, l * 32, 32, t_lc)
        bc_load(V(DSK, 0, 64, l * 32, [(1, 32)]), d_skip, l * 32, 32, t_lc)
        bc_load(V(QG8, 0, 64, l * 64, [(1, 64)]), q_norm_g, l * 64, 64, t_lc)
        bc_load(V(KG, 0, 64, l * 64, [(1, 64)]), k_norm_g, l * 64, 64, t_lc)
        bc_load(V(ESINK, 0, 64, l * 16, [(1, 16)]), sinks, l * 16, 16, t_lc)
        P.dma("pool", V(WDT, 0, 128, l * 256, [(32, 8), (1, 32)]), DA(w_in, l * D * INC + C_DT, [(INC, 128), (128 * INC, 8), (1, 32)]), s_pool, writes=[t_lc])
    P.op("act", lambda e: e.activation(out=V(ABC, 0, 64, 0, [(1, DEPTH * 32)]), in_=V(ABC, 0, 64, 0, [(1, DEPTH * 32)]), func=AF.Exp), reads=[t_lc], writes=[t_lc])
    P.op("dve", lambda e: e.tensor_scalar(out=V(ABC, 0, 64, 0, [(1, DEPTH * 32)]), in0=V(ABC, 0, 64, 0, [(1, DEPTH * 32)]), scalar1=-1.0, scalar2=None, op0=ALU.mult), reads=[t_lc], writes=[t_lc])
    P.op("act", lambda e: e.activation(out=V(ESINK, 0, 64, 0, [(1, DEPTH * 16)]), in_=V(ESINK, 0, 64, 0, [(1, DEPTH * 16)]), func=AF.Exp), reads=[t_lc], writes=[t_lc])
    P.op("dve", lambda e: e.tensor_scalar(out=V(QG8, 0, 64, 0, [(1, DEPTH * 64)]), in0=V(QG8, 0, 64, 0, [(1, DEPTH * 64)]), scalar1=0.125, scalar2=None, op0=ALU.mult), reads=[t_lc], writes=[t_lc])
    P.dma("sp", V(RELF, 0, 32, 0, [(1, 16)]), DA(rel_bias, 0, [(16, 32), (1, 16)]), s_in2, writes=[T("relf")])
    P.op("dve", lambda e: e.tensor_copy(out=V(RELB, 0, 32, 0, [(1, 16)]), in_=V(RELF, 0, 32, 0, [(1, 16)])), reads=[T("relf")], writes=[T("relb")])
    for blk in range(3):
        i = wctr[0] % NWB; wctr[0] += 1
        P.dma("pool", V(WB[i], 0, 32, 0, [(1, 4096)]), DA(onehot, blk * 4096, [(3 * 4096, 32), (1, 4096)]), wslot[i], writes=[wtok[i]])
        b, bt = ps_pair()

        def ohmm(e, i=i, b=b):
            for q in range(64):
                last = e.matmul(pf(b, 64, q * 16, [(1, 16)]), lhsT=V(WB[i], 0, 32, q * 64, [(1, 64)]), rhs=V(RELB, 0, 32, 0, [(1, 16)]), start=True, stop=True)
            return last
        P.op("pe", ohmm, reads=[wtok[i], T("relb")], writes=bt)
        P.op("dve", lambda e, b=b, blk=blk: e.tensor_copy(out=V(BIAS, 0, 64, blk * 1024, [(64, 16), (1, 64)]), in_=pf(b, 64, 0, [(1, 16), (16, 64)])), reads=bt, writes=[T("bias")])
    t_scr = T("SCR")
    P.dma("sp", V(CROW, 0, NSEQ, 0, [(1, 1024)]), DA(cvec, 0, [(D, NSEQ), (1, D)]), s_in, writes=[T("SCR")])
    P.op("act", lambda e: e.activation(out=V(CROW, 0, NSEQ, 0, [(1, 1024)]), in_=V(CROW, 0, NSEQ, 0, [(1, 1024)]), func=AF.Silu), reads=[T("SCR")], writes=[T("SCR")])
    b, bt = ps_single()

    def trc(e, b=b):
        for k in range(8):
            last = e.transpose(pf(b, 128, k * NSEQ, [(1, NSEQ)]), V(CROW, 0, NSEQ, k * 128, [(1, 128)]), V(identf, 0, NSEQ, 0, [(1, NSEQ)]))
        return last
    P.op("pe", trc, reads=[T("SCR"), t_const], writes=bt)
    P.op("dve", lambda e, b=b: e.tensor_copy(out=V(SCT, 0, 128, 0, [(1, 8 * NSEQ)]), in_=pf(b, 128, 0, [(1, 8 * NSEQ)])), reads=bt, writes=[T("sct")])
    for l in range(DEPTH):
        b, bt = ps_single()
        for g in range(12):
            wbuf, wt = wload([(512, 8), (1, 512)], DA(ada_w, l * D * 6144 + g * 512, [(6144, 128), (128 * 6144, 8), (1, 512)]))

            def adamm(e, wbuf=wbuf, g=g, b=b):
                for jj in range(4):
                    j = g * 4 + jj
                    for k in range(8):
                        last = e.matmul(pf(b, 128, j * NSEQ, [(1, NSEQ)]), lhsT=V(wbuf, 0, 128, k * 512 + jj * 128, [(1, 128)]),
                                        rhs=V(SCT, 0, 128, k * NSEQ, [(1, NSEQ)]), start=(k == 0), stop=(k == 7))
                return last
            P.op("pe", adamm, reads=[wt, T("sct")], writes=bt)
        P.op("dve", lambda e, b=b, l=l: e.tensor_tensor(out=V(MODT, 0, 128, l * 48 * NSEQ, [(NSEQ, 48), (1, NSEQ)]), in0=pf(b, 128, 0, [(NSEQ, 48), (1, NSEQ)]),
                                                         in1=V(ADAB, 0, 128, l * 48, [(1, 48), (0, NSEQ)]), op=ALU.add), reads=bt + [t_lc], writes=[T("modt")])
        for mf in range(2):
            sc0 = 8 if mf == 0 else 32
            P.op("dve", lambda e, l=l, mf=mf, sc0=sc0: e.scalar_tensor_tensor(
                out=V(GMUL, 0, 128, (l * 2 + mf) * 8 * NSEQ, [(NSEQ, 8), (1, NSEQ)]),
                in0=V(MODT, 0, 128, (l * 48 + sc0) * NSEQ, [(NSEQ, 8), (1, NSEQ)]), scalar=1.0,
                in1=V(GN, 0, 128, l * 16 + mf, [(2, 8), (0, NSEQ)]), op0=ALU.add, op1=ALU.mult), reads=[T("modt"), t_lc], writes=[T("gmul")])

    def modv(l, which, k, seq):
        return V(MODT, 0, 128, (l * 48 + which * 8 + k) * NSEQ + seq, [(1, 1)])

    cpar = [0]
    exctr = [0]
    uctr = [0]
    import os as _os
    _kstop = int(_os.environ.get("K_STOP", "-1"))
    _kctr = [0]

    class StopBuild(Exception):
        pass

    def ckpt(name):
        if _kstop >= 0:
            print("CKPT", _kctr[0], name, flush=True)
            if _kctr[0] == _kstop:
                raise StopBuild()
        _kctr[0] += 1

    def acttok(j):
        return T(f"XC{j}") if j < 16 else (T(f"BT{j - 16}") if j < 20 else T(f"CT{j - 20}"))

    def do_tile(kind, seq, kt):
        if kind == "p":
            nseg, L, seq0 = 1, NT, seq
            xsrc, ydst, row0 = xp, yp, seq * PLEN + kt * NT
            last_tile = (kt == NPT - 1)
        else:
            nseg, L, seq0 = NSS, 64, NPS
            xsrc, ydst, row0 = xs, ys, 0
            last_tile = True
        txT = TL("xT", 8)
        for sub in range(2):
            P.dma("sp", V(SCR, 0, 128, 0, [(1, 1024)]), DA(xsrc, (row0 + sub * 128) * D, [(D, 128), (1, D)]), s_in, writes=[t_scr])
            b, bt = ps_pair()

            def trx(e, b=b):
                for k in range(8):
                    last = e.transpose(pf(b, 128, k * 128, [(1, 128)]), V(SCR, 0, 128, k * 128, [(1, 128)]), identf[:, :])
                return last
            P.op("pe", trx, reads=[t_scr, t_const], writes=bt)
            P.op("dve", lambda e, b=b, sub=sub: e.tensor_copy(out=V(xT, 0, 128, sub * 128, [(NT, 8), (1, 128)]), in_=pf(b, 128, 0, [(128, 8), (1, 128)])), reads=bt, writes=txT)

        ckpt("xload")

        def norm_to_hT(l, mf):
            P.op("act", lambda e: e.activation(out=SQ[:, :], in_=xT[:, :], func=AF.Square), reads=txT, writes=[T("DEC_0"), T("DEC_1")])
            b, bt = ps_single()

            def ssmm(e, b=b):
                for k in range(8):
                    last = e.matmul(pf(b, 128, 0, [(1, NT)]), lhsT=onesb[:, :], rhs=V(SQ, 0, 128, k * NT, [(1, NT)]), start=(k == 0), stop=(k == 7))
                return last
            P.op("pe", ssmm, reads=[T("DEC_0"), T("DEC_1"), t_const], writes=bt)
            P.op("act", lambda e, b=b: e.activation(out=RSTD[:, :], in_=pf(b, 128, 0, [(1, NT)]), func=AF.Sqrt, bias=CONST[:, 0:1], scale=1.0 / D), reads=bt + [t_const], writes=[T("RSTD")])
            P.op("dve", lambda e: e.reciprocal(out=RSTD[:, :], in_=RSTD[:, :]), reads=[T("RSTD")], writes=[T("RSTD")])
            for k in range(8):
                tr1 = T(f"R1_{k // 4}")
                for sg in range(nseg):
                    o_ = k * NT + sg * L
                    P.op("dve", lambda e, k=k, sg=sg, o_=o_: e.scalar_tensor_tensor(
                        out=V(NTMP, 0, 128, o_, [(1, L)]), in0=V(xT, 0, 128, o_, [(1, L)]),
                        scalar=V(GMUL, 0, 128, ((l * 2 + mf) * 8 + k) * NSEQ + seq0 + sg, [(1, 1)]),
                        in1=V(RSTD, 0, 128, sg * L, [(1, L)]), op0=ALU.mult, op1=ALU.mult),
                        reads=[txT[k], T("RSTD"), T("gmul")], writes=[tr1])
                    P.op("act", lambda e, k=k, sg=sg, o_=o_: e.activation(
                        out=V(hT, 0, 128, o_, [(1, L)]), in_=V(NTMP, 0, 128, o_, [(1, L)]), func=AF.Identity,
                        bias=modv(l, 0 if mf == 0 else 3, k, seq0 + sg), scale=1.0),
                        reads=[tr1, T("modt")], writes=[T(f"hT{k}")])

        def fm_proj(wbuf, wt, kc, col0, rhs_t, rhs_tok, nk, nreads=1):
            b, bt = ps_single(nreads)

            def mm(e, b=b):
                for k in range(nk):
                    last = e.matmul(pf(b, 128, 0, [(1, NT)]), lhsT=V(wbuf, 0, 128, k * kc + col0, [(1, 128)]), rhs=V(rhs_t, 0, 128, k * NT, [(1, NT)]),
                                    start=(k == 0), stop=(k == nk - 1))
                return last
            P.op("pe", mm, reads=[wt] + rhs_tok, writes=bt)
            return b, bt

        for l in range(DEPTH):
            win0 = l * D * INC

            def win_ap(c0, w):
                return DA(w_in, win0 + c0, [(INC, 128), (128 * INC, 8), (1, w)])

            if kind == "p" and kt == 0:
                P.op("pool", lambda e, l=l: e.memset(V(CIN, 0, 128, l * 288, [(12, 24), (1, 3)]), 0.0), writes=[T(f"CIN{l}")])
            if kind == "s":
                R = 3 * NSS
                for j0 in range(0, 24, 12):
                    P.dma("sp", V(SCR, 0, R, 0, [(1, 1536)]), DA(sconv, l * R * 3072 + j0 * 128, [(3072, R), (1, 1536)]), s_in, writes=[t_scr])
                    b, bt = ps_single()

                    def trs(e, j0=j0, b=b, R=R):
                        for jj in range(12):
                            last = e.transpose(pf(b, 128, jj * R, [(1, R)]), V(SCR, 0, R, jj * 128, [(1, 128)]), V(identf, 0, R, 0, [(1, R)]))
                        return last
                    P.op("pe", trs, reads=[t_scr, t_const], writes=bt)
                    P.op("dve", lambda e, b=b, j0=j0, l=l, R=R: e.tensor_copy(out=V(CIN, 0, 128, l * 288 + j0 * 12, [(12, 12), (1, R)]), in_=pf(b, 128, 0, [(R, 12), (1, R)])),
                         reads=bt, writes=[T(f"CIN{l}")])
            P.op("dve", lambda e, l=l: e.tensor_tensor(out=V(DI, 0, 64, 0, [(64, 32), (1, 64)]), in0=V(DSK, 0, 64, l * 32, [(1, 32), (0, 64)]),
                                                        in1=V(identf, 0, 64, 0, [(0, 32), (1, 64)]), op=ALU.mult), reads=[t_lc, t_const], writes=[T("DI")])
            if kind == "p" and kt > 0:
                P.op("act", lambda e, l=l: e.activation(out=STB[:, :], in_=V(ST, 0, 128, l * 2048, [(1, 2048)]), func=AF.Copy), reads=[T(f"ST{l}_0"), T(f"ST{l}_1")], writes=[T("STB_0"), T("STB_1")])
            ckpt("layerstart")
            norm_to_hT(l, 0)
            thT = TL("hT", 8)
            ckpt("norm1")
            for g6 in range(6):
                wbuf, wt = wload([(512, 8), (1, 512)], win_ap(C_XBC + g6 * 512, 512), key=("xbc", l, g6))
                for jj in range(4):
                    j = g6 * 4 + jj
                    b, bt = fm_proj(wbuf, wt, 512, jj * 128, hT, thT, 8)
                    ci = cpar[0] % 2; cpar[0] += 1
                    S, ACC = CS[ci], CACC[ci]
                    tS, tA = T(f"CS{ci}"), T(f"CACC{ci}")
                    tcin = T(f"CIN{l}")
                    P.op("pool", lambda e, S=S, j=j, l=l: e.tensor_copy(out=V(S, 0, 128, 0, [(L + 3, nseg), (1, 3)]), in_=V(CIN, 0, 128, l * 288 + j * 12, [(3, nseg), (1, 3)])), reads=[tcin], writes=[tS])
                    P.op("act", lambda e, S=S, b=b: e.activation(out=V(S, 0, 128, 3, [(L + 3, nseg), (1, L)]), in_=pf(b, 128, 0, [(L, nseg), (1, L)]), func=AF.Copy), reads=bt, writes=[tS])
                    P.op("pool", lambda e, S=S, j=j, l=l: e.tensor_copy(out=V(CIN, 0, 128, l * 288 + j * 12, [(3, nseg), (1, 3)]), in_=V(S, 0, 128, L, [(L + 3, nseg), (1, 3)])), reads=[tS], writes=[tcin])

                    cwv = lambda k, j=j, l=l: V(CWB, 0, 128, l * 120 + j * 5 + k, [(1, 1)])
                    P.op("act", lambda e, S=S, ACC=ACC, cwv=cwv: e.activation(out=V(ACC, 0, 128, 0, [(L, nseg), (1, L)]), in_=V(S, 0, 128, 0, [(L + 3, nseg), (1, L)]),
                                                                       func=AF.Copy, scale=cwv(0)), reads=[tS, t_lc], writes=[tA])
                    for k in range(1, 4):
                        P.op("dve", lambda e, S=S, ACC=ACC, cwv=cwv, k=k: e.scalar_tensor_tensor(out=V(ACC, 0, 128, 0, [(L, nseg), (1, L)]), in0=V(S, 0, 128, k, [(L + 3, nseg), (1, L)]),
                                                                                       scalar=cwv(k), in1=V(ACC, 0, 128, 0, [(L, nseg), (1, L)]), op0=ALU.mult, op1=ALU.add),
                             reads=[tS, t_lc, tA], writes=[tA])
                    if j < 16:
                        dst, dtok = V(XC, 0, 128, j * NT, [(1, NT)]), T(f"XC{j}")
                    elif j < 20:
                        dst, dtok = V(PH1, 0, 128, BT_OFF + (j - 16) * NT, [(1, NT)]), T(f"BT{j - 16}")
                    else:
                        dst, dtok = V(PH1, 0, 128, CT_OFF + (j - 20) * NT, [(1, NT)]), T(f"CT{j - 20}")
                    P.op("act", lambda e, ACC=ACC, dst=dst, j=j, l=l: e.activation(out=dst, in_=ACC[:, :], func=AF.Silu, bias=V(CWB, 0, 128, l * 120 + j * 5 + 4, [(1, 1)]), scale=1.0),
                         reads=[tA, t_lc], writes=[dtok])
            ckpt("xbc")
            if last_tile:
                R = 3 * nseg
                for half in range(2):
                    b, bt = ps_quad()

                    def trc2(e, half=half, b=b, R=R, l=l):
                        for jj in range(12):
                            j = half * 12 + jj
                            last = e.transpose(pf(b, R, jj * 128, [(1, 128)]), V(CIN, 0, 128, l * 288 + j * 12, [(1, R)]), identf[:, :])
                        return last
                    P.op("pe", trc2, reads=[T(f"CIN{l}"), t_const], writes=bt)
                    P.op("act", lambda e, b=b, R=R: e.activation(out=V(SCR, 0, R, 0, [(1, 1536)]), in_=pf(b, R, 0, [(1, 1536)]), func=AF.Copy), reads=bt, writes=[t_scr])
                    if kind == "p":
                        dst = DA(ncp, ((l * NPS + seq) * 3) * 3072 + half * 1536, [(3072, 3), (1, 1536)])
                    else:
                        dst = DA(ncs, (l * 3 * NSS) * 3072 + half * 1536, [(3072, 3 * NSS), (1, 1536)])
                    P.dma("sp", dst, V(SCR, 0, R, 0, [(1, 1536)]), out_slot(), reads=[t_scr])
            ckpt("newconv")
            for g4 in range(4):
                wbuf, wt = wload([(512, 8), (1, 512)], win_ap(C_Z + g4 * 512, 512), key=("z", l, g4))
                for jj in range(4):
                    j = g4 * 4 + jj
                    b, bt = fm_proj(wbuf, wt, 512, jj * 128, hT, thT, 8)
                    P.op("act", lambda e, b=b, j=j: e.activation(out=V(ZT, 0, 128, j * NT, [(1, NT)]), in_=pf(b, 128, 0, [(1, NT)]), func=AF.Silu), reads=bt, writes=[T(f"ZT{j}")])
            for g2 in range(2):
                wbuf, wt = wload([(512, 8), (1, 512)], win_ap(C_Q + g2 * 512, 512), key=("q", l, g2))
                for jj in range(4):
                    j = g2 * 4 + jj
                    b, bt = fm_proj(wbuf, wt, 512, jj * 128, hT, thT, 8)
                    P.op("dve", lambda e, b=b, j=j: e.tensor_copy(out=V(QTr, 0, 128, j * NT, [(1, NT)]), in_=pf(b, 128, 0, [(1, NT)])), reads=bt, writes=[T(f"QTr{j}")])
            ckpt("zq")
            bdt, bdtt = ps_single()

            def dtmm(e, b=bdt, l=l):
                for c in range(NCH):
                    for k in range(8):
                        last = e.matmul(pf(b, 64, c * 32, [(1, 32)]), lhsT=V(hT, 0, 128, k * NT + c * 64, [(1, 64)]), rhs=V(WDT, 0, 128, l * 256 + k * 32, [(1, 32)]),
                                        start=(k == 0), stop=(k == 7))
                return last
            P.op("pe", dtmm, reads=thT + [t_lc], writes=bdtt)
            tdt = T("dtv")
            P.op("dve", lambda e, b=bdt, l=l: e.tensor_tensor(out=V(DTX, 0, 64, 0, [(32, NCH), (1, 32)]), in0=pf(b, 64, 0, [(32, NCH), (1, 32)]),
                                                               in1=V(DTB, 0, 64, l * 32, [(0, NCH), (1, 32)]), op=ALU.add), reads=bdtt + [t_lc], writes=[tdt])
            P.op("act", lambda e: e.activation(out=DTA[0:64, :], in_=DTX[0:64, :], func=AF.Abs), reads=[tdt], writes=[tdt])
            P.op("act", lambda e: e.activation(out=DTE[0:64, :], in_=DTA[0:64, :], func=AF.Exp, scale=-1.0), reads=[tdt], writes=[tdt])
            P.op("act", lambda e: e.activation(out=DTE[0:64, :], in_=DTE[0:64, :], func=AF.Ln, bias=CONST[0:64, 1:2], scale=1.0), reads=[tdt, t_const], writes=[tdt])
            P.op("dve", lambda e: e.scalar_tensor_tensor(out=DTv[0:64, :], in0=DTX[0:64, :], scalar=0.0, in1=DTE[0:64, :], op0=ALU.max, op1=ALU.add), reads=[tdt], writes=[tdt])
            P.op("dve", lambda e, l=l: e.tensor_tensor(out=V(DAv, 0, 64, 0, [(32, NCH), (1, 32)]), in0=V(DTv, 0, 64, 0, [(32, NCH), (1, 32)]),
                                                        in1=V(ABC, 0, 64, l * 32, [(0, NCH), (1, 32)]), op=ALU.mult), reads=[tdt, t_lc], writes=[tdt])
            ckpt("dt")
            wbuf, wt = wload([(512, 8), (1, 512)], win_ap(C_K, 512), key=("kv", l))
            for c in range(NCH):
                cidx = kt * NCH + c if kind == "p" else 0
                blk = 2 + c
                b, bt = ps_single(3)

                def kvmm(e, b=b, c=c, wbuf=wbuf):
                    for k in range(8):
                        last = e.matmul(pf(b, 64, 0, [(1, 512)]), lhsT=V(hT, 0, 128, k * NT + c * 64, [(1, 64)]), rhs=V(wbuf, 0, 128, k * 512, [(1, 512)]),
                                        start=(k == 0), stop=(k == 7))
                    return last
                P.op("pe", kvmm, reads=thT + [wt], writes=bt)
                oi = 0
                KOb, VOb = KO[oi], VO[oi]
                tko, tvo = T(f"KO{oi}"), T(f"VO{oi}")
                tkk = T("ktmp")
                P.op("act", lambda e, b=b: e.activation(out=SQK[0:64, :], in_=pf(b, 64, 0, [(1, 256)]), func=AF.Square), reads=bt, writes=[tkk])
                P.op("dve", lambda e: e.tensor_reduce(out=SSK[0:64, :], in_=V(SQK, 0, 64, 0, [(64, 4), (1, 64)]), axis=AX.X, op=ALU.add), reads=[tkk], writes=[tkk])
                P.op("act", lambda e: e.activation(out=RK[0:64, :], in_=SSK[0:64, :], func=AF.Sqrt, bias=CONST[0:64, 0:1], scale=1.0 / 64), reads=[tkk, t_const], writes=[tkk])
                P.op("dve", lambda e: e.reciprocal(out=RK[0:64, :], in_=RK[0:64, :]), reads=[tkk], writes=[tkk])
                P.op("dve", lambda e, b=b, KOb=KOb: e.tensor_tensor(out=V(KOb, 0, 64, 0, [(64, 4), (1, 64)]), in0=pf(b, 64, 0, [(64, 4), (1, 64)]),
                                                                     in1=V(RK, 0, 64, 0, [(1, 4), (0, 64)]), op=ALU.mult), reads=bt + [tkk], writes=[tko])
                P.op("dve", lambda e, KOb=KOb, l=l: e.tensor_tensor(out=V(KOb, 0, 64, 0, [(64, 4), (1, 64)]), in0=V(KOb, 0, 64, 0, [(64, 4), (1, 64)]),
                                                                     in1=V(KG, 0, 64, l * 64, [(0, 4), (1, 64)]), op=ALU.mult), reads=[tko, t_lc], writes=[tko])
                P.op("act", lambda e, KOb=KOb: e.activation(out=KNB[0:64, :], in_=KOb[0:64, :], func=AF.Copy), reads=[tko], writes=[T("KNB")])
                b2, bt2 = ps_single()

                def trkk(e, b2=b2):
                    for kv in range(4):
                        last = e.transpose(pb(b2, 64, kv * 64, [(1, 64)]), V(KNB, 0, 64, kv * 64, [(1, 64)]), V(identb, 0, 64, 0, [(1, 64)]))
                    return last
                P.op("pe", trkk, reads=[T("KNB"), t_const], writes=bt2)
                P.op("dve", lambda e, b2=b2, l=l, blk=blk: e.tensor_copy(out=V(KT, 0, 64, l * 1536 + blk * 64, [(384, 4), (1, 64)]), in_=pb(b2, 64, 0, [(64, 4), (1, 64)])),
                     reads=bt2, writes=[ktok[l][blk]])
                P.op("act", lambda e, b=b, VOb=VOb: e.activation(out=VOb[0:64, :], in_=pf(b, 64, 256, [(1, 256)]), func=AF.Copy), reads=bt, writes=[tvo])
                P.op("dve", lambda e, VOb=VOb, l=l, blk=blk: e.tensor_copy(out=V(VBf, 0, 64, l * 1560 + blk * 260, [(65, 4), (1, 64)]), in_=V(VOb, 0, 64, 0, [(64, 4), (1, 64)])),
                     reads=[tvo], writes=[vtok[l][blk]])
                if kind == "s":
                    P.dma("sp", DA(nks, (l * NSS + c) * 64 * 256, [(256, 64), (1, 256)]), KOb[0:64, :], out_slot(), reads=[tko])
                    P.dma("sp", DA(nvs, (l * NSS + c) * 64 * 256, [(256, 64), (1, 256)]), VOb[0:64, :], out_slot(), reads=[tvo])
                elif cidx >= PLEN // 64 - 2:
                    r0 = (cidx - (PLEN // 64 - 2)) * 64
                    P.dma("sp", DA(nkp, ((l * NPS + seq) * 128 + r0) * 256, [(256, 64), (1, 256)]), KOb[0:64, :], out_slot(), reads=[tko])
                    P.dma("sp", DA(nvp, ((l * NPS + seq) * 128 + r0) * 256, [(256, 64), (1, 256)]), VOb[0:64, :], out_slot(), reads=[tvo])

            ckpt("kv")
            tST, tSTB = [T(f"ST{l}_0"), T(f"ST{l}_1")], [T("STB_0"), T("STB_1")]
            for c in range(NCH):
                cs = c * 64
                cidx = kt * NCH + c if kind == "p" else 0
                first = (kind == "p" and cidx == 0)
                if kind == "s":
                    for part in range(2):
                        P.dma("sp", V(SCR, 0, 128, part * 1024, [(128, 8), (1, 128)]), DA(sssm, (l * NSS + c) * 2048 * 128 + part * 1024 * 128, [(128, 128), (128 * 128, 8), (1, 128)]),
                              s_in, writes=[t_scr], cont=(part > 0))
                    b, bt = ps_quad()

                    def trst(e, b=b):
                        for j in range(16):
                            last = e.transpose(pf(b, 128, j * 128, [(1, 128)]), V(SCR, 0, 128, j * 128, [(1, 128)]), identf[:, :])
                        return last
                    P.op("pe", trst, reads=[t_scr, t_const], writes=bt)
                    P.op("dve", lambda e, b=b, l=l: e.tensor_copy(out=V(ST, 0, 128, l * 2048, [(1, 2048)]), in_=pf(b, 128, 0, [(1, 2048)])), reads=bt, writes=tST)
                    P.op("act", lambda e, l=l: e.activation(out=STB[:, :], in_=V(ST, 0, 128, l * 2048, [(1, 2048)]), func=AF.Copy), reads=tST, writes=tSTB)
                ckpt("stateload")
                exi = exctr[0] % 2; exctr[0] += 1
                EX = EXs[exi]; tEX = T(f"EX{exi}")
                bs, bst = ps_single(2)

                def smallmm(e, b=bs, c=c):
                    da = V(DAv, 0, 64, c * 32, [(1, 32)])
                    e.matmul(pf(b, 64, 0, [(1, 32)]), lhsT=Umat[0:64, :], rhs=da, start=True, stop=True)
                    e.matmul(pf(b, 64, 32, [(1, 32)]), lhsT=SUmat[0:64, :], rhs=da, start=True, stop=True)
                    return e.matmul(pf(b, 128, 64, [(1, 32)]), lhsT=V(onesf, 0, 64, 0, [(1, 128)]), rhs=da, start=True, stop=True)
                P.op("pe", smallmm, reads=[tdt, t_const], writes=bst)
                P.op("act", lambda e, b=bs, EX=EX: e.activation(out=EX[0:64, 0:64], in_=pf(b, 64, 0, [(1, 64)]), func=AF.Exp), reads=bst, writes=[tEX])
                P.op("act", lambda e, b=bs, EX=EX: e.activation(out=EX[:, 64:96], in_=pf(b, 128, 64, [(1, 32)]), func=AF.Exp), reads=bst, writes=[tEX])
                for hh in range(2):
                    u = uctr[0] % 2; uctr[0] += 1
                    o1 = u * 1024
                    h0, g0 = hh * 16, hh * 2
                    tR1, tDEC, tXTs, tXDT, tTb, tCBM, tBTOK = [T(f"{n}{u}") for n in ("R1_", "DEC_", "XTs_", "XDT_", "Tb_", "CBM_", "BTOK_")]
                    tSTh, tSTBh = T(f"ST{l}_{hh}"), T(f"STB_{hh}")
                    P.op("dve", lambda e, c=c, o1=o1, h0=h0: e.tensor_tensor(out=V(R1, 0, 64, o1, [(64, 16), (1, 64)]), in0=V(DAv, 0, 64, c * 32 + h0, [(1, 16), (0, 64)]),
                                                                           in1=V(Umat, 0, 64, 0, [(0, 16), (1, 64)]), op=ALU.mult), reads=[tdt, t_const], writes=[tR1])
                    bq, bqt = ps_pair()

                    def segmm(e, b=bq, o1=o1):
                        for i in range(2):
                            last = e.matmul(pf(b, 64, i * 512, [(1, 512)]), lhsT=SUmat[0:64, :], rhs=V(R1, 0, 64, o1 + i * 512, [(1, 512)]), start=True, stop=True)
                        return last
                    P.op("pe", segmm, reads=[tR1, t_const], writes=bqt)
                    P.op("act", lambda e, b=bq, o1=o1: e.activation(out=V(DEC, 0, 64, o1, [(1, 1024)]), in_=pf(b, 64, 0, [(1, 1024)]), func=AF.Exp), reads=bqt, writes=[tDEC])
                    bc_, bct = ps_single()

                    def cbmm(e, b=bc_, cs=cs, g0=g0):
                        for gg in range(2):
                            g = g0 + gg
                            last = e.matmul(pf(b, 64, gg * 64, [(1, 64)]), lhsT=V(PH1, 0, 128, BT_OFF + g * NT + cs, [(1, 64)]), rhs=V(PH1, 0, 128, CT_OFF + g * NT + cs, [(1, 64)]), start=True, stop=True)
                        return last
                    P.op("pe", cbmm, reads=[T(f"BT{g0}"), T(f"BT{g0 + 1}"), T(f"CT{g0}"), T(f"CT{g0 + 1}")], writes=bct)
                    P.op("dve", lambda e, b=bc_, u=u: e.tensor_tensor(out=V(CBM, 0, 64, u * 128, [(64, 2), (1, 64)]), in0=pf(b, 64, 0, [(64, 2), (1, 64)]),
                                                                       in1=V(Umb, 0, 64, 0, [(0, 2), (1, 64)]), op=ALU.mult), reads=bct + [t_const], writes=[tCBM])
                    P.op("dve", lambda e, o1=o1, u=u: e.tensor_tensor(out=V(DEC, 0, 64, o1, [(512, 2), (64, 8), (1, 64)]), in0=V(DEC, 0, 64, o1, [(512, 2), (64, 8), (1, 64)]),
                                                                       in1=V(CBM, 0, 64, u * 128, [(64, 2), (0, 8), (1, 64)]), op=ALU.mult), reads=[tDEC, tCBM], writes=[tDEC])
                    bx, bxt = ps_single(2)

                    def trxx(e, b=bx, cs=cs, hh=hh):
                        for j in range(8):
                            last = e.transpose(pb(b, 64, j * 128, [(1, 128)]), V(XC, 0, 128, (hh * 8 + j) * NT + cs, [(1, 64)]), identb[:, :])
                        return last
                    P.op("pe", trxx, reads=[T(f"XC{hh * 8 + j}") for j in range(8)] + [t_const], writes=bxt)
                    P.op("act", lambda e, b=bx, o1=o1: e.activation(out=V(XTs, 0, 64, o1, [(1, 1024)]), in_=pb(b, 64, 0, [(1, 1024)]), func=AF.Copy), reads=bxt, writes=[tXTs])
                    P.op("dve", lambda e, b=bx, c=c, o1=o1, h0=h0: e.tensor_tensor(out=V(XDT, 0, 64, o1, [(64, 16), (1, 64)]), in0=pb(b, 64, 0, [(64, 16), (1, 64)]),
                                                                                 in1=V(DTv, 0, 64, c * 32 + h0, [(1, 16), (0, 64)]), op=ALU.mult), reads=bxt + [tdt], writes=[tXDT])
                    P.op("pool", lambda e, o1=o1, h0=h0, EX=EX: e.tensor_tensor(out=V(XDTS, 0, 64, 2 * o1, [(64, 16), (1, 64)]), in0=V(XDT, 0, 64, o1, [(64, 16), (1, 64)]),
                                                                             in1=V(EX, 0, 64, 32 + h0, [(1, 16), (0, 64)]), op=ALU.mult), reads=[tXDT, tEX], writes=[tR1])
                    bb, bbt = ps_single()

                    def trb(e, b=bb, cs=cs, g0=g0):
                        for gg in range(2):
                            last = e.transpose(pb(b, 64, gg * 128, [(1, 128)]), V(PH1, 0, 128, BT_OFF + (g0 + gg) * NT + cs, [(1, 64)]), identb[:, :])
                        return last
                    P.op("pe", trb, reads=[T(f"BT{g0}"), T(f"BT{g0 + 1}"), t_const], writes=bbt)
                    P.op("act", lambda e, b=bb, u=u: e.activation(out=V(BTOK, 0, 64, u * 256, [(1, 256)]), in_=pb(b, 64, 0, [(1, 256)]), func=AF.Copy), reads=bbt, writes=[tBTOK])
                    by, byt = ps_pair()

                    def ymm(e, b=by, o1=o1, h0=h0):
                        for h in range(16):
                            e.matmul(pf(b, 64, h * 64, [(1, 64)]), lhsT=V(DEC, 0, 64, o1 + h * 64, [(1, 64)]), rhs=V(XDT, 0, 64, o1 + h * 64, [(1, 64)]), start=True, stop=False)
                            last = e.matmul(pf(b, 64, h * 64, [(1, 64)]), lhsT=V(DI, 0, 64, (h0 + h) * 64, [(1, 64)]), rhs=V(XTs, 0, 64, o1 + h * 64, [(1, 64)]), start=False, stop=True)
                        return last
                    P.op("pe", ymm, reads=[tDEC, tXDT, tXTs, T("DI")], writes=byt)
                    if not first:
                        bp, bpt = ps_pair()

                        def ypmm(e, b=bp, cs=cs, g0=g0):
                            for gg in range(2):
                                g = g0 + gg
                                last = e.matmul(pf(b, 64, gg * 512, [(1, 512)]), lhsT=V(PH1, 0, 128, CT_OFF + g * NT + cs, [(1, 64)]), rhs=V(STB, 0, 128, g * 512, [(1, 512)]), start=True, stop=True)
                            return last
                        P.op("pe", ypmm, reads=[T(f"CT{g0}"), T(f"CT{g0 + 1}"), tSTBh], writes=bpt)
                        P.op("dve", lambda e, b=bp, o1=o1, h0=h0, EX=EX: e.tensor_tensor(out=V(Tb, 0, 64, o1, [(64, 16), (1, 64)]), in0=pf(b, 64, 0, [(64, 16), (1, 64)]),
                                                                                      in1=V(EX, 0, 64, h0, [(1, 16), (0, 64)]), op=ALU.mult), reads=bpt + [tEX], writes=[tTb])
                        P.op("dve", lambda e, b=by, o1=o1: e.tensor_tensor(out=V(Tb, 0, 64, o1, [(1, 1024)]), in0=pf(b, 64, 0, [(1, 1024)]), in1=V(Tb, 0, 64, o1, [(1, 1024)]), op=ALU.add),
                             reads=byt + [tTb], writes=[tTb])
                    else:
                        P.op("dve", lambda e, b=by, o1=o1: e.tensor_copy(out=V(Tb, 0, 64, o1, [(1, 1024)]), in_=pf(b, 64, 0, [(1, 1024)])), reads=byt, writes=[tTb])
                    bs2, bs2t = ps_pair()

                    def stmm(e, b=bs2, o1=o1, u=u):
                        for gg in range(2):
                            last = e.matmul(pf(b, 128, gg * 512, [(1, 512)]), lhsT=V(BTOK, 0, 64, u * 256 + gg * 128, [(1, 128)]), rhs=V(XDTS, 0, 64, 2 * o1 + gg * 512, [(1, 512)]), start=True, stop=True)
                        return last
                    P.op("pe", stmm, reads=[tBTOK, tR1], writes=bs2t)
                    so = l * 2048 + hh * 1024
                    if first:
                        P.op("dve", lambda e, b=bs2, so=so: e.tensor_copy(out=V(ST, 0, 128, so, [(1, 1024)]), in_=pf(b, 128, 0, [(1, 1024)])), reads=bs2t, writes=[tSTh])
                    else:
                        P.op("dve", lambda e, so=so, h0=h0, EX=EX: e.tensor_tensor(out=V(ST, 0, 128, so, [(64, 16), (1, 64)]), in0=V(ST, 0, 128, so, [(64, 16), (1, 64)]),
                                                                                 in1=V(EX, 0, 128, 64 + h0, [(1, 16), (0, 64)]), op=ALU.mult), reads=[tSTh, tEX], writes=[tSTh])
                        P.op("dve", lambda e, b=bs2, so=so: e.tensor_tensor(out=V(ST, 0, 128, so, [(1, 1024)]), in0=V(ST, 0, 128, so, [(1, 1024)]),
                                                                             in1=pf(b, 128, 0, [(1, 1024)]), op=ALU.add), reads=bs2t + [tSTh], writes=[tSTh])
                    P.op("act", lambda e, so=so, hh=hh: e.activation(out=V(STB, 0, 128, hh * 1024, [(1, 1024)]), in_=V(ST, 0, 128, so, [(1, 1024)]), func=AF.Copy), reads=[tSTh], writes=[tSTBh])
                    bz, bzt = ps_single()

                    def trz(e, b=bz, cs=cs, hh=hh):
                        for j in range(8):
                            last = e.transpose(pb(b, 64, j * 128, [(1, 128)]), V(ZT, 0, 128, (hh * 8 + j) * NT + cs, [(1, 64)]), identb[:, :])
                        return last
                    P.op("pe", trz, reads=[T(f"ZT{hh * 8 + j}") for j in range(8)] + [t_const], writes=bzt)
                    P.op("dve", lambda e, b=bz, o1=o1: e.tensor_tensor(out=V(Tb, 0, 64, o1, [(1, 1024)]), in0=V(Tb, 0, 64, o1, [(1, 1024)]), in1=pb(b, 64, 0, [(1, 1024)]), op=ALU.mult),
                         reads=bzt + [tTb], writes=[tTb])
                    P.op("act", lambda e, o1=o1: e.activation(out=V(R1, 0, 64, o1, [(1, 1024)]), in_=V(Tb, 0, 64, o1, [(1, 1024)]), func=AF.Square), reads=[tTb], writes=[tR1])
                    tss = T(f"ss4_{u}")
                    P.op("dve", lambda e, o1=o1, u=u: e.tensor_reduce(out=V(SS4, 0, 64, u * 2, [(1, 2)]), in_=V(R1, 0, 64, o1, [(512, 2), (1, 512)]), axis=AX.X, op=ALU.add), reads=[tR1], writes=[tss])
                    P.op("act", lambda e, u=u: e.activation(out=V(RS4, 0, 64, u * 2, [(1, 2)]), in_=V(SS4, 0, 64, u * 2, [(1, 2)]), func=AF.Sqrt, bias=CONST[0:64, 0:1], scale=1.0 / 512), reads=[tss, t_const], writes=[tss])
                    P.op("dve", lambda e, u=u: e.reciprocal(out=V(RS4, 0, 64, u * 2, [(1, 2)]), in_=V(RS4, 0, 64, u * 2, [(1, 2)])), reads=[tss], writes=[tss])
                    P.op("dve", lambda e, o1=o1, u=u: e.tensor_tensor(out=V(YN, 0, 64, o1, [(512, 2), (1, 512)]), in0=V(Tb, 0, 64, o1, [(512, 2), (1, 512)]),
                                                                       in1=V(RS4, 0, 64, u * 2, [(1, 2), (0, 512)]), op=ALU.mult), reads=[tTb, tss], writes=[tXDT])
                    bt_, btt = ps_single()

                    def try_(e, b=bt_, o1=o1):
                        for j in range(8):
                            last = e.transpose(pb(b, 128, j * 64, [(1, 64)]), V(YN, 0, 64, o1 + j * 128, [(1, 128)]), V(identb, 0, 64, 0, [(1, 64)]))
                        return last
                    P.op("pe", try_, reads=[tXDT, t_const], writes=btt)
                    P.op("dve", lambda e, b=bt_, cs=cs, l=l, hh=hh: e.tensor_tensor(out=V(YT, 0, 128, hh * 8 * NT + cs, [(NT, 8), (1, 64)]), in0=pb(b, 128, 0, [(64, 8), (1, 64)]),
                                                                                  in1=V(GSSM, 0, 128, l * 16 + hh * 8, [(1, 8), (0, 64)]), op=ALU.mult), reads=btt + [t_lc], writes=[T(f"YT{c}")])
                final_state = (kind == "s") or (kind == "p" and cidx == PLEN // 64 - 1)
                if final_state:
                    b, bt = ps_quad()

                    def trso(e, b=b, l=l):
                        for j in range(16):
                            last = e.transpose(pf(b, 128, j * 128, [(1, 128)]), V(ST, 0, 128, l * 2048 + j * 128, [(1, 128)]), identf[:, :])
                        return last
                    P.op("pe", trso, reads=[T(f"ST{l}_0"), T(f"ST{l}_1"), t_const], writes=bt)
                    P.op("act", lambda e, b=b: e.activation(out=V(SCR, 0, 128, 0, [(1, 2048)]), in_=pf(b, 128, 0, [(1, 2048)]), func=AF.Copy), reads=bt, writes=[t_scr])
                    osl = out_slot()
                    for part in range(2):
                        if kind == "s":
                            dst = DA(nss, (l * NSS + c) * 2048 * 128 + part * 1024 * 128, [(128, 128), (128 * 128, 8), (1, 128)])
                        else:
                            dst = DA(nsp, (l * NPS + seq) * 2048 * 128 + part * 1024 * 128, [(128, 128), (128 * 128, 8), (1, 128)])
                        P.dma("sp", dst, V(SCR, 0, 128, part * 1024, [(128, 8), (1, 128)]), osl, reads=[t_scr], cont=(part > 0))
                bq_, bqt_ = ps_single(2)

                def trq(e, b=bq_, cs=cs):
                    for j in range(8):
                        last = e.transpose(pb(b, 64, j * 128, [(1, 128)]), V(QTr, 0, 128, j * NT + cs, [(1, 64)]), identb[:, :])
                    return last
                P.op("pe", trq, reads=TL("QTr", 8) + [t_const], writes=bqt_)
                tq = T("SQQ")
                P.op("act", lambda e, b=bq_: e.activation(out=SQQ[0:64, 0:1024], in_=pb(b, 64, 0, [(1, 1024)]), func=AF.Square), reads=bqt_, writes=[tq])
                P.op("dve", lambda e: e.tensor_reduce(out=SSQ[0:64, :], in_=V(SQQ, 0, 64, 0, [(64, 16), (1, 64)]), axis=AX.X, op=ALU.add), reads=[tq], writes=[tq])
                P.op("act", lambda e: e.activation(out=RQ[0:64, :], in_=SSQ[0:64, :], func=AF.Sqrt, bias=CONST[0:64, 0:1], scale=1.0 / 64), reads=[tq, t_const], writes=[tq])
                P.op("dve", lambda e: e.reciprocal(out=RQ[0:64, :], in_=RQ[0:64, :]), reads=[tq], writes=[tq])
                P.op("dve", lambda e, b=bq_: e.tensor_tensor(out=V(QN, 0, 64, 0, [(64, 16), (1, 64)]), in0=pb(b, 64, 0, [(64, 16), (1, 64)]),
                                                              in1=V(RQ, 0, 64, 0, [(1, 16), (0, 64)]), op=ALU.mult), reads=bqt_ + [tq], writes=[T("QN")])
                P.op("dve", lambda e, l=l: e.tensor_tensor(out=V(QN, 0, 64, 0, [(64, 16), (1, 64)]), in0=V(QN, 0, 64, 0, [(64, 16), (1, 64)]),
                                                            in1=V(QG8, 0, 64, l * 64, [(0, 16), (1, 64)]), op=ALU.mult), reads=[T("QN"), t_lc], writes=[T("QN")])
                bq2, bq2t = ps_single()

                def trq2(e, b=bq2):
                    for h in range(16):
                        last = e.transpose(pb(b, 64, h * 64, [(1, 64)]), V(QN, 0, 64, h * 64, [(1, 64)]), V(identb, 0, 64, 0, [(1, 64)]))
                    return last
                P.op("pe", trq2, reads=[T("QN"), t_const], writes=bq2t)
                P.op("act", lambda e, b=bq2: e.activation(out=QTc[0:64, :], in_=pb(b, 64, 0, [(1, 1024)]), func=AF.Copy), reads=bq2t, writes=[T("QTc")])
                if kind == "s":
                    P.dma("sp", V(SCR, 0, 128, 0, [(1, 256)]), DA(ck, (l * NSS + c) * 128 * 256, [(256, 128), (1, 256)]), s_in, writes=[t_scr])
                    P.op("act", lambda e: e.activation(out=V(SCR, 0, 128, 512, [(1, 128)]).bitcast(BF16), in_=V(SCR, 0, 128, 0, [(1, 256)]), func=AF.Copy), reads=[t_scr], writes=[t_scr])
                    bk_, bkt_ = ps_single()

                    def trk(e, b=bk_):
                        src = V(SCR, 0, 128, 512, [(1, 128)]).bitcast(BF16)
                        for kv in range(4):
                            last = e.transpose(pb(b, 64, kv * 128, [(1, 128)]), src[:, kv * 64:(kv + 1) * 64], identb[:, :])
                        return last
                    P.op("pe", trk, reads=[t_scr, t_const], writes=bkt_)
                    P.op("dve", lambda e, b=bk_: e.tensor_copy(out=V(KP, 0, 64, 0, [(128, 4), (1, 128)]), in_=pb(b, 64, 0, [(128, 4), (1, 128)])), reads=bkt_, writes=[T("KP")])
                    P.dma("sp", V(SCR, 0, 64, 1024, [(256, 2), (1, 256)]), DA(cv, (l * NSS + c) * 128 * 256, [(256, 64), (64 * 256, 2), (1, 256)]), s_in, writes=[t_scr])
                    P.op("dve", lambda e: e.tensor_copy(out=V(VP, 0, 64, 0, [(260, 2), (65, 4), (1, 64)]), in_=V(SCR, 0, 64, 1024, [(256, 2), (64, 4), (1, 64)])), reads=[t_scr], writes=[T("VP")])
                ckpt("att_q")
                blocks = []
                if kind == "p":
                    for pos in range(3):
                        if cidx + pos - 2 < 0:
                            continue
                        blk = c + pos
                        blocks.append((pos,
                                       lambda kv, blk=blk, l=l: V(KT, 0, 64, l * 1536 + kv * 384 + blk * 64, [(1, 64)]),
                                       lambda kv, blk=blk, l=l: V(VBf, 0, 64, l * 1560 + blk * 260 + kv * 65, [(1, 65)]),
                                       [ktok[l][blk], vtok[l][blk]]))
                else:
                    for pos in range(2):
                        blocks.append((pos,
                                       lambda kv, pos=pos: V(KP, 0, 64, kv * 128 + pos * 64, [(1, 64)]),
                                       lambda kv, pos=pos: V(VP, 0, 64, pos * 260 + kv * 65, [(1, 65)]),
                                       [T("KP"), T("VP")]))
                    blk = 2 + c
                    blocks.append((2,
                                   lambda kv, blk=blk, l=l: V(KT, 0, 64, l * 1536 + kv * 384 + blk * 64, [(1, 64)]),
                                   lambda kv, blk=blk, l=l: V(VBf, 0, 64, l * 1560 + blk * 260 + kv * 65, [(1, 65)]),
                                   [ktok[l][blk], vtok[l][blk]]))
                bo, bot = ps_quad(2)
                nb = len(blocks)
                for bi, (pos, kfn, vfn, btoks) in enumerate(blocks):
                    bsc, bsct = ps_pair()

                    def scmm(e, b=bsc, kfn=kfn, pos=pos):
                        for kv in range(4):
                            e.matmul(pf(b, 64, kv * 256, [(1, 256)]), lhsT=kfn(kv), rhs=V(QTc, 0, 64, kv * 256, [(1, 256)]), start=True, stop=False)
                            last = e.matmul(pf(b, 64, kv * 256, [(1, 256)]), lhsT=V(identb, 0, 64, 0, [(1, 64)]), rhs=V(BIAS, 0, 64, pos * 1024 + kv * 256, [(1, 256)]), start=False, stop=True)
                        return last
                    P.op("pe", scmm, reads=[btoks[0], T("QTc"), T("bias"), t_const], writes=bsct)
                    pi = 0
                    P.op("act", lambda e, b=bsc, pi=pi: e.activation(out=PT[pi][0:64, :], in_=pf(b, 64, 0, [(1, 1024)]), func=AF.Exp), reads=bsct, writes=[T(f"PT{pi}")])

                    def omm(e, b=bo, vfn=vfn, pi=pi, bi=bi, nb=nb):
                        for h in range(16):
                            last = e.matmul(pf(b, 64, h * 128, [(1, 65)]), lhsT=V(PT[pi], 0, 64, h * 64, [(1, 64)]), rhs=vfn(h // 4), start=(bi == 0 and h % 4 == 0), stop=(bi == nb - 1), skip_group_check=True)
                        return last
                    P.op("pe", omm, reads=[T(f"PT{pi}"), btoks[1]], writes=bot)
                tden = T("den")
                P.op("dve", lambda e, b=bo, l=l: e.tensor_tensor(out=DEN[0:64, :], in0=pf(b, 64, 64, [(128, 16)]), in1=V(ESINK, 0, 64, l * 16, [(1, 16)]), op=ALU.add), reads=bot + [t_lc], writes=[tden])
                P.op("dve", lambda e: e.reciprocal(out=RDEN[0:64, :], in_=DEN[0:64, :]), reads=[tden], writes=[tden])
                P.op("dve", lambda e, b=bo: e.tensor_tensor(out=V(ON, 0, 64, 0, [(64, 16), (1, 64)]), in0=pf(b, 64, 0, [(128, 16), (1, 64)]),
                                                             in1=V(RDEN, 0, 64, 0, [(1, 16), (0, 64)]), op=ALU.mult), reads=bot + [tden], writes=[T("QN")])
                bo2, bo2t = ps_single()

                def tro(e, b=bo2):
                    for j in range(8):
                        last = e.transpose(pb(b, 128, j * 64, [(1, 64)]), V(ON, 0, 64, j * 128, [(1, 128)]), V(identb, 0, 64, 0, [(1, 64)]))
                    return last
                P.op("pe", tro, reads=[T("QN"), t_const], writes=bo2t)
                P.op("act", lambda e, b=bo2, cs=cs: e.activation(out=V(OT, 0, 128, cs, [(NT, 8), (1, 64)]), in_=pb(b, 128, 0, [(64, 8), (1, 64)]), func=AF.Copy), reads=bo2t, writes=[T(f"OT{c}")])

            ckpt("att_done")
            if kind == "p" and not last_tile:
                P.op("pool", lambda e, l=l: e.tensor_copy(out=V(KT, 0, 64, l * 1536, [(384, 4), (1, 128)]), in_=V(KT, 0, 64, l * 1536 + 256, [(384, 4), (1, 128)])),
                     reads=ktok[l][4:6], writes=ktok[l][0:2])
                P.op("pool", lambda e, l=l: e.tensor_copy(out=V(VBf, 0, 64, l * 1560, [(1, 520)]), in_=V(VBf, 0, 64, l * 1560 + 4 * 260, [(1, 520)])),
                     reads=vtok[l][4:6], writes=vtok[l][0:2])

            tYT, tOT = TL("YT", NCH), TL("OT", NCH)
            for J in range(2):
                wgs, wgst = wload([(512, 8), (1, 512)], win_ap(C_GS + J * 512, 512), key=("gs", l, J))
                wga, wgat = wload([(512, 8), (1, 512)], win_ap(C_GA + J * 512, 512), key=("ga", l, J))
                wba, wbat = wload([(512, 8), (1, 512)], DA(w_br_attn, l * D * D + J * 512, [(D, 128), (128 * D, 8), (1, 512)]), key=("ba", l, J))
                for half in range(2):
                    wbs, wbst = wload_bs(l, J * 512 + half * 256)
                    for j2 in range(2):
                        jj = half * 2 + j2
                        j = J * 4 + jj
                        b, bt = fm_proj(wgs, wgst, 512, jj * 128, hT, thT, 8)
                        P.op("act", lambda e, b=b: e.activation(out=GSs[:, :], in_=pf(b, 128, 0, [(1, NT)]), func=AF.Sigmoid), reads=bt, writes=[T("GSs")])
                        b, bt = fm_proj(wga, wgat, 512, jj * 128, hT, thT, 8)
                        P.op("act", lambda e, b=b: e.activation(out=GAs[:, :], in_=pf(b, 128, 0, [(1, NT)]), func=AF.Sigmoid), reads=bt, writes=[T("GAs")])
                        b, bt = fm_proj(wbs, wbst, 256, j2 * 128, YT, tYT, 16)
                        P.op("dve", lambda e, b=b: e.tensor_tensor(out=MXs[:, :], in0=pf(b, 128, 0, [(1, NT)]), in1=GSs[:, :], op=ALU.mult), reads=bt + [T("GSs")], writes=[T("MXs")])
                        b, bt = fm_proj(wba, wbat, 512, jj * 128, OT, tOT, 8)
                        P.op("dve", lambda e, b=b: e.tensor_tensor(out=GAs[:, :], in0=pf(b, 128, 0, [(1, NT)]), in1=GAs[:, :], op=ALU.mult), reads=bt + [T("GAs")], writes=[T("GAs")])
                        P.op("dve", lambda e, j=j: e.tensor_tensor(out=V(MIXT, 0, 128, j * NT, [(1, NT)]), in0=MXs[:, :], in1=GAs[:, :], op=ALU.add),
                             reads=[T("MXs"), T("GAs")], writes=[T(f"ZT{j}")])
            tMIX = TL("ZT", 8)
            for J in range(2):
                wo, wot = wload([(512, 8), (1, 512)], DA(w_out, l * D * D + J * 512, [(D, 128), (128 * D, 8), (1, 512)]), key=("wo", l, J))
                for jj in range(4):
                    j = J * 4 + jj
                    b, bt = fm_proj(wo, wot, 512, jj * 128, MIXT, tMIX, 8, nreads=nseg)
                    for sg in range(nseg):
                        P.op("dve", lambda e, b=b, j=j, sg=sg, l=l: e.scalar_tensor_tensor(out=V(xT, 0, 128, j * NT + sg * L, [(1, L)]), in0=pf(b, 128, sg * L, [(1, L)]),
                                                                                        scalar=modv(l, 2, j, seq0 + sg), in1=V(xT, 0, 128, j * NT + sg * L, [(1, L)]), op0=ALU.mult, op1=ALU.add),
                             reads=bt + [T("modt")], writes=[txT[j]])

            ckpt("phase2")
            norm_to_hT(l, 1)
            wgu0 = l * D * 2 * DFF
            for G in range(6):
                ncg = 4 if G < 5 else 2
                wg, wgt = wload([(512, 8), (1, ncg * 128)], DA(w_gate_up, wgu0 + G * 512, [(2 * DFF, 128), (128 * 2 * DFF, 8), (1, ncg * 128)]), key=("wg", l, G))
                wu, wut = wload([(512, 8), (1, ncg * 128)], DA(w_gate_up, wgu0 + DFF + G * 512, [(2 * DFF, 128), (128 * 2 * DFF, 8), (1, ncg * 128)]), key=("wu", l, G))
                for jj in range(ncg):
                    j = G * 4 + jj
                    b, bt = fm_proj(wg, wgt, 512, jj * 128, hT, thT, 8)
                    si = j % 2
                    P.op("act", lambda e, b=b, si=si: e.activation(out=SGs[si][:, :], in_=pf(b, 128, 0, [(1, NT)]), func=AF.Silu), reads=bt, writes=[T(f"SGs{si}")])
                    b, bt = fm_proj(wu, wut, 512, jj * 128, hT, thT, 8)
                    P.op("dve", lambda e, b=b, si=si, j=j: e.tensor_tensor(out=V(ACTT, 0, 128, j * NT, [(1, NT)]), in0=pf(b, 128, 0, [(1, NT)]), in1=SGs[si][:, :], op=ALU.mult),
                         reads=bt + [T(f"SGs{si}")], writes=[acttok(j)])
            tACT = [acttok(j) for j in range(22)]
            for j in range(8):
                wd, wdt = wload_down(l, j)
                b, bt = fm_proj(wd, wdt, 128, 0, ACTT, tACT, 22, nreads=nseg)
                for sg in range(nseg):
                    P.op("dve", lambda e, b=b, j=j, sg=sg, l=l: e.scalar_tensor_tensor(out=V(xT, 0, 128, j * NT + sg * L, [(1, L)]), in0=pf(b, 128, sg * L, [(1, L)]),
                                                                                    scalar=modv(l, 5, j, seq0 + sg), in1=V(xT, 0, 128, j * NT + sg * L, [(1, L)]), op0=ALU.mult, op1=ALU.add),
                         reads=bt + [T("modt")], writes=[txT[j]])

        ckpt("ffn")
        for sub in range(2):
            b, bt = ps_pair()

            def try2(e, b=b, sub=sub):
                for k in range(8):
                    last = e.transpose(pf(b, 128, k * 128, [(1, 128)]), V(xT, 0, 128, k * NT + sub * 128, [(1, 128)]), identf[:, :])
                return last
            P.op("pe", try2, reads=txT + [t_const], writes=bt)
            P.op("act", lambda e, b=b: e.activation(out=V(SCR, 0, 128, 0, [(1, 1024)]), in_=pf(b, 128, 0, [(1, 1024)]), func=AF.Copy), reads=bt, writes=[t_scr])
            P.dma("sp", DA(ydst, (row0 + sub * 128) * D, [(D, 128), (1, D)]), V(SCR, 0, 128, 0, [(1, 1024)]), out_slot(), reads=[t_scr])

    try:
        ckpt("consts")
        if NSS:
            do_tile("s", 0, 0)
        for s in range(NPS):
            for kt in range(NPT):
                do_tile("p", s, kt)
    except StopBuild:
        pass
    import os as _os2
    P.emit(window=int(_os2.environ.get("K_WINDOW", "600")))
    return nc


_W_NAMES = ["rel_bias", "ada_w", "ada_b", "norm_mix_g", "norm_ffn_g", "w_in", "conv_w", "conv_b", "dt_bias", "a_log",
            "d_skip", "ssm_norm_g", "q_norm_g", "k_norm_g", "sinks", "w_br_ssm", "w_br_attn", "w_out", "w_gate_up", "w_down"]
_PROG_CACHE = {}


def run_cores(inputs, n_cores, NPS, PLEN, NSS):
    f = lambda a: np.ascontiguousarray(np.asarray(a, dtype=np.float32))
    key = (NPS, PLEN, NSS)
    nc = build_program(NPS, PLEN, NSS)
    oh = onehot_table()
    in_maps = []
    for i in range(n_cores):
        m = {n: f(inputs[n]) for n in _W_NAMES}
        m["onehot"] = oh
        m["xp"] = f(inputs["x_prompt"][i * NPS:(i + 1) * NPS]).reshape(NPS * PLEN, D)
        m["xs"] = f(inputs["x_sample"][i * NSS:(i + 1) * NSS]).reshape(NSS * 64, D)
        m["ck"] = f(inputs["cache_k"][:, i * NSS:(i + 1) * NSS]).reshape(2, NSS, 128, 256)
        m["cv"] = f(inputs["cache_v"][:, i * NSS:(i + 1) * NSS]).reshape(2, NSS, 128, 256)
        m["sconv"] = f(inputs["state_conv"][:, i * NSS:(i + 1) * NSS]).reshape(2, NSS * 3, 3072)
        m["sssm"] = f(inputs["state_ssm"][:, i * NSS:(i + 1) * NSS]).reshape(2, NSS, 2048, 128)
        m["cvec"] = np.concatenate([f(inputs["c_prompt"][i * NPS:(i + 1) * NPS]), f(inputs["c_sample"][i * NSS:(i + 1) * NSS])], axis=0)
        in_maps.append(m)
    res = run_bass_kernel_spmd(nc, in_maps, core_ids=list(range(n_cores)))
    R = res.results
    cat = lambda name, shp, ax: np.concatenate([np.asarray(r[name], dtype=np.float32).reshape(shp) for r in R], axis=ax)
    out = (
        cat("yp", (NPS, PLEN, D), 0), cat("ys", (NSS, 64, D), 0),
        cat("ncp", (2, NPS, 3, 3072), 1), cat("ncs", (2, NSS, 3, 3072), 1),
        cat("nsp", (2, NPS, 32, 64, 128), 1), cat("nss", (2, NSS, 32, 64, 128), 1),
        cat("nkp", (2, NPS, 128, 4, 64), 1), cat("nks", (2, NSS, 64, 4, 64), 1),
        cat("nvp", (2, NPS, 128, 4, 64), 1), cat("nvs", (2, NSS, 64, 4, 64), 1),
    )
    return out


def kernel(**inputs):
    return run_cores(inputs, 8, 2, 2048, 4)
```
